# Optimizing a Trainium2 kernel written in Bass

```python
import jax, jax.numpy as jnp
from jax import lax
import numpy as np

D_MODEL = 1024
BATCH = 1
SEQ = 16384
DEPTH = 2
DEC_BATCH = 128
DEC_SEQ = 4
PAST_LEN = 16384
PAGE_SIZE = 128

A_DK = 64
A_DV = 64
A_WIDTH = D_MODEL // 4
A_HEADS = A_WIDTH // A_DK
HGRN_CHUNK = 64
B_WIDTH = D_MODEL // 4
B_GROUPS = 4
CONV_WIDTH = 31
C_HEAD_DIM = 64
C_WIDTH = D_MODEL // 2
C_HEADS = C_WIDTH // C_HEAD_DIM
C_KV_HEADS = C_HEADS // 4
C_GROUP = C_HEADS // C_KV_HEADS
WINDOW = 128
ROPE_THETA = 10000.0
MIX_WIDTH = A_WIDTH + B_WIDTH + C_WIDTH
IN_COLS = 4 * A_WIDTH + 2 * B_WIDTH + C_WIDTH + 2 * C_KV_HEADS * C_HEAD_DIM
D_FF = -(-(8 * D_MODEL) // (3 * 256)) * 256
PLE_DIM = 256
EPS = 1e-6

kernel_name = 'hymba_hgrn2_conformer_swa_sink_step'


def _rmsnorm(x, g):
    xf = x.astype(jnp.float32)
    y = xf * lax.rsqrt(jnp.mean(xf * xf, axis=-1, keepdims=True) + EPS)
    return (y * g.astype(jnp.float32)).astype(x.dtype)


def _rope(x, pos):
    half = x.shape[-1] // 2
    inv = ROPE_THETA ** (-jnp.arange(half, dtype=jnp.float32) / half)
    ang = pos[:, None] * inv[None, :]
    cos = jnp.cos(ang)[:, None, :]
    sin = jnp.sin(ang)[:, None, :]
    xf = x.astype(jnp.float32)
    x1, x2 = xf[..., :half], xf[..., half:]
    return jnp.concatenate([x1 * cos - x2 * sin, x2 * cos + x1 * sin], axis=-1).astype(x.dtype)


def _hgrn_chunk(S, xs):
    q, k, v, g = xs
    C = q.shape[1]
    G = jnp.cumsum(g, axis=1)
    causal = jnp.tril(jnp.ones((C, C), dtype=bool))[None, :, :, None, None]
    decay = jnp.exp(jnp.where(causal, G[:, :, None] - G[:, None, :], -jnp.inf))
    att = jnp.einsum('bthk,btshk,bshk->bhts', q, decay, k)
    o = jnp.einsum('bhts,bshv->bthv', att, v) + jnp.einsum('bthk,bhkv->bthv', q * jnp.exp(G), S)
    G_end = G[:, -1]
    S = jnp.exp(G_end)[..., None] * S + jnp.einsum('bshk,bshv->bhkv', k * jnp.exp(G_end[:, None] - G), v)
    return S, o


def _hgrn(S0, q, k, v, g):
    B, T, H, _ = q.shape
    C = min(HGRN_CHUNK, T)
    n = T // C

    def chunks(a):
        return a.reshape(B, n, C, H, a.shape[-1]).swapaxes(0, 1)

    S, o = lax.scan(_hgrn_chunk, S0, (chunks(q), chunks(k), chunks(v), chunks(g)))
    return S, o.swapaxes(0, 1).reshape(B, T, H, -1)


def _conv_module(u, buf, w, b, ln_g, ln_b):
    B, T, Cw = u.shape
    xpad = jnp.concatenate([buf.astype(u.dtype), u], axis=1)
    y = lax.conv_general_dilated(xpad, w[:, None, :].astype(u.dtype), (1,), 'VALID',
                                 dimension_numbers=('NWC', 'WIO', 'NWC'), feature_group_count=Cw)
    y = y.astype(jnp.float32) + b.astype(jnp.float32)
    yg = y.reshape(B, T, B_GROUPS, Cw // B_GROUPS)
    mu = jnp.mean(yg, axis=-1, keepdims=True)
    var = jnp.mean(jnp.square(yg - mu), axis=-1, keepdims=True)
    y = ((yg - mu) * lax.rsqrt(var + EPS)).reshape(B, T, Cw) * ln_g.astype(jnp.float32) + ln_b.astype(jnp.float32)
    return jax.nn.silu(y).astype(u.dtype), xpad[:, -(CONV_WIDTH - 1):]


def _sink_attention(q, k, v, mask, sinks):
    s = jnp.einsum('...qhgd,...khd->...hgqk', q, k).astype(jnp.float32) * (C_HEAD_DIM ** -0.5)
    s = jnp.where(mask[..., None, None, :, :], s, -jnp.inf)
    sink = sinks.astype(jnp.float32).reshape(C_KV_HEADS, C_GROUP)[:, :, None, None]
    m = jnp.maximum(jnp.max(s, axis=-1, keepdims=True), sink)
    p = jnp.exp(s - m)
    p = p / (jnp.sum(p, axis=-1, keepdims=True) + jnp.exp(sink - m))
    return jnp.einsum('...hgqk,...khd->...qhgd', p.astype(v.dtype), v)


def _swa_prompt(q, k, v, sinks):
    B, T = q.shape[:2]
    nb = T // WINDOW
    qb = q.reshape(B, nb, WINDOW, C_KV_HEADS, C_GROUP, C_HEAD_DIM)

    def band(a):
        ab = a.reshape(B, nb, WINDOW, C_KV_HEADS, C_HEAD_DIM)
        prev = jnp.pad(ab[:, :-1], ((0, 0), (1, 0), (0, 0), (0, 0), (0, 0)))
        return jnp.concatenate([prev, ab], axis=2)

    i = jnp.arange(WINDOW)[:, None]
    j = jnp.arange(2 * WINDOW)[None, :]
    rel = WINDOW + i - j
    key_pos = jnp.arange(nb)[:, None, None] * WINDOW - WINDOW + j[None]
    mask = (rel >= 0) & (rel <= WINDOW) & (key_pos >= 0)
    o = _sink_attention(qb, band(k), band(v), mask, sinks)
    return o.reshape(B, T, C_WIDTH)


def _swa_sample(q, k, v, kbuf, vbuf, start, sinks):
    B, T = q.shape[:2]
    w_buf = kbuf.shape[1]
    k_all = jnp.concatenate([kbuf.astype(k.dtype), k], axis=1)
    v_all = jnp.concatenate([vbuf.astype(v.dtype), v], axis=1)
    q_pos = start + jnp.arange(T)
    k_pos = start - w_buf + jnp.arange(w_buf + T)
    rel = q_pos[:, None] - k_pos[None, :]
    mask = (rel >= 0) & (rel <= WINDOW)
    o = _sink_attention(q.reshape(B, T, C_KV_HEADS, C_GROUP, C_HEAD_DIM), k_all, v_all, mask, sinks)
    return o.reshape(B, T, C_WIDTH), k_all[:, -w_buf:], v_all[:, -w_buf:]


def _layer(x, p, start, s_hgrn, conv_buf, kbuf, vbuf, lb, wl):
    (w_in, a_onorm, conv_w, conv_b, conv_ln_g, conv_ln_b, q_norm, k_norm, sinks, w_out,
     norm_mix, norm_ffn, w_gate, w_up, w_down, ple_norm, w_ple_gate, w_ple_proj) = wl
    B, T, _ = x.shape
    f32 = jnp.float32
    z = _rmsnorm(x, norm_mix) @ w_in
    kv_w = C_KV_HEADS * C_HEAD_DIM
    sizes = [A_WIDTH] * 4 + [B_WIDTH] * 2 + [C_WIDTH, kv_w, kv_w]
    aq, af, ai, ag, bu, bg, cq, ck, cv = jnp.split(z, np.cumsum(sizes)[:-1].tolist(), axis=-1)

    def heads(a, d):
        return a.reshape(B, T, -1, d)

    zf = af.astype(f32)
    logf = jnp.logaddexp(jnp.log(lb), jnp.log1p(-lb) + jax.nn.log_sigmoid(zf))
    kin = (1.0 - lb) * jax.nn.sigmoid(-zf)
    if s_hgrn is None:
        s_hgrn = jnp.zeros((B, A_HEADS, A_DK, A_DV), f32)
    s_new, oa = _hgrn(s_hgrn.astype(f32), heads(aq.astype(f32), A_DK), heads(kin, A_DK),
                      heads(ai.astype(f32), A_DV), heads(logf, A_DK))
    oa = _rmsnorm(oa, a_onorm) * jax.nn.silu(heads(ag.astype(f32), A_DV))
    oa = oa.reshape(B, T, A_WIDTH).astype(x.dtype)

    u = bu * jax.nn.sigmoid(bg)
    if conv_buf is None:
        conv_buf = jnp.zeros((B, CONV_WIDTH - 1, B_WIDTH), u.dtype)
    ob, conv_new = _conv_module(u, conv_buf, conv_w, conv_b, conv_ln_g, conv_ln_b)

    pos = start + jnp.arange(T, dtype=f32)
    q = _rope(_rmsnorm(heads(cq, C_HEAD_DIM), q_norm), pos)
    k = _rope(_rmsnorm(heads(ck, C_HEAD_DIM), k_norm), pos)
    v = heads(cv, C_HEAD_DIM)
    if kbuf is None:
        oc = _swa_prompt(q, k, v, sinks)
        k_new, v_new = k[:, -WINDOW:], v[:, -WINDOW:]
    else:
        oc, k_new, v_new = _swa_sample(q, k, v, kbuf, vbuf, start, sinks)

    h = x + jnp.concatenate([oa, ob, oc], axis=-1) @ w_out
    hn = _rmsnorm(h, norm_ffn)
    h = h + (jax.nn.silu(hn @ w_gate) * (hn @ w_up)) @ w_down
    gate = jax.nn.sigmoid(_rmsnorm(h, ple_norm) @ w_ple_gate)
    h = h + gate * (p.astype(h.dtype) @ w_ple_proj)
    return h, s_new, conv_new, k_new, v_new


def setup_inputs(seed: int = 0) -> dict:
    key = jax.random.key(seed)
    ks = jax.random.split(key, 32)
    f32 = jnp.float32
    w_buf = min(WINDOW, PAST_LEN)

    def nrm(k, shape, s):
        return s * jax.random.normal(k, shape, f32)

    return {
        'x_prompt': nrm(ks[0], (BATCH, SEQ, D_MODEL), 1.0),
        'x_sample': nrm(ks[1], (DEC_BATCH, DEC_SEQ, D_MODEL), 1.0),
        'state_hgrn': nrm(ks[2], (DEPTH, DEC_BATCH, A_HEADS, A_DK, A_DV), 0.5),
        'state_conv': nrm(ks[3], (DEPTH, DEC_BATCH, CONV_WIDTH - 1, B_WIDTH), 0.5),
        'cache_swa_k': nrm(ks[4], (DEPTH, DEC_BATCH, w_buf, C_KV_HEADS, C_HEAD_DIM), 1.0),
        'cache_swa_v': nrm(ks[5], (DEPTH, DEC_BATCH, w_buf, C_KV_HEADS, C_HEAD_DIM), 1.0),
        'p_prompt': nrm(ks[6], (DEPTH, BATCH, SEQ, PLE_DIM), 1.0),
        'p_sample': nrm(ks[7], (DEPTH, DEC_BATCH, DEC_SEQ, PLE_DIM), 1.0),
        'a_lower': nrm(ks[8], (DEPTH, A_WIDTH), 1.0),
        'w_in': nrm(ks[9], (DEPTH, D_MODEL, IN_COLS), D_MODEL ** -0.5),
        'a_onorm': 1.0 + nrm(ks[10], (DEPTH, A_DV), 0.05),
        'conv_w': nrm(ks[11], (DEPTH, CONV_WIDTH, B_WIDTH), CONV_WIDTH ** -0.5),
        'conv_b': nrm(ks[12], (DEPTH, B_WIDTH), 0.01),
        'conv_ln_g': 1.0 + nrm(ks[13], (DEPTH, B_WIDTH), 0.05),
        'conv_ln_b': nrm(ks[14], (DEPTH, B_WIDTH), 0.01),
        'q_norm': 1.0 + nrm(ks[15], (DEPTH, C_HEAD_DIM), 0.05),
        'k_norm': 1.0 + nrm(ks[16], (DEPTH, C_HEAD_DIM), 0.05),
        'sinks': nrm(ks[17], (DEPTH, C_HEADS), 0.5),
        'w_out': nrm(ks[18], (DEPTH, MIX_WIDTH, D_MODEL), MIX_WIDTH ** -0.5),
        'norm_mix': 1.0 + nrm(ks[19], (DEPTH, D_MODEL), 0.05),
        'norm_ffn': 1.0 + nrm(ks[20], (DEPTH, D_MODEL), 0.05),
        'w_gate': nrm(ks[21], (DEPTH, D_MODEL, D_FF), D_MODEL ** -0.5),
        'w_up': nrm(ks[22], (DEPTH, D_MODEL, D_FF), D_MODEL ** -0.5),
        'w_down': nrm(ks[23], (DEPTH, D_FF, D_MODEL), D_FF ** -0.5),
        'ple_norm': 1.0 + nrm(ks[24], (DEPTH, D_MODEL), 0.05),
        'w_ple_gate': nrm(ks[25], (DEPTH, D_MODEL, D_MODEL), D_MODEL ** -0.5),
        'w_ple_proj': nrm(ks[26], (DEPTH, PLE_DIM, D_MODEL), PLE_DIM ** -0.5),
    }


def reference(x_prompt, x_sample, state_hgrn, state_conv, cache_swa_k, cache_swa_v, p_prompt, p_sample,
              a_lower, w_in, a_onorm, conv_w, conv_b, conv_ln_g, conv_ln_b, q_norm, k_norm, sinks, w_out,
              norm_mix, norm_ffn, w_gate, w_up, w_down, ple_norm, w_ple_gate, w_ple_proj):
    lbs = jnp.cumsum(jax.nn.softmax(a_lower.astype(jnp.float32), axis=0), axis=0)
    lbs = lbs - lbs[0:1]
    hp, hs = x_prompt, x_sample
    sp_h, sp_c, sp_k, sp_v = [], [], [], []
    ss_h, ss_c, ss_k, ss_v = [], [], [], []
    for l in range(DEPTH):
        wl = (w_in[l], a_onorm[l], conv_w[l], conv_b[l], conv_ln_g[l], conv_ln_b[l], q_norm[l], k_norm[l],
              sinks[l], w_out[l], norm_mix[l], norm_ffn[l], w_gate[l], w_up[l], w_down[l], ple_norm[l],
              w_ple_gate[l], w_ple_proj[l])
        hp, a1, a2, a3, a4 = _layer(hp, p_prompt[l], 0, None, None, None, None, lbs[l], wl)
        hs, b1, b2, b3, b4 = _layer(hs, p_sample[l], PAST_LEN, state_hgrn[l], state_conv[l],
                                    cache_swa_k[l], cache_swa_v[l], lbs[l], wl)
        sp_h.append(a1); sp_c.append(a2); sp_k.append(a3); sp_v.append(a4)
        ss_h.append(b1); ss_c.append(b2); ss_k.append(b3); ss_v.append(b4)
    return (hp, hs,
            jnp.stack(sp_h), jnp.stack(sp_c), jnp.stack(sp_k), jnp.stack(sp_v),
            jnp.stack(ss_h), jnp.stack(ss_c), jnp.stack(ss_k), jnp.stack(ss_v))
```

```python
import numpy as np
import ml_dtypes
from contextlib import ExitStack
import concourse.bass as bass
import concourse.mybir as mybir
from concourse.bass_utils import run_bass_kernel_spmd

F32 = mybir.dt.float32
BF16 = mybir.dt.bfloat16
AF = mybir.ActivationFunctionType
ALU = mybir.AluOpType
AX = mybir.AxisListType

NCORE = 8
D = 1024
SEQ = 16384
DEPTH = 2
NSEQ = 128
SPC = NSEQ // NCORE
DSEQ = 4
PAST = 16384
DFF = 2816
INC = 2304
EPS = 1e-6
ST = 512
SAME_SYNC = True
STAGE = 99
MSTAGE = 99
MBSEL = [0]
HSEL = [0, 1, 2, 3]


class Tl:
    def __init__(self, t):
        self.t = t
        self.w = None
        self.r = {}

    def __getitem__(self, idx):
        return self.t[idx]


class KB:
    def __init__(self, nc, es):
        self.nc = nc
        self.es = es
        self.E = {}
        for name, h in (("pe", nc.tensor), ("act", nc.scalar), ("dve", nc.vector), ("pool", nc.gpsimd), ("sp", nc.sync)):
            self.E[name] = dict(h=h, sem=es.enter_context(nc.semaphore("sem_" + name)), count=0, waited={})
        self.dsem = {q: [[es.enter_context(nc.semaphore("dsem%s%d" % (q, i))), 0] for i in range(24)] for q in ("sp", "pool")}
        self.dnext = {"sp": 0, "pool": 0}
        self.n_inst = 0

    def sb(self, name, shape, dt):
        return Tl(self.es.enter_context(self.nc.sbuf_tensor(name, shape, dt)))

    def ps(self, name, shape, dt):
        return Tl(self.es.enter_context(self.nc.psum_tensor(name, shape, dt)))

    def _wait(self, eng, r, w):
        E = self.E[eng]
        deps = {}

        def add(ev):
            if ev is None:
                return
            s, v = ev
            k = id(s)
            if k not in deps or deps[k][1] < v:
                deps[k] = (s, v)
        for t in r:
            add(t.w)
        for t in w:
            add(t.w)
            for e in t.r.values():
                add(e)
        for k, (s, v) in deps.items():
            if s is E["sem"] and (eng == "pe" or not SAME_SYNC):
                continue
            if E["waited"].get(k, 0) >= v:
                continue
            E["h"].wait_ge(s, v)
            E["waited"][k] = v

    def _note(self, ev, r, w):
        for t in r:
            t.r[id(ev[0])] = ev
        for t in w:
            t.w = ev
            t.r = {}

    def op(self, eng, fn, r=(), w=(), inc=True):
        E = self.E[eng]
        self._wait(eng, r, w)
        inst = fn(E["h"])
        self.n_inst += 1
        if inc:
            E["count"] += 1
            inst.then_inc(E["sem"], 1)
            ev = (E["sem"], E["count"])
        else:
            ev = (E["sem"], E["count"] + 1)
        self._note(ev, r, w)

    def dma(self, eng, out, in_, r=(), w=(), slow=False):
        E = self.E[eng]
        self._wait(eng, r, w)
        slot = self.dsem[eng][self.dnext[eng]]
        self.dnext[eng] = (self.dnext[eng] + 1) % len(self.dsem[eng])
        s, v = slot
        if v > 0 and E["waited"].get(id(s), 0) < v:
            E["h"].wait_ge(s, v)
            E["waited"][id(s)] = v
        if slow:
            E["h"].dma_start(out=out, in_=in_, allow_slow_non_contiguous=True).then_inc(s, 16)
        else:
            E["h"].dma_start(out=out, in_=in_).then_inc(s, 16)
        slot[1] = v + 16
        self.n_inst += 1
        self._note((s, v + 16), r, w)

    def final_wait(self):
        E = self.E["sp"]
        for q in self.dsem:
            for s, v in self.dsem[q]:
                if v > 0:
                    E["h"].wait_ge(s, v)
        for name in ("pe", "act", "dve", "pool"):
            e = self.E[name]
            if e["count"] > 0:
                E["h"].wait_ge(e["sem"], e["count"])

    def mm(self, out, lhsT, rhs, start, stop, r, w, inc=None):
        if inc is None:
            inc = stop
        self.op("pe", lambda e: e.matmul(out, lhsT=lhsT, rhs=rhs, start=start, stop=stop), r=r, w=w, inc=inc)

    def tr(self, out, in_, ident, r, w):
        self.op("pe", lambda e: e.transpose(out, in_, ident), r=r, w=w)

    def act(self, out, in_, func, r, w, bias=None, scale=None, accum=None):
        kw = {}
        if bias is not None:
            kw["bias"] = bias
        if scale is not None:
            kw["scale"] = scale
        if accum is not None:
            kw["accum_out"] = accum
        self.op("act", lambda e: e.activation(out=out, in_=in_, func=func, **kw), r=r, w=w)

    def tt(self, out, in0, in1, op, r, w, eng="dve"):
        self.op(eng, lambda e: e.tensor_tensor(out=out, in0=in0, in1=in1, op=op), r=r, w=w)

    def ts(self, out, in0, s1, s2, op0, op1, r, w, eng="dve"):
        if op1 is None:
            self.op(eng, lambda e: e.tensor_scalar(out=out, in0=in0, scalar1=s1, scalar2=None, op0=op0), r=r, w=w)
        else:
            self.op(eng, lambda e: e.tensor_scalar(out=out, in0=in0, scalar1=s1, scalar2=s2, op0=op0, op1=op1), r=r, w=w)

    def stt(self, out, in0, scalar, in1, op0, op1, r, w):
        self.op("dve", lambda e: e.scalar_tensor_tensor(out=out, in0=in0, scalar=scalar, in1=in1, op0=op0, op1=op1), r=r, w=w)

    def cp(self, out, in_, r, w, eng="dve"):
        if eng == "act":
            self.act(out, in_, AF.Copy, r, w)
        else:
            self.op(eng, lambda e: e.tensor_copy(out=out, in_=in_), r=r, w=w)

    def recip(self, out, in_, r, w):
        self.op("dve", lambda e: e.reciprocal(out=out, in_=in_), r=r, w=w)

    def red(self, out, in_, r, w):
        self.op("dve", lambda e: e.tensor_reduce(out=out, in_=in_, axis=AX.X, op=ALU.add), r=r, w=w)

    def memset(self, ap, val, w, eng="dve"):
        self.op(eng, lambda e: e.memset(ap, val), r=(), w=w)


def build_program():
    nc = bass.Bass("TRN2", target_bir_lowering=False)

    def din(name, shape):
        return nc.dram_tensor(name, list(shape), F32, kind="ExternalInput").ap()

    def dout(name, shape):
        return nc.dram_tensor(name, list(shape), F32, kind="ExternalOutput").ap()

    x_prompt = din("x_prompt", (SEQ, D))
    x_sample = din("x_sample", (SPC * DSEQ, D))
    state_hgrn = din("state_hgrn", (DEPTH, SPC, 4, 64, 64))
    state_conv = din("state_conv", (DEPTH, SPC, 30, 256))
    cache_k = din("cache_k", (DEPTH, SPC, 128, 128))
    cache_v = din("cache_v", (DEPTH, SPC, 128, 128))
    p_prompt = din("p_prompt", (DEPTH, SEQ, 256))
    p_sample = din("p_sample", (DEPTH, SPC * DSEQ, 256))
    a_lower = din("a_lower", (DEPTH, 256))
    w_in = din("w_in", (DEPTH, D, INC))
    a_onorm = din("a_onorm", (DEPTH, 64))
    conv_w = din("conv_w", (DEPTH, 31, 256))
    conv_b = din("conv_b", (DEPTH, 256))
    conv_ln_g = din("conv_ln_g", (DEPTH, 256))
    conv_ln_b = din("conv_ln_b", (DEPTH, 256))
    q_norm = din("q_norm", (DEPTH, 64))
    k_norm = din("k_norm", (DEPTH, 64))
    sinks = din("sinks", (DEPTH, 8))
    w_out = din("w_out", (DEPTH, D, D))
    norm_mix = din("norm_mix", (DEPTH, D))
    norm_ffn = din("norm_ffn", (DEPTH, D))
    w_gate = din("w_gate", (DEPTH, D, DFF))
    w_up = din("w_up", (DEPTH, D, DFF))
    w_down = din("w_down", (DEPTH, DFF, D))
    ple_norm = din("ple_norm", (DEPTH, D))
    w_ple_gate = din("w_ple_gate", (DEPTH, D, D))
    w_ple_proj = din("w_ple_proj", (DEPTH, 256, D))
    c_ident = din("c_ident", (128, 128))
    c_U = din("c_U", (128, 128))
    c_W = din("c_W", (128, 128))
    c_mprev = din("c_mprev", (128, 128))
    c_mdiag = din("c_mdiag", (128, 128))
    c_bones = din("c_bones", (128, 128))
    c_cos = din("c_cos", (SEQ + DSEQ, 256))
    c_sin = din("c_sin", (SEQ + DSEQ, 256))

    y_prompt = dout("y_prompt", (SEQ, D))
    y_sample = dout("y_sample", (SPC * DSEQ, D))
    o_hp = dout("o_hp", (DEPTH, 4, 64, 64))
    o_cp = dout("o_cp", (DEPTH, 30, 256))
    o_kp = dout("o_kp", (DEPTH, 128, 128))
    o_vp = dout("o_vp", (DEPTH, 128, 128))
    o_hs = dout("o_hs", (DEPTH, SPC, 4, 64, 64))
    o_cs = dout("o_cs", (DEPTH, SPC, 30, 256))
    o_ks = dout("o_ks", (DEPTH, SPC, 128, 128))
    o_vs = dout("o_vs", (DEPTH, SPC, 128, 128))

    es = ExitStack()
    with es:
        k = KB(nc, es)
        sb, ps = k.sb, k.ps

        identf = sb("identf", [128, 128], F32)
        identb = sb("identb", [128, 128], BF16)
        Uf = sb("Uf", [128, 128], F32)
        mprev4 = sb("mprev4", [64, 4, 64], BF16)
        mdiag4 = sb("mdiag4", [64, 4, 64], BF16)
        onesb = sb("onesb", [128, 128], BF16)
        zerosb = sb("zerosb", [128, 128], BF16)
        eps_t = sb("eps_t", [128, 1], F32)
        gate_sb = sb("gate_sb", [128, 512], F32)
        for t, src in ((identf, c_ident), (Uf, c_U)):
            k.dma("sp", t[:], src, w=[t])
        for i_, src in enumerate((c_W, c_bones, c_mprev, c_mdiag)):
            k.dma("sp", gate_sb[:, i_ * 128:(i_ + 1) * 128], src, w=[gate_sb])
        k.cp(identb[:], identf[:], r=[identf], w=[identb])
        Ub = sb("Ub", [128, 128], BF16)
        Wb = sb("Wb", [128, 128], BF16)
        bonesb = sb("bonesb", [128, 128], BF16)
        k.cp(Ub[:], Uf[:], r=[Uf], w=[Ub])
        k.cp(Wb[:], gate_sb[:, 0:128], r=[gate_sb], w=[Wb])
        k.cp(bonesb[:], gate_sb[:, 128:256], r=[gate_sb], w=[bonesb])
        for g in range(4):
            k.cp(mprev4[:, g, :], gate_sb[0:64, 256:320], r=[gate_sb], w=[mprev4])
            k.cp(mdiag4[:, g, :], gate_sb[0:64, 384:448], r=[gate_sb], w=[mdiag4])
        k.memset(onesb[:], 1.0, w=[onesb])
        k.memset(zerosb[:], 0.0, w=[zerosb])
        k.memset(eps_t[:], EPS, w=[eps_t])

        P = []
        dg = sb("dg", [128, 2, 31, 128], BF16)
        for l in range(DEPTH):
            L = {}
            for nm, src in (("g_mix", norm_mix), ("g_ffn", norm_ffn), ("g_ple", ple_norm)):
                t = sb("%s%d" % (nm, l), [128, 8], F32)
                k.dma("sp", t[:], src[l].rearrange("(c p) -> p c", p=128), w=[t], slow=True)
                L[nm] = t
            aon = sb("aon%d" % l, [64, 256], F32)
            for h in range(4):
                k.dma("sp", aon[:, h * 64:(h + 1) * 64], a_onorm[l].partition_broadcast(64), w=[aon])
            L["aon"] = aon
            qn = sb("qnr%d" % l, [64, 64], F32)
            kn = sb("knr%d" % l, [64, 64], F32)
            k.dma("sp", qn[:], q_norm[l].partition_broadcast(64), w=[qn])
            k.dma("sp", kn[:], k_norm[l].partition_broadcast(64), w=[kn])
            L["qn"], L["kn"] = qn, kn
            lbrow = sb("lbrow%d" % l, [64, 256], F32)
            omlrow = sb("omlrow%d" % l, [64, 256], F32)
            lbp = sb("lbp%d" % l, [64, 4], F32)
            omlp = sb("omlp%d" % l, [64, 4], F32)
            nomlp = sb("nomlp%d" % l, [64, 4], F32)
            if l == 0:
                k.memset(lbrow[:], 0.0, w=[lbrow])
                k.memset(lbp[:], 0.0, w=[lbp])
            else:
                a0 = sb("a0row", [64, 256], F32)
                a0p = sb("a0p", [64, 4], F32)
                k.dma("sp", lbrow[:], a_lower[1].partition_broadcast(64), w=[lbrow])
                k.dma("sp", a0[:], a_lower[0].partition_broadcast(64), w=[a0])
                k.dma("sp", lbp[:], a_lower[1].rearrange("(h p) -> p h", p=64), w=[lbp], slow=True)
                k.dma("sp", a0p[:], a_lower[0].rearrange("(h p) -> p h", p=64), w=[a0p], slow=True)
                k.tt(lbrow[:], lbrow[:], a0[:], ALU.subtract, r=[a0], w=[lbrow])
                k.tt(lbp[:], lbp[:], a0p[:], ALU.subtract, r=[a0p], w=[lbp])
                k.act(lbrow[:], lbrow[:], AF.Sigmoid, r=[], w=[lbrow])
                k.act(lbp[:], lbp[:], AF.Sigmoid, r=[], w=[lbp])
            k.ts(omlrow[:], lbrow[:], -1.0, 1.0, ALU.mult, ALU.add, r=[lbrow], w=[omlrow])
            k.ts(omlp[:], lbp[:], -1.0, 1.0, ALU.mult, ALU.add, r=[lbp], w=[omlp])
            k.ts(nomlp[:], omlp[:], -1.0, None, ALU.mult, None, r=[omlp], w=[nomlp])
            L.update(lbrow=lbrow, omlrow=omlrow, lbp=lbp, omlp=omlp, nomlp=nomlp)
            ptmp = sb("ptmp%d" % l, [128, 2, 31], F32)
            for j in range(2):
                k.dma("sp", ptmp[:, j, :], conv_w[l][:, j * 128:(j + 1) * 128].rearrange("i p -> p i"), w=[ptmp], slow=True)
            cb = sb("cb%d" % l, [128, 2], F32)
            cg = sb("cg%d" % l, [128, 2], F32)
            cbb = sb("cbb%d" % l, [128, 2], F32)
            k.dma("sp", cb[:], conv_b[l].rearrange("(j p) -> p j", p=128), w=[cb], slow=True)
            k.dma("sp", cg[:], conv_ln_g[l].rearrange("(j p) -> p j", p=128), w=[cg], slow=True)
            k.dma("sp", cbb[:], conv_ln_b[l].rearrange("(j p) -> p j", p=128), w=[cbb], slow=True)
            L.update(dg=dg, cb=cb, cg=cg, cbb=cbb, ptmp=ptmp)
            sk = sb("sk%d" % l, [64, 8], F32)
            esink = sb("esink%d" % l, [64, 8, 64], F32)
            k.dma("sp", sk[:], sinks[l].partition_broadcast(64), w=[sk])
            k.act(sk[:], sk[:], AF.Exp, r=[], w=[sk])
            for h in range(8):
                k.ts(esink[:, h, :], onesb[0:64, 0:64], sk[:, h:h + 1], None, ALU.mult, None, r=[sk, onesb], w=[esink])
            L["esink"] = esink
            P.append(L)

        NB = ST // 128
        H = [sb("H%d" % i, [128, D], F32) for i in range(NB)]
        xT = sb("xT", [128, 8, ST], BF16)
        mixT = sb("mixT", [128, 4, ST], BF16)
        ocT = sb("ocT", [64, 8, ST], BF16)
        aT = sb("aT", [128, 12, ST], BF16)
        pT = sb("pT", [128, 2, ST], BF16)
        WIN = sb("WIN", [128, 8, INC], BF16)
        WINr = [Tl(None) for _ in range(5)]
        NPAN = 3
        PAN = [sb("pan%d" % i, [128, 8, 512], BF16) for i in range(NPAN)]
        pan_i = [0]
        PSF = [ps("psf%d" % i, [128, 512], F32) for i in range(7)]
        PSB = [ps("psb%d" % i, [128, 1024], BF16) for i in range(1)]
        psf_i = [0]
        psb_i = [0]

        def psf():
            t = PSF[psf_i[0] % 7]
            psf_i[0] += 1
            return t

        def psb():
            t = PSB[0]
            psb_i[0] += 1
            return t

        cA = [0]
        cC = [0]

        def psfA():
            cA[0] += 1
            return PSF[cA[0] % 2]

        def psfC():
            cC[0] += 1
            return PSF[3 + cC[0] % 3]

        def hiloc(src_ap, f, r):
            k.cp(hi_c[:, 0:f], src_ap, r=r, w=[hi_c])
            k.tt(lo_c[:, 0:f], src_ap, hi_c[:, 0:f], ALU.subtract, r=r + [hi_c], w=[lo_c])

        def next_pan():
            t = PAN[pan_i[0] % NPAN]
            pan_i[0] += 1
            return t

        xn = sb("xn", [128, D], BF16)
        ssq = sb("ssq", [128, 8], F32)
        rst = sb("rst", [128, 8], F32)
        MB = 64
        qT_2 = [sb("qT_%d" % i_, [64, 4, MB], F32) for i_ in range(2)]
        kinT_2 = [sb("kinT_%d" % i_, [64, 4, MB], F32) for i_ in range(2)]
        sgT = sb("sgT", [128, MB], F32)
        logf_2 = [sb("logf_%d" % i_, [64, 256], F32) for i_ in range(2)]
        kin_2 = [sb("kin_%d" % i_, [64, 256], F32) for i_ in range(2)]
        ftm = sb("ftm", [64, 256], F32)
        va_2 = [sb("va_%d" % i_, [64, 256], BF16) for i_ in range(2)]
        sg_2 = [sb("sg_%d" % i_, [64, 256], F32) for i_ in range(2)]
        GT = sb("GT", [64, 4, MB], F32)
        ER = sb("ER", [64, 256], F32)
        kd = sb("kd", [64, 256], BF16)
        gm = sb("gm", [64, 4], F32)
        ngm = sb("ngm", [64, 4], F32)
        E1 = sb("E1", [64, 4, MB], F32)
        E2 = sb("E2", [64, 4, MB], F32)
        E3 = sb("E3", [64, 4, MB], F32)
        qeT = sb("qeT", [64, 4, MB], BF16)
        keT = sb("keT", [64, 4, MB], BF16)
        qgT = sb("qgT", [64, 4, MB], BF16)
        attT = sb("attT", [64, 4, MB], BF16)
        oaf = sb("oaf", [64, 256], BF16)
        UF_2 = [sb("UF_%d" % i_, [128, 2, 32 + MB], F32) for i_ in range(2)]
        UB_2 = [sb("UB_%d" % i_, [128, 2, 32 + MB], BF16) for i_ in range(2)]
        UBo_2 = [sb("UBo_%d" % i_, [128, 2, 32 + MB], BF16) for i_ in range(2)]
        yb = sb("yb", [128, MB], F32)
        ysq = sb("ysq", [128, MB], F32)
        cmean = sb("cmean", [128, MB], F32)
        cm2 = sb("cm2", [128, MB], F32)
        cvar = sb("cvar", [128, MB], F32)
        cd = sb("cd", [128, MB], F32)
        zq_2 = [sb("zq_%d" % i_, [64, 512], F32) for i_ in range(2)]
        zkv_2 = [sb("zkv_%d" % i_, [64, 256], F32) for i_ in range(2)]
        qnf = sb("qnf", [64, 512], F32)
        qr = sb("qr", [64, 512], BF16)
        rt = [sb("rt%d" % i, [64, 8, 32], F32) for i in range(4)]
        QT = sb("QT", [64, 8, MB], BF16)
        knf = sb("knf", [64, 128], F32)
        krf = sb("krf", [64, 128], F32)
        krb = sb("krb", [64, 128], BF16)
        vf = sb("vf", [64, 128], F32)
        Pm = [sb("Pm%d" % i, [64, 4, MB], BF16) for i in range(3)]
        dent = sb("dent", [64, 4, MB], F32)
        cosT = sb("cosT", [64, 256], F32)
        sinT = sb("sinT", [64, 256], F32)
        silu_sb = sb("silu_sb", [128, 128], F32)
        pb = sb("pb", [128, 256], BF16)
        tok30 = sb("tok30", [32, 256], F32)
        halo_tok = tok30
        osq = sb("osq", [64, 256], F32)
        ssqA = sb("ssqA", [64, 4], F32)
        rstA = sb("rstA", [64, 4], F32)
        hi_c = sb("hi_c", [128, MB], BF16)
        lo_c = sb("lo_c", [128, MB], BF16)
        hi_t = sb("hi_t", [128, 256], BF16)
        lo_t = sb("lo_t", [128, 256], BF16)

        def new_state(tag):
            st = dict(S=sb("S" + tag, [64, 4, 64], F32), Sb=sb("Sb" + tag, [64, 4, 64], BF16),
                      UFh=sb("UFh" + tag, [128, 2, 30], F32))
            kts = [sb("KT%s_%d" % (tag, i), [64, 2, 64], BF16) for i in range(3)]
            vbs = [sb("VB%s_%d" % (tag, i), [64, 128], BF16) for i in range(3)]
            st.update(KT_old=kts[0], KT_mid=kts[1], KT_free=kts[2], VB_old=vbs[0], VB_mid=vbs[1], VB_free=vbs[2],
                      ones_old=zerosb, ones_mid=zerosb)
            return st

        def hilo(src_ap, p, f, r):
            k.cp(hi_t[0:p, 0:f], src_ap, r=r, w=[hi_t])
            k.tt(lo_t[0:p, 0:f], src_ap, hi_t[0:p, 0:f], ALU.subtract, r=r + [hi_t], w=[lo_t])

        def tr32(out_ps_ap, src_ap, p, f, r, w):
            hilo(src_ap, p, f, r)
            k.mm(out_ps_ap, lhsT=hi_t[0:p, 0:f], rhs=identb[0:p, 0:p], start=True, stop=False, r=[hi_t, identb], w=w)
            k.mm(out_ps_ap, lhsT=lo_t[0:p, 0:f], rhs=identb[0:p, 0:p], start=False, stop=True, r=[lo_t, identb], w=w)

        def rmsnorm_T(h_t, n, grow, off):
            k.act(xn[0:n, :], h_t[0:n, :], AF.Square, r=[h_t], w=[xn, ssq], accum=ssq[0:n, 0:1])
            k.act(rst[0:n, 0:1], ssq[0:n, 0:1], AF.Sqrt, r=[ssq, eps_t], w=[rst], bias=eps_t[0:n, :], scale=1.0 / D)
            k.recip(rst[0:n, 0:1], rst[0:n, 0:1], r=[], w=[rst])
            k.ts(xn[0:n, :], h_t[0:n, :], rst[0:n, 0:1], None, ALU.mult, None, r=[h_t, rst], w=[xn])
            pt = psb()
            for c in range(8):
                k.tr(pt[:, c * 128:c * 128 + n], xn[0:n, c * 128:(c + 1) * 128], identb[0:n, 0:n], r=[xn, identb], w=[pt])
            for c in range(8):
                k.ts(xT[:, c, off:off + n], pt[:, c * 128:c * 128 + n], grow[:, c:c + 1], None, ALU.mult, None, r=[pt, grow], w=[xT])

        def lin_tok(out_ps, n, off, c0, c1, wt, wr, act_t, act_r, nk=8, kp=128, first=True, last=True):
            for kc in range(nk):
                k.mm(out_ps[0:n, 0:c1 - c0], lhsT=act_t[0:kp, kc, off:off + n], rhs=wt[0:kp, kc, c0:c1],
                     start=(first and kc == 0), stop=(last and kc == nk - 1), r=[act_r] + wr, w=[out_ps],
                     inc=(kc == nk - 1))

        def lin_feat(out_ps, n, off, c0, wt, wr, act_t, act_r, width=128, col0=0):
            for kc in range(8):
                k.mm(out_ps[0:width, col0:col0 + n], lhsT=wt[:, kc, c0:c0 + width], rhs=act_t[:, kc, off:off + n],
                     start=(kc == 0), stop=(kc == 7), r=[act_r] + wr, w=[out_ps])

        def mixer_parts(l, n, off, st, pos0, si_):
            L = P[l]
            S, Sb = st["S"], st["Sb"]
            qT, kinT, logf, kin, va, sg = qT_2[si_], kinT_2[si_], logf_2[si_], kin_2[si_], va_2[si_], sg_2[si_]
            UF, UB, UBo, zq, zkv = UF_2[si_], UB_2[si_], UBo_2[si_], zq_2[si_], zkv_2[si_]
            def proj():
                for h in range(4):
                    p_ = PSF[6]
                    yield
                    lin_feat(p_, n, off, h * 64, WIN, [WINr[0]], xT, xT, width=64)
                    yield
                    k.cp(qT[:, h, 0:n], p_[0:64, 0:n], r=[p_], w=[qT], eng="act")
                for h in range(4):
                    p_ = PSF[6]
                    yield
                    lin_feat(p_, n, off, 256 + h * 64, WIN, [WINr[0]], xT, xT, width=64)
                    yield
                    k.act(sgT[0:64, 0:n], p_[0:64, 0:n], AF.Sigmoid, r=[p_], w=[sgT])
                    yield
                    k.ts(kinT[:, h, 0:n], sgT[0:64, 0:n], L["nomlp"][:, h:h + 1], L["omlp"][:, h:h + 1], ALU.mult, ALU.add,
                         r=[sgT, L["nomlp"], L["omlp"]], w=[kinT])
                p_ = PSF[6]
                yield
                lin_tok(p_, n, off, 256, 512, WIN, [WINr[0]], xT, xT)
                yield
                k.act(ftm[0:n, :], p_[0:n, 0:256], AF.Sigmoid, r=[p_], w=[ftm])
                yield
                k.tt(ftm[0:n, :], ftm[0:n, :], L["omlrow"][0:n, :], ALU.mult, r=[L["omlrow"]], w=[ftm])
                yield
                k.tt(ftm[0:n, :], ftm[0:n, :], L["lbrow"][0:n, :], ALU.add, r=[L["lbrow"]], w=[ftm])
                yield
                k.act(logf[0:n, :], ftm[0:n, :], AF.Ln, r=[ftm], w=[logf])
                yield
                k.ts(kin[0:n, :], ftm[0:n, :], -1.0, 1.0, ALU.mult, ALU.add, r=[ftm], w=[kin])
                p_ = PSF[6]
                yield
                lin_tok(p_, n, off, 512, 1024, WIN, [WINr[1]], xT, xT)
                yield
                k.cp(va[0:n, :], p_[0:n, 0:256], r=[p_], w=[va])
                yield
                k.act(sg[0:n, :], p_[0:n, 256:512], AF.Silu, r=[p_], w=[sg])
                yield
                k.tt(sg[0:n, :], sg[0:n, :], L["aon"][0:n, :], ALU.mult, r=[L["aon"]], w=[sg])
                for j in range(2):
                    pu = PSF[6]
                    yield
                    lin_feat(pu, n, off, 1024 + j * 128, WIN, [WINr[2]], xT, xT)
                    pg = PSF[6]
                    yield
                    lin_feat(pg, n, off, 1280 + j * 128, WIN, [WINr[2]], xT, xT, col0=256)
                    yield
                    k.act(sgT[:, 0:n], pg[:, 256:256 + n], AF.Sigmoid, r=[pg], w=[sgT])
                    yield
                    k.tt(UF[:, j, 30:30 + n], pu[:, 0:n], sgT[:, 0:n], ALU.mult, r=[pu, sgT], w=[UF])
                p_ = PSF[6]
                yield
                lin_tok(p_, n, off, 1536, 2048, WIN, [WINr[3]], xT, xT)
                yield
                k.cp(zq[0:n, :], p_[0:n, :], r=[p_], w=[zq], eng="act")
                p_ = PSF[6]
                yield
                lin_tok(p_, n, off, 2048, 2304, WIN, [WINr[4]], xT, xT)
                yield
                k.cp(zkv[0:n, :], p_[0:n, 0:256], r=[p_], w=[zkv], eng="act")


                yield

            def chainA():
                yield
                hilo(logf[0:n, :], n, 256, [logf])
                for h in range(4):
                    p_ = psfA()
                    yield
                    k.mm(p_[0:64, 0:n], lhsT=hi_t[0:n, h * 64:(h + 1) * 64], rhs=Ub[0:n, 0:n], start=True, stop=False,
                         r=[hi_t, Ub], w=[p_])
                    yield
                    k.mm(p_[0:64, 0:n], lhsT=lo_t[0:n, h * 64:(h + 1) * 64], rhs=Ub[0:n, 0:n], start=False, stop=True,
                         r=[lo_t, Ub], w=[p_])
                    yield
                    k.cp(GT[:, h, 0:n], p_[0:64, 0:n], r=[p_], w=[GT], eng="act")
                p_ = psfA()
                yield
                k.mm(p_[0:n, 0:256], lhsT=Wb[0:n, 0:n], rhs=hi_t[0:n, :], start=True, stop=False, r=[hi_t, Wb], w=[p_])
                yield
                k.mm(p_[0:n, 0:256], lhsT=Wb[0:n, 0:n], rhs=lo_t[0:n, :], start=False, stop=True, r=[lo_t, Wb], w=[p_])
                yield
                k.act(ER[0:n, :], p_[0:n, 0:256], AF.Exp, r=[p_], w=[ER])
                yield
                k.tt(kd[0:n, :], kin[0:n, :], ER[0:n, :], ALU.mult, r=[kin, ER], w=[kd])
                rc = max(n // 2 - 1, 0)
                for h in range(4):
                    yield
                    k.cp(gm[:, h:h + 1], GT[:, h, rc:rc + 1], r=[GT], w=[gm])
                    yield
                    k.ts(ngm[:, h:h + 1], GT[:, h, rc:rc + 1], -1.0, None, ALU.mult, None, r=[GT], w=[ngm])
                    yield
                    k.act(E1[:, h, 0:n], GT[:, h, 0:n], AF.Exp, r=[GT, ngm], w=[E1], bias=ngm[:, h:h + 1], scale=1.0)
                    yield
                    k.act(E2[:, h, 0:n], GT[:, h, 0:n], AF.Exp, r=[GT, gm], w=[E2], bias=gm[:, h:h + 1], scale=-1.0)
                yield
                k.act(E3[:, :, 0:n], GT[:, :, 0:n], AF.Exp, r=[GT], w=[E3])
                yield
                k.tt(qeT[:, :, 0:n], qT[:, :, 0:n], E1[:, :, 0:n], ALU.mult, r=[qT, E1], w=[qeT])
                yield
                k.tt(keT[:, :, 0:n], kinT[:, :, 0:n], E2[:, :, 0:n], ALU.mult, r=[kinT, E2], w=[keT])
                yield
                k.tt(qgT[:, :, 0:n], qT[:, :, 0:n], E3[:, :, 0:n], ALU.mult, r=[qT, E3], w=[qgT])
                pA = psfA()
                for h in range(4):
                    yield
                    k.mm(pA[0:n, h * 64:h * 64 + n], lhsT=keT[:, h, 0:n], rhs=qeT[:, h, 0:n], start=True, stop=True,
                         r=[keT, qeT], w=[pA])
                for h in range(4):
                    yield
                    k.tt(attT[0:n, h, 0:n], pA[0:n, h * 64:h * 64 + n], Uf[0:n, 0:n], ALU.mult, r=[pA, Uf], w=[attT])
                pO = psfA()
                for h in range(4):
                    yield
                    k.mm(pO[0:n, h * 64:(h + 1) * 64], lhsT=attT[0:n, h, 0:n], rhs=va[0:n, h * 64:(h + 1) * 64],
                         start=True, stop=False, r=[attT, va], w=[pO])
                    yield
                    k.mm(pO[0:n, h * 64:(h + 1) * 64], lhsT=qgT[:, h, 0:n], rhs=Sb[:, h, :],
                         start=False, stop=True, r=[qgT, Sb], w=[pO])
                pU = psfA()
                for h in range(4):
                    yield
                    k.mm(pU[0:64, h * 64:(h + 1) * 64], lhsT=kd[0:n, h * 64:(h + 1) * 64], rhs=va[0:n, h * 64:(h + 1) * 64],
                         start=True, stop=True, r=[kd, va], w=[pU])
                for h in range(4):
                    yield
                    k.stt(S[:, h, :], S[:, h, :], E3[:, h, n - 1:n], pU[0:64, h * 64:(h + 1) * 64],
                          ALU.mult, ALU.add, r=[E3, pU], w=[S])
                yield
                k.cp(Sb[:], S[:], r=[S], w=[Sb])
                yield
                k.act(osq[0:n, :], pO[0:n, 0:256], AF.Square, r=[pO], w=[osq])
                yield
                k.red(ssqA[0:n, 0:4], osq[0:n, :].rearrange("p (h d) -> p h d", h=4), r=[osq], w=[ssqA])
                yield
                k.act(rstA[0:n, 0:4], ssqA[0:n, 0:4], AF.Sqrt, r=[ssqA, eps_t], w=[rstA], bias=eps_t[0:n, :], scale=1.0 / 64)
                yield
                k.recip(rstA[0:n, 0:4], rstA[0:n, 0:4], r=[], w=[rstA])
                for h in range(4):
                    cs = slice(h * 64, (h + 1) * 64)
                    yield
                    k.stt(oaf[0:n, cs], pO[0:n, cs], rstA[0:n, h:h + 1], sg[0:n, cs], ALU.mult, ALU.mult, r=[pO, rstA, sg], w=[oaf])
                pt = PSB[0]
                for m in range(2):
                    yield
                    k.tr(pt[:, m * 128:m * 128 + n], oaf[0:n, m * 128:(m + 1) * 128], identb[0:n, 0:n], r=[oaf, identb], w=[pt])
                for m in range(2):
                    yield
                    k.cp(mixT[:, m, off:off + n], pt[:, m * 128:m * 128 + n], r=[pt], w=[mixT])


                yield
            def chainB():
                st["lastUF"] = UF
                yield
                k.cp(UF[:, :, 0:30], st["UFh"][:], r=[st["UFh"]], w=[UF])
                yield
                k.cp(UB[:, :, 0:30 + n], UF[:, :, 0:30 + n], r=[UF], w=[UB])
                yield
                k.cp(UBo[:, :, 0:29 + n], UF[:, :, 1:30 + n], r=[UF], w=[UBo])
                for j in range(2):
                    pY = PSF[2]
                    for i in range(31):
                        src_ = UB[:, j, i:i + n] if i % 2 == 0 else UBo[:, j, i - 1:i - 1 + n]
                        yield
                        k.mm(pY[:, 0:n], lhsT=L["dg"][:, j, i, :], rhs=src_, start=(i == 0), stop=(i == 30),
                             r=[L["dg"], UB, UBo], w=[pY])
                    yield
                    k.ts(yb[:, 0:n], pY[:, 0:n], L["cb"][:, j:j + 1], None, ALU.add, None, r=[pY, L["cb"]], w=[yb])
                    yield
                    k.tt(ysq[:, 0:n], yb[:, 0:n], yb[:, 0:n], ALU.mult, r=[yb], w=[ysq])
                    pM = PSF[2]
                    yield
                    hiloc(yb[:, 0:n], n, [yb])
                    yield
                    k.mm(pM[:, 0:n], lhsT=bonesb[:], rhs=hi_c[:, 0:n], start=True, stop=False, r=[bonesb, hi_c], w=[pM])
                    yield
                    k.mm(pM[:, 0:n], lhsT=bonesb[:], rhs=lo_c[:, 0:n], start=False, stop=True, r=[bonesb, lo_c], w=[pM])
                    pQ = PSF[2]
                    yield
                    hiloc(ysq[:, 0:n], n, [ysq])
                    yield
                    k.mm(pQ[:, 256:256 + n], lhsT=bonesb[:], rhs=hi_c[:, 0:n], start=True, stop=False, r=[bonesb, hi_c], w=[pQ])
                    yield
                    k.mm(pQ[:, 256:256 + n], lhsT=bonesb[:], rhs=lo_c[:, 0:n], start=False, stop=True, r=[bonesb, lo_c], w=[pQ])
                    yield
                    k.cp(cmean[:, 0:n], pM[:, 0:n], r=[pM], w=[cmean], eng="act")
                    yield
                    k.tt(cm2[:, 0:n], cmean[:, 0:n], cmean[:, 0:n], ALU.mult, r=[cmean], w=[cm2])
                    yield
                    k.tt(cvar[:, 0:n], pQ[:, 256:256 + n], cm2[:, 0:n], ALU.subtract, r=[pQ, cm2], w=[cvar])
                    yield
                    k.act(cvar[:, 0:n], cvar[:, 0:n], AF.Ln, r=[eps_t], w=[cvar], bias=eps_t[:], scale=1.0)
                    yield
                    k.act(cvar[:, 0:n], cvar[:, 0:n], AF.Exp, r=[], w=[cvar], scale=-0.5)
                    yield
                    k.tt(cd[:, 0:n], yb[:, 0:n], cmean[:, 0:n], ALU.subtract, r=[yb, cmean], w=[cd])
                    yield
                    k.tt(cd[:, 0:n], cd[:, 0:n], cvar[:, 0:n], ALU.mult, r=[cvar], w=[cd])
                    yield
                    k.ts(cd[:, 0:n], cd[:, 0:n], L["cg"][:, j:j + 1], L["cbb"][:, j:j + 1], ALU.mult, ALU.add, r=[L["cg"], L["cbb"]], w=[cd])
                    yield
                    k.act(mixT[:, 2 + j, off:off + n], cd[:, 0:n], AF.Silu, r=[cd], w=[mixT])
                yield
                k.cp(st["UFh"][:], UF[:, :, n:n + 30], r=[UF], w=[st["UFh"]])


                yield
            def chainC():
                yield
                k.dma("sp", cosT[0:n, :], c_cos[pos0:pos0 + n, :], w=[cosT])
                yield
                k.dma("sp", sinT[0:n, :], c_sin[pos0:pos0 + n, :], w=[sinT])
                yield
                k.act(qnf[0:n, :], zq[0:n, :], AF.Square, r=[zq], w=[qnf])
                yield
                k.red(ssq[0:n, 0:8], qnf[0:n, :].rearrange("p (h d) -> p h d", h=8), r=[qnf], w=[ssq])
                yield
                k.act(rst[0:n, 0:8], ssq[0:n, 0:8], AF.Sqrt, r=[ssq, eps_t], w=[rst], bias=eps_t[0:n, :], scale=1.0 / 64)
                yield
                k.recip(rst[0:n, 0:8], rst[0:n, 0:8], r=[], w=[rst])
                for h in range(8):
                    cs = slice(h * 64, (h + 1) * 64)
                    yield
                    k.stt(qnf[0:n, cs], zq[0:n, cs], rst[0:n, h:h + 1], L["qn"][0:n, :], ALU.mult, ALU.mult, r=[zq, rst, L["qn"]], w=[qnf])
                cosv = cosT[0:n, :].rearrange("p (m d) -> p m d", m=8)
                sinv = sinT[0:n, :].rearrange("p (m d) -> p m d", m=8)
                src = qnf[0:n, :].rearrange("p (m d) -> p m d", m=8)
                dst = qr[0:n, :].rearrange("p (m d) -> p m d", m=8)
                x1, x2 = src[:, :, 0:32], src[:, :, 32:64]
                a, b, c, d_ = [t[0:n] for t in rt]
                yield
                k.tt(a, x1, cosv, ALU.mult, r=[qnf, cosT], w=[rt[0]])
                yield
                k.tt(b, x2, sinv, ALU.mult, r=[qnf, sinT], w=[rt[1]])
                yield
                k.tt(dst[:, :, 0:32], a, b, ALU.subtract, r=[rt[0], rt[1]], w=[qr])
                yield
                k.tt(c, x2, cosv, ALU.mult, r=[qnf, cosT], w=[rt[2]])
                yield
                k.tt(d_, x1, sinv, ALU.mult, r=[qnf, sinT], w=[rt[3]])
                yield
                k.tt(dst[:, :, 32:64], c, d_, ALU.add, r=[rt[2], rt[3]], w=[qr])
                pt = PSB[0]
                for h in range(8):
                    yield
                    k.tr(pt[0:64, 256 + h * 64:256 + h * 64 + n], qr[0:n, h * 64:(h + 1) * 64], identb[0:n, 0:n], r=[qr, identb], w=[pt])
                for h in range(8):
                    yield
                    k.cp(QT[:, h, 0:n], pt[0:64, 256 + h * 64:256 + h * 64 + n], r=[pt], w=[QT])
                yield
                k.act(knf[0:n, :], zkv[0:n, 0:128], AF.Square, r=[zkv], w=[knf])
                yield
                k.red(ssq[0:n, 0:2], knf[0:n, :].rearrange("p (h d) -> p h d", h=2), r=[knf], w=[ssq])
                yield
                k.act(rst[0:n, 0:2], ssq[0:n, 0:2], AF.Sqrt, r=[ssq, eps_t], w=[rst], bias=eps_t[0:n, :], scale=1.0 / 64)
                yield
                k.recip(rst[0:n, 0:2], rst[0:n, 0:2], r=[], w=[rst])
                for h in range(2):
                    cs = slice(h * 64, (h + 1) * 64)
                    yield
                    k.stt(knf[0:n, cs], zkv[0:n, cs], rst[0:n, h:h + 1], L["kn"][0:n, :], ALU.mult, ALU.mult, r=[zkv, rst, L["kn"]], w=[knf])
                src = knf[0:n, :].rearrange("p (m d) -> p m d", m=2)
                x1, x2 = src[:, :, 0:32], src[:, :, 32:64]
                dst = krf[0:n, :].rearrange("p (m d) -> p m d", m=2)
                cos2 = cosT[0:n, 0:64].rearrange("p (m d) -> p m d", m=2)
                sin2 = sinT[0:n, 0:64].rearrange("p (m d) -> p m d", m=2)
                a, b, c, d_ = [t[0:n, 0:2, :] for t in rt]
                yield
                k.tt(a, x1, cos2, ALU.mult, r=[knf, cosT], w=[rt[0]])
                yield
                k.tt(b, x2, sin2, ALU.mult, r=[knf, sinT], w=[rt[1]])
                yield
                k.tt(dst[:, :, 0:32], a, b, ALU.subtract, r=[rt[0], rt[1]], w=[krf])
                yield
                k.tt(c, x2, cos2, ALU.mult, r=[knf, cosT], w=[rt[2]])
                yield
                k.tt(d_, x1, sin2, ALU.mult, r=[knf, sinT], w=[rt[3]])
                yield
                k.tt(dst[:, :, 32:64], c, d_, ALU.add, r=[rt[2], rt[3]], w=[krf])
                yield
                k.cp(krb[0:n, :], krf[0:n, :], r=[krf], w=[krb])
                KTc, VBc = st["KT_free"], st["VB_free"]
                st["cur"] = (KTc, VBc)
                pt = PSB[0]
                for kv in range(2):
                    yield
                    k.tr(pt[0:64, 768 + kv * 64:768 + kv * 64 + n], krb[0:n, kv * 64:(kv + 1) * 64], identb[0:n, 0:n], r=[krb, identb], w=[pt])
                for kv in range(2):
                    yield
                    k.cp(KTc[:, kv, 0:n], pt[0:64, 768 + kv * 64:768 + kv * 64 + n], r=[pt], w=[KTc])
                yield
                k.cp(vf[0:n, :], zkv[0:n, 128:256], r=[zkv], w=[vf], eng="act")
                yield
                k.cp(VBc[0:n, :], zkv[0:n, 128:256], r=[zkv], w=[VBc])
                kblocks = ((st["KT_old"], st["VB_old"], st["ones_old"], 64, mprev4),
                           (st["KT_mid"], st["VB_mid"], st["ones_mid"], 64, None),
                           (KTc, VBc, onesb, n, mdiag4))
                for kv in range(2):
                    pN = PSF[3]
                    ND = 0
                    pD = PSF[4]
                    for bi, (KTk, VBk, ones_k, nk, mask) in enumerate(kblocks):
                        pS = PSF[5]
                        yield
                        k.mm(pS[0:nk, 0:4 * n], lhsT=KTk[:, kv, 0:nk], rhs=QT[:, kv * 4:(kv + 1) * 4, 0:n], start=True, stop=True,
                             r=[KTk, QT], w=[pS])
                        Pt = Pm[bi]
                        yield
                        k.act(Pt[0:nk, :, 0:n], pS[0:nk, 0:4 * n].rearrange("p (m t) -> p m t", m=4), AF.Exp, r=[pS], w=[Pt], scale=0.125)
                        if mask is not None:
                            yield
                            k.tt(Pt[0:nk, :, 0:n], Pt[0:nk, :, 0:n], mask[0:nk, :, 0:n], ALU.mult, r=[mask], w=[Pt])
                        yield
                        k.mm(pN[0:64, 0:4 * n], lhsT=VBk[0:nk, kv * 64:(kv + 1) * 64], rhs=Pt[0:nk, :, 0:n], start=(bi == 0), stop=(bi == 2),
                             r=[VBk, Pt], w=[pN], inc=True)
                        yield
                        k.mm(pD[0:64, ND:ND + 4 * n], lhsT=ones_k[0:nk, 0:64], rhs=Pt[0:nk, :, 0:n], start=(bi == 0), stop=(bi == 2),
                             r=[ones_k, Pt], w=[pD], inc=True)
                    yield
                    k.tt(dent[:, :, 0:n], pD[0:64, ND:ND + 4 * n].rearrange("p (m t) -> p m t", m=4), L["esink"][:, kv * 4:(kv + 1) * 4, 0:n],
                         ALU.add, r=[pD, L["esink"]], w=[dent])
                    yield
                    k.recip(dent[:, :, 0:n], dent[:, :, 0:n], r=[], w=[dent])
                    yield
                    k.tt(ocT[:, kv * 4:(kv + 1) * 4, off:off + n], pN[0:64, 0:4 * n].rearrange("p (m t) -> p m t", m=4), dent[:, :, 0:n],
                         ALU.mult, r=[pN, dent], w=[ocT])

                yield
            def chains():
                gens = [chainA(), chainB(), chainC()]
                while gens:
                    for g_ in list(gens):
                        try:
                            next(g_)
                        except StopIteration:
                            gens.remove(g_)
                    yield
                if n == 64:
                    KTc, VBc = st["cur"]
                    st["KT_free"], st["KT_old"], st["KT_mid"] = st["KT_old"], st["KT_mid"], KTc
                    st["VB_free"], st["VB_old"], st["VB_mid"] = st["VB_old"], st["VB_mid"], VBc
                    st["ones_old"], st["ones_mid"] = st["ones_mid"], onesb

            return proj(), chains()

        def drain(g_):
            for _ in g_:
                pass

        def run_pipelined(part_fns, posts):
            prev = None
            for i, fn in enumerate(part_fns):
                pj, ch = fn()
                if prev is None:
                    drain(pj)
                else:
                    a_done = b_done = False
                    while not (a_done and b_done):
                        if not a_done:
                            try:
                                next(prev)
                            except StopIteration:
                                a_done = True
                                if posts[i - 1]:
                                    posts[i - 1]()
                        if not b_done:
                            try:
                                next(pj)
                            except StopIteration:
                                b_done = True
                prev = ch
            drain(prev)
            if posts[-1]:
                posts[-1]()

        def load_win(l):
            wv = w_in[l].rearrange("(c p) n -> p c n", p=128)
            for pi in range(5):
                c0, c1 = pi * 512, min((pi + 1) * 512, INC)
                k.dma("pool", WIN[:, :, c0:c1], wv[:, :, c0:c1], w=[WINr[pi]])

        def load_pan(src_ap, nk, ncol, p0=0, rows=slice(0, 128), t=None):
            if t is None:
                t = next_pan()
            k.dma("pool", t[rows, p0:p0 + nk, 0:ncol], src_ap, w=[t])
            return t

        def layer(l, blocks, mix_blocks, p_src):
            L = P[l]
            load_win(l)
            for j in range(2):
                for i in range(31):
                    k.ts(dg[:, j, i, :], identf[:], L["ptmp"][:, j, i:i + 1], None, ALU.mult, None, r=[L["ptmp"], identf], w=[dg])
            for off, n, Ht in blocks:
                rmsnorm_T(Ht, n, L["g_mix"], off)
            mix_blocks(l)
            wo = w_out[l]
            for ch in range(2):
                cs = slice(ch * 512, (ch + 1) * 512)
                t1 = load_pan(wo[0:512, cs].rearrange("(c p) n -> p c n", p=128), 4, 512)
                t2 = load_pan(wo[512:1024, cs].rearrange("(h d) n -> d h n", d=64), 8, 512, rows=slice(0, 64))
                for off, n, Ht in blocks:
                    p_ = psf()
                    lin_tok(p_, n, off, 0, 512, t1, [t1], mixT, mixT, nk=4, last=False)
                    lin_tok(p_, n, off, 0, 512, t2, [t2], ocT, ocT, nk=8, kp=64, first=False)
                    k.tt(Ht[0:n, cs], Ht[0:n, cs], p_[0:n, :], ALU.add, r=[p_], w=[Ht])
            for off, n, Ht in blocks:
                rmsnorm_T(Ht, n, L["g_ffn"], off)
            wg = w_gate[l].rearrange("(c p) n -> p c n", p=128)
            wu = w_up[l].rearrange("(c p) n -> p c n", p=128)
            wd = w_down[l].rearrange("(c p) n -> p c n", p=128)
            ntok = blocks[-1][0] + blocks[-1][1]
            for half, (f0, f1) in enumerate(((0, 12), (12, 22))):
                for pi in range(f0 // 4, (f1 + 3) // 4):
                    c0, c1 = pi * 512, min((pi + 1) * 512, DFF)
                    tg = load_pan(wg[:, :, c0:c1], 8, c1 - c0)
                    tu = load_pan(wu[:, :, c0:c1], 8, c1 - c0)
                    for j in range((c1 - c0) // 128):
                        fc = (c0 // 128) + j - f0
                        pg_ = psf()
                        lin_feat(pg_, ntok, 0, j * 128, tg, [tg], xT, xT)
                        pu_ = psf()
                        lin_feat(pu_, ntok, 0, j * 128, tu, [tu], xT, xT)
                        k.act(aT[:, fc, 0:ntok], pg_[:, 0:ntok], AF.Silu, r=[pg_], w=[aT])
                        k.tt(aT[:, fc, 0:ntok], pu_[:, 0:ntok], aT[:, fc, 0:ntok], ALU.mult, r=[pu_], w=[aT])
                nch = f1 - f0
                groups = [(g0, min(8, nch - g0)) for g0 in range(0, nch, 8)]
                for ch in range(2):
                    cs = slice(ch * 512, (ch + 1) * 512)
                    accs = [psf() for _ in blocks]
                    for (g0, nk) in groups:
                        t = load_pan(wd[:, f0 + g0:f0 + g0 + nk, cs], nk, 512)
                        for bi, (off, n, Ht) in enumerate(blocks):
                            for kc in range(nk):
                                k.mm(accs[bi][0:n, :], lhsT=aT[:, g0 + kc, off:off + n], rhs=t[:, kc, :],
                                     start=(g0 + kc == 0), stop=(g0 + kc == nch - 1), r=[aT, t], w=[accs[bi]],
                                     inc=(kc == nk - 1))
                    for bi, (off, n, Ht) in enumerate(blocks):
                        k.tt(Ht[0:n, cs], Ht[0:n, cs], accs[bi][0:n, :], ALU.add, r=[accs[bi]], w=[Ht])
            for off, n, Ht in blocks:
                rmsnorm_T(Ht, n, L["g_ple"], off)
                k.dma("pool", pb[0:n, :], p_src(l, off, n), w=[pb])
                pt = psb()
                for c in range(2):
                    k.tr(pt[:, c * 128:c * 128 + n], pb[0:n, c * 128:(c + 1) * 128], identb[0:n, 0:n], r=[pb, identb], w=[pt])
                for c in range(2):
                    k.cp(pT[:, c, off:off + n], pt[:, c * 128:c * 128 + n], r=[pt], w=[pT])
            wpg = w_ple_gate[l].rearrange("(c p) n -> p c n", p=128)
            wpp = w_ple_proj[l].rearrange("(c p) n -> p c n", p=128)
            for ch in range(2):
                cs = slice(ch * 512, (ch + 1) * 512)
                tg = load_pan(wpg[:, :, cs], 8, 512)
                tp = load_pan(wpp[:, :, cs], 2, 512)
                for off, n, Ht in blocks:
                    p1 = psf()
                    lin_tok(p1, n, off, 0, 512, tg, [tg], xT, xT)
                    p2 = psf()
                    lin_tok(p2, n, off, 0, 512, tp, [tp], pT, pT, nk=2)
                    k.act(gate_sb[0:n, :], p1[0:n, :], AF.Sigmoid, r=[p1], w=[gate_sb])
                    k.tt(gate_sb[0:n, :], p2[0:n, :], gate_sb[0:n, :], ALU.mult, r=[p2], w=[gate_sb])
                    k.tt(Ht[0:n, cs], Ht[0:n, cs], gate_sb[0:n, :], ALU.add, r=[gate_sb], w=[Ht])

        sst = new_state("s")
        kc_old = sb("kc_old", [64, 128], BF16)
        kc_mid = sb("kc_mid", [64, 128], BF16)

        def sample_mix_block(b):
            def run(l):
                st = sst
                st["ones_old"], st["ones_mid"] = onesb, onesb
                S = st["S"]
                k.dma("sp", S[:], state_hgrn[l, b].rearrange("h k v -> k h v"), w=[S])
                k.cp(st["Sb"][:], S[:], r=[S], w=[st["Sb"]])
                k.dma("sp", halo_tok[0:30, :], state_conv[l, b], w=[halo_tok])
                for j in range(2):
                    p_ = psf()
                    tr32(p_[:, 0:30], halo_tok[0:30, j * 128:(j + 1) * 128], 30, 128, [halo_tok], [p_])
                    k.cp(st["UFh"][:, j, :], p_[:, 0:30], r=[p_], w=[st["UFh"]], eng="act")
                k.dma("pool", kc_old[:], cache_k[l, b, 0:64, :], w=[kc_old])
                k.dma("pool", kc_mid[:], cache_k[l, b, 64:128, :], w=[kc_mid])
                k.dma("pool", st["VB_old"][:], cache_v[l, b, 0:64, :], w=[st["VB_old"]])
                k.dma("pool", st["VB_mid"][:], cache_v[l, b, 64:128, :], w=[st["VB_mid"]])
                for src_t, dst_t in ((kc_old, st["KT_old"]), (kc_mid, st["KT_mid"])):
                    pt = psb()
                    for kv in range(2):
                        k.tr(pt[0:64, kv * 64:(kv + 1) * 64], src_t[:, kv * 64:(kv + 1) * 64], identb[0:64, 0:64], r=[src_t, identb], w=[pt])
                    for kv in range(2):
                        k.cp(dst_t[:, kv, :], pt[0:64, kv * 64:(kv + 1) * 64], r=[pt], w=[dst_t])
                pj, ch = mixer_parts(l, DSEQ, b * DSEQ, st, SEQ, 0)
                drain(pj)
                drain(ch)
                UF = st["lastUF"]
                k.dma("sp", o_hs[l, b].rearrange("h k v -> k h v"), S[:], r=[S])
                k.dma("sp", o_cs[l, b, 0:26, :], state_conv[l, b, 4:30, :])
                for j in range(2):
                    p_ = psf()
                    tr32(p_[0:DSEQ, 0:128], UF[:, j, 30:30 + DSEQ], 128, DSEQ, [UF], [p_])
                    k.cp(tok30[0:DSEQ, j * 128:(j + 1) * 128], p_[0:DSEQ, 0:128], r=[p_], w=[tok30], eng="act")
                k.dma("sp", o_cs[l, b, 26:30, :], tok30[0:DSEQ, :], r=[tok30])
                k.dma("sp", o_ks[l, b, 0:124, :], cache_k[l, b, 4:128, :])
                k.dma("sp", o_vs[l, b, 0:124, :], cache_v[l, b, 4:128, :])
                k.dma("sp", o_ks[l, b, 124:128, :], krf[0:DSEQ, :], r=[krf])
                k.dma("sp", o_vs[l, b, 124:128, :], vf[0:DSEQ, :], r=[vf])
            return run

        NS = SPC * DSEQ
        k.dma("sp", H[0][0:NS, :], x_sample, w=[H[0]])
        sblocks = [(0, NS, H[0])]
        for l in range(DEPTH):
            layer(l, sblocks, lambda l_: [sample_mix_block(b)(l_) for b in range(SPC)], lambda l_, off, n: p_sample[l_, off:off + n, :])
        k.dma("sp", y_sample, H[0][0:NS, :], r=[H[0]])
        if STAGE <= 4:
            k.final_wait()
            return nc

        pst = []
        for l in range(DEPTH):
            st = new_state("p%d" % l)
            for nm in ("S", "Sb", "UFh", "KT_old", "KT_mid", "VB_old", "VB_mid"):
                k.memset(st[nm][:], 0.0, w=[st[nm]])
            pst.append(st)
        nst = SEQ // ST
        NMB = ST // 64
        for si in range(nst):
            t0 = si * ST
            blocks = []
            for bi in range(NB):
                k.dma("sp", H[bi][:], x_prompt[t0 + bi * 128:t0 + (bi + 1) * 128, :], w=[H[bi]])
                blocks.append((bi * 128, 128, H[bi]))
            for l in range(DEPTH):
                def mk_post(mi, l_):
                    if si != nst - 1 or mi < NMB - 2:
                        return None

                    def post():
                        half = mi - (NMB - 2)
                        k.dma("sp", o_kp[l_, half * 64:(half + 1) * 64, :], krf[0:64, :], r=[krf])
                        k.dma("sp", o_vp[l_, half * 64:(half + 1) * 64, :], vf[0:64, :], r=[vf])
                        if mi == NMB - 1:
                            stt_ = pst[l_]
                            UF = stt_["lastUF"]
                            k.dma("sp", o_hp[l_].rearrange("h k v -> k h v"), stt_["S"][:], r=[stt_["S"]])
                            for j in range(2):
                                p_ = psf()
                                tr32(p_[0:30, 0:128], UF[:, j, 64:94], 128, 30, [UF], [p_])
                                k.cp(tok30[0:30, j * 128:(j + 1) * 128], p_[0:30, 0:128], r=[p_], w=[tok30], eng="act")
                            k.dma("sp", o_cp[l_], tok30[0:30, :], r=[tok30])
                    return post

                def run_mix(l_):
                    fns = [(lambda mi=mi: mixer_parts(l_, 64, mi * 64, pst[l_], t0 + mi * 64, mi % 2)) for mi in range(NMB)]
                    run_pipelined(fns, [mk_post(mi, l_) for mi in range(NMB)])
                layer(l, blocks, run_mix, lambda l_, off, n: p_prompt[l_, t0 + off:t0 + off + n, :])
            for bi in range(NB):
                k.dma("sp", y_prompt[t0 + bi * 128:t0 + (bi + 1) * 128, :], H[bi][:], r=[H[bi]])
        k.final_wait()
    return nc


_CONSTS = None


def _consts():
    global _CONSTS
    if _CONSTS is None:
        i = np.arange(128)
        same = (i[:, None] // 64) == (i[None, :] // 64)
        U = ((i[:, None] <= i[None, :]) & same).astype(np.float32)
        W = ((i[:, None] > i[None, :]) & same).astype(np.float32)
        mprev = (i[:, None] >= i[None, :]).astype(np.float32)
        mdiag = (i[:, None] <= i[None, :]).astype(np.float32)
        bones = (same.astype(np.float32) / 64.0).astype(np.float32)
        half = 32
        inv = (10000.0 ** (-np.arange(half, dtype=np.float32) / half)).astype(np.float32)
        pos = np.concatenate([np.arange(SEQ, dtype=np.float32), PAST + np.arange(DSEQ, dtype=np.float32)])
        ang = (pos[:, None] * inv[None, :]).astype(np.float32)
        cos = np.tile(np.cos(ang).astype(np.float32), (1, 8))
        sin = np.tile(np.sin(ang).astype(np.float32), (1, 8))
        _CONSTS = dict(c_ident=np.eye(128, dtype=np.float32), c_U=U, c_W=W, c_mprev=mprev, c_mdiag=mdiag,
                       c_bones=bones, c_cos=np.ascontiguousarray(cos), c_sin=np.ascontiguousarray(sin))
    return _CONSTS


def kernel(x_prompt, x_sample, state_hgrn, state_conv, cache_swa_k, cache_swa_v, p_prompt, p_sample,
           a_lower, w_in, a_onorm, conv_w, conv_b, conv_ln_g, conv_ln_b, q_norm, k_norm, sinks, w_out,
           norm_mix, norm_ffn, w_gate, w_up, w_down, ple_norm, w_ple_gate, w_ple_proj):
    f = lambda a: np.ascontiguousarray(np.asarray(a, dtype=np.float32))
    shared = dict(x_prompt=f(x_prompt).reshape(SEQ, D), p_prompt=f(p_prompt).reshape(DEPTH, SEQ, 256),
                  a_lower=f(a_lower), w_in=f(w_in), a_onorm=f(a_onorm), conv_w=f(conv_w), conv_b=f(conv_b),
                  conv_ln_g=f(conv_ln_g), conv_ln_b=f(conv_ln_b), q_norm=f(q_norm), k_norm=f(k_norm), sinks=f(sinks),
                  w_out=f(w_out), norm_mix=f(norm_mix), norm_ffn=f(norm_ffn), w_gate=f(w_gate), w_up=f(w_up),
                  w_down=f(w_down), ple_norm=f(ple_norm), w_ple_gate=f(w_ple_gate), w_ple_proj=f(w_ple_proj))
    shared.update(_consts())
    xs, sh, sc = f(x_sample), f(state_hgrn), f(state_conv)
    ck, cv, ps_ = f(cache_swa_k), f(cache_swa_v), f(p_sample)
    in_maps = []
    for c in range(NCORE):
        b = slice(c * SPC, (c + 1) * SPC)
        m = dict(shared)
        m.update(x_sample=np.ascontiguousarray(xs[b]).reshape(SPC * DSEQ, D),
                 state_hgrn=np.ascontiguousarray(sh[:, b]),
                 state_conv=np.ascontiguousarray(sc[:, b]),
                 cache_k=np.ascontiguousarray(ck[:, b]).reshape(DEPTH, SPC, 128, 128),
                 cache_v=np.ascontiguousarray(cv[:, b]).reshape(DEPTH, SPC, 128, 128),
                 p_sample=np.ascontiguousarray(ps_[:, b]).reshape(DEPTH, SPC * DSEQ, 256))
        in_maps.append(m)
    nc = build_program()
    res = run_bass_kernel_spmd(nc, in_maps, core_ids=list(range(NCORE)))
    R = res.results
    cat = lambda name, ax: np.concatenate([R[c][name] for c in range(NCORE)], axis=ax)
    y_p = R[0]["y_prompt"].reshape(1, SEQ, D)
    y_s = cat("y_sample", 0).reshape(NSEQ, DSEQ, D)
    return (y_p.astype(np.float32), y_s.astype(np.float32),
            R[0]["o_hp"].reshape(DEPTH, 1, 4, 64, 64), R[0]["o_cp"].reshape(DEPTH, 1, 30, 256),
            R[0]["o_kp"].reshape(DEPTH, 1, 128, 2, 64), R[0]["o_vp"].reshape(DEPTH, 1, 128, 2, 64),
            cat("o_hs", 1), cat("o_cs", 1),
            cat("o_ks", 1).reshape(DEPTH, NSEQ, 128, 2, 64), cat("o_vs", 1).reshape(DEPTH, NSEQ, 128, 2, 64))
```

```python
import numpy as np
import ml_dtypes
from contextlib import ExitStack
import concourse.bass as bass
import concourse.mybir as mybir
from concourse.bass_utils import run_bass_kernel_spmd

F32 = mybir.dt.float32
BF16 = mybir.dt.bfloat16
AF = mybir.ActivationFunctionType
ALU = mybir.AluOpType
AX = mybir.AxisListType

NCORE = 8
D = 1024
SEQ = 16384
DEPTH = 2
NSEQ = 128
SPC = NSEQ // NCORE
DSEQ = 4
PAST = 16384
DFF = 2816
INC = 2304
EPS = 1e-6
ST = 512
SAME_SYNC = True
STAGE = 99
MSTAGE = 99
MBSEL = [0]
HSEL = [0, 1, 2, 3]


class Tl:
    def __init__(self, t):
        self.t = t
        self.w = None
        self.r = {}

    def __getitem__(self, idx):
        return self.t[idx]


class TlView(Tl):
    def __init__(self, parent, ap):
        self.t = ap
        self.p = parent

    w = property(lambda self: self.p.w, lambda self, v: setattr(self.p, "w", v))
    r = property(lambda self: self.p.r, lambda self, v: setattr(self.p, "r", v))


class KB:
    def __init__(self, nc, es):
        self.nc = nc
        self.es = es
        self.E = {}
        for name, h in (("pe", nc.tensor), ("act", nc.scalar), ("dve", nc.vector), ("pool", nc.gpsimd), ("sp", nc.sync)):
            self.E[name] = dict(h=h, sem=es.enter_context(nc.semaphore("sem_" + name)), count=0, waited={})
        self.dsem = {q: [[es.enter_context(nc.semaphore("dsem%s%d" % (q, i))), 0] for i in range(24)] for q in ("sp", "pool")}
        self.dnext = {"sp": 0, "pool": 0}
        self.n_inst = 0

    def sb(self, name, shape, dt):
        return Tl(self.es.enter_context(self.nc.sbuf_tensor(name, shape, dt)))

    def ps(self, name, shape, dt):
        return Tl(self.es.enter_context(self.nc.psum_tensor(name, shape, dt)))

    def _wait(self, eng, r, w):
        E = self.E[eng]
        deps = {}

        def add(ev):
            if ev is None:
                return
            s, v = ev
            k = id(s)
            if k not in deps or deps[k][1] < v:
                deps[k] = (s, v)
        for t in r:
            add(t.w)
        for t in w:
            add(t.w)
            for e in t.r.values():
                add(e)
        for k, (s, v) in deps.items():
            if s is E["sem"] and (eng == "pe" or not SAME_SYNC):
                continue
            if E["waited"].get(k, 0) >= v:
                continue
            E["h"].wait_ge(s, v)
            E["waited"][k] = v

    def _note(self, ev, r, w):
        for t in r:
            t.r[id(ev[0])] = ev
        for t in w:
            t.w = ev
            t.r = {}

    def op(self, eng, fn, r=(), w=(), inc=True):
        E = self.E[eng]
        self._wait(eng, r, w)
        inst = fn(E["h"])
        self.n_inst += 1
        if inc:
            E["count"] += 1
            inst.then_inc(E["sem"], 1)
            ev = (E["sem"], E["count"])
        else:
            ev = (E["sem"], E["count"] + 1)
        self._note(ev, r, w)

    def dma(self, eng, out, in_, r=(), w=(), slow=False):
        E = self.E[eng]
        self._wait(eng, r, w)
        slot = self.dsem[eng][self.dnext[eng]]
        self.dnext[eng] = (self.dnext[eng] + 1) % len(self.dsem[eng])
        s, v = slot
        if v > 0 and E["waited"].get(id(s), 0) < v:
            E["h"].wait_ge(s, v)
            E["waited"][id(s)] = v
        if slow:
            E["h"].dma_start(out=out, in_=in_, allow_slow_non_contiguous=True).then_inc(s, 16)
        else:
            E["h"].dma_start(out=out, in_=in_).then_inc(s, 16)
        slot[1] = v + 16
        self.n_inst += 1
        self._note((s, v + 16), r, w)

    def final_wait(self):
        E = self.E["sp"]
        for q in self.dsem:
            for s, v in self.dsem[q]:
                if v > 0:
                    E["h"].wait_ge(s, v)
        for name in ("pe", "act", "dve", "pool"):
            e = self.E[name]
            if e["count"] > 0:
                E["h"].wait_ge(e["sem"], e["count"])

    def mm(self, out, lhsT, rhs, start, stop, r, w, inc=None):
        if inc is None:
            inc = stop
        self.op("pe", lambda e: e.matmul(out, lhsT=lhsT, rhs=rhs, start=start, stop=stop), r=r, w=w, inc=inc)

    def tr(self, out, in_, ident, r, w):
        self.op("pe", lambda e: e.transpose(out, in_, ident), r=r, w=w)

    def act(self, out, in_, func, r, w, bias=None, scale=None, accum=None):
        kw = {}
        if bias is not None:
            kw["bias"] = bias
        if scale is not None:
            kw["scale"] = scale
        if accum is not None:
            kw["accum_out"] = accum
        self.op("act", lambda e: e.activation(out=out, in_=in_, func=func, **kw), r=r, w=w)

    def tt(self, out, in0, in1, op, r, w, eng="dve"):
        self.op(eng, lambda e: e.tensor_tensor(out=out, in0=in0, in1=in1, op=op), r=r, w=w)

    def ts(self, out, in0, s1, s2, op0, op1, r, w, eng="dve"):
        if op1 is None:
            self.op(eng, lambda e: e.tensor_scalar(out=out, in0=in0, scalar1=s1, scalar2=None, op0=op0), r=r, w=w)
        else:
            self.op(eng, lambda e: e.tensor_scalar(out=out, in0=in0, scalar1=s1, scalar2=s2, op0=op0, op1=op1), r=r, w=w)

    def stt(self, out, in0, scalar, in1, op0, op1, r, w):
        self.op("dve", lambda e: e.scalar_tensor_tensor(out=out, in0=in0, scalar=scalar, in1=in1, op0=op0, op1=op1), r=r, w=w)

    def cp(self, out, in_, r, w, eng="dve"):
        if eng == "act":
            self.act(out, in_, AF.Copy, r, w)
        else:
            self.op(eng, lambda e: e.tensor_copy(out=out, in_=in_), r=r, w=w)

    def recip(self, out, in_, r, w):
        self.op("dve", lambda e: e.reciprocal(out=out, in_=in_), r=r, w=w)

    def red(self, out, in_, r, w):
        self.op("dve", lambda e: e.tensor_reduce(out=out, in_=in_, axis=AX.X, op=ALU.add), r=r, w=w)

    def memset(self, ap, val, w, eng="dve"):
        self.op(eng, lambda e: e.memset(ap, val), r=(), w=w)


def build_program():
    nc = bass.Bass("TRN2", target_bir_lowering=False)

    def din(name, shape):
        return nc.dram_tensor(name, list(shape), F32, kind="ExternalInput").ap()

    def dout(name, shape):
        return nc.dram_tensor(name, list(shape), F32, kind="ExternalOutput").ap()

    x_prompt = din("x_prompt", (SEQ, D))
    x_sample = din("x_sample", (SPC * DSEQ, D))
    state_hgrn = din("state_hgrn", (DEPTH, SPC, 4, 64, 64))
    state_conv = din("state_conv", (DEPTH, SPC, 30, 256))
    cache_k = din("cache_k", (DEPTH, SPC, 128, 128))
    cache_v = din("cache_v", (DEPTH, SPC, 128, 128))
    p_prompt = din("p_prompt", (DEPTH, SEQ, 256))
    p_sample = din("p_sample", (DEPTH, SPC * DSEQ, 256))
    a_lower = din("a_lower", (DEPTH, 256))
    w_in = din("w_in", (DEPTH, D, INC))
    a_onorm = din("a_onorm", (DEPTH, 64))
    conv_w = din("conv_w", (DEPTH, 31, 256))
    conv_b = din("conv_b", (DEPTH, 256))
    conv_ln_g = din("conv_ln_g", (DEPTH, 256))
    conv_ln_b = din("conv_ln_b", (DEPTH, 256))
    q_norm = din("q_norm", (DEPTH, 64))
    k_norm = din("k_norm", (DEPTH, 64))
    sinks = din("sinks", (DEPTH, 8))
    w_out = din("w_out", (DEPTH, D, D))
    norm_mix = din("norm_mix", (DEPTH, D))
    norm_ffn = din("norm_ffn", (DEPTH, D))
    w_gate = din("w_gate", (DEPTH, D, DFF))
    w_up = din("w_up", (DEPTH, D, DFF))
    w_down = din("w_down", (DEPTH, DFF, D))
    ple_norm = din("ple_norm", (DEPTH, D))
    w_ple_gate = din("w_ple_gate", (DEPTH, D, D))
    w_ple_proj = din("w_ple_proj", (DEPTH, 256, D))
    c_ident = din("c_ident", (128, 128))
    c_U = din("c_U", (128, 128))
    c_W = din("c_W", (128, 128))
    c_mprev = din("c_mprev", (128, 128))
    c_mdiag = din("c_mdiag", (128, 128))
    c_bones = din("c_bones", (128, 128))
    c_cos = din("c_cos", (SEQ + DSEQ, 256))
    c_sin = din("c_sin", (SEQ + DSEQ, 256))

    y_prompt = dout("y_prompt", (SEQ, D))
    y_sample = dout("y_sample", (SPC * DSEQ, D))
    o_hp = dout("o_hp", (DEPTH, 4, 64, 64))
    o_cp = dout("o_cp", (DEPTH, 30, 256))
    o_kp = dout("o_kp", (DEPTH, 128, 128))
    o_vp = dout("o_vp", (DEPTH, 128, 128))
    o_hs = dout("o_hs", (DEPTH, SPC, 4, 64, 64))
    o_cs = dout("o_cs", (DEPTH, SPC, 30, 256))
    o_ks = dout("o_ks", (DEPTH, SPC, 128, 128))
    o_vs = dout("o_vs", (DEPTH, SPC, 128, 128))

    es = ExitStack()
    with es:
        k = KB(nc, es)
        sb, ps = k.sb, k.ps

        identf = sb("identf", [128, 128], F32)
        identb = sb("identb", [128, 128], BF16)
        Uf = sb("Uf", [128, 128], F32)
        mprev4 = sb("mprev4", [64, 4, 64], BF16)
        mdiag4 = sb("mdiag4", [64, 4, 64], BF16)
        onesb = sb("onesb", [128, 128], BF16)
        zerosb = sb("zerosb", [128, 128], BF16)
        eps_t = sb("eps_t", [128, 1], F32)
        gate_sb = sb("gate_sb", [128, 512], F32)
        for t, src in ((identf, c_ident), (Uf, c_U)):
            k.dma("sp", t[:], src, w=[t])
        for i_, src in enumerate((c_W, c_bones, c_mprev, c_mdiag)):
            k.dma("sp", gate_sb[:, i_ * 128:(i_ + 1) * 128], src, w=[gate_sb])
        k.cp(identb[:], identf[:], r=[identf], w=[identb])
        Ub = sb("Ub", [128, 128], BF16)
        Wb = sb("Wb", [128, 128], BF16)
        bonesb = sb("bonesb", [128, 128], BF16)
        k.cp(Ub[:], Uf[:], r=[Uf], w=[Ub])
        k.cp(Wb[:], gate_sb[:, 0:128], r=[gate_sb], w=[Wb])
        k.cp(bonesb[:], gate_sb[:, 128:256], r=[gate_sb], w=[bonesb])
        for g in range(4):
            k.cp(mprev4[:, g, :], gate_sb[0:64, 256:320], r=[gate_sb], w=[mprev4])
            k.cp(mdiag4[:, g, :], gate_sb[0:64, 384:448], r=[gate_sb], w=[mdiag4])
        k.memset(onesb[:], 1.0, w=[onesb])
        k.memset(zerosb[:], 0.0, w=[zerosb])
        k.memset(eps_t[:], EPS, w=[eps_t])
        one_t = sb("one_t", [128, 1], F32)
        k.memset(one_t[:], 1.0, w=[one_t])

        P = []
        dg = sb("dg", [128, 2, 31, 128], BF16)
        for l in range(DEPTH):
            L = {}
            for nm, src in (("g_mix", norm_mix), ("g_ffn", norm_ffn), ("g_ple", ple_norm)):
                t = sb("%s%d" % (nm, l), [128, 8], F32)
                k.dma("sp", t[:], src[l].rearrange("(c p) -> p c", p=128), w=[t], slow=True)
                L[nm] = t
            aon = sb("aon%d" % l, [64, 256], F32)
            for h in range(4):
                k.dma("sp", aon[:, h * 64:(h + 1) * 64], a_onorm[l].partition_broadcast(64), w=[aon])
            L["aon"] = aon
            qn = sb("qnr%d" % l, [64, 64], F32)
            kn = sb("knr%d" % l, [64, 64], F32)
            k.dma("sp", qn[:], q_norm[l].partition_broadcast(64), w=[qn])
            k.dma("sp", kn[:], k_norm[l].partition_broadcast(64), w=[kn])
            L["qn"], L["kn"] = qn, kn
            lbrow = sb("lbrow%d" % l, [64, 256], F32)
            omlrow = sb("omlrow%d" % l, [64, 256], F32)
            lbp = sb("lbp%d" % l, [64, 4], F32)
            omlp = sb("omlp%d" % l, [64, 4], F32)
            nomlp = sb("nomlp%d" % l, [64, 4], F32)
            if l == 0:
                k.memset(lbrow[:], 0.0, w=[lbrow])
                k.memset(lbp[:], 0.0, w=[lbp])
            else:
                a0 = sb("a0row", [64, 256], F32)
                a0p = sb("a0p", [64, 4], F32)
                k.dma("sp", lbrow[:], a_lower[1].partition_broadcast(64), w=[lbrow])
                k.dma("sp", a0[:], a_lower[0].partition_broadcast(64), w=[a0])
                k.dma("sp", lbp[:], a_lower[1].rearrange("(h p) -> p h", p=64), w=[lbp], slow=True)
                k.dma("sp", a0p[:], a_lower[0].rearrange("(h p) -> p h", p=64), w=[a0p], slow=True)
                k.tt(lbrow[:], lbrow[:], a0[:], ALU.subtract, r=[a0], w=[lbrow])
                k.tt(lbp[:], lbp[:], a0p[:], ALU.subtract, r=[a0p], w=[lbp])
                k.act(lbrow[:], lbrow[:], AF.Sigmoid, r=[], w=[lbrow])
                k.act(lbp[:], lbp[:], AF.Sigmoid, r=[], w=[lbp])
            k.ts(omlrow[:], lbrow[:], -1.0, 1.0, ALU.mult, ALU.add, r=[lbrow], w=[omlrow])
            k.ts(omlp[:], lbp[:], -1.0, 1.0, ALU.mult, ALU.add, r=[lbp], w=[omlp])
            k.ts(nomlp[:], omlp[:], -1.0, None, ALU.mult, None, r=[omlp], w=[nomlp])
            L.update(lbrow=lbrow, omlrow=omlrow, lbp=lbp, omlp=omlp, nomlp=nomlp)
            ptmp = sb("ptmp%d" % l, [128, 2, 31], F32)
            for j in range(2):
                k.dma("sp", ptmp[:, j, :], conv_w[l][:, j * 128:(j + 1) * 128].rearrange("i p -> p i"), w=[ptmp], slow=True)
            cb = sb("cb%d" % l, [128, 2], F32)
            cg = sb("cg%d" % l, [128, 2], F32)
            cbb = sb("cbb%d" % l, [128, 2], F32)
            k.dma("sp", cb[:], conv_b[l].rearrange("(j p) -> p j", p=128), w=[cb], slow=True)
            k.dma("sp", cg[:], conv_ln_g[l].rearrange("(j p) -> p j", p=128), w=[cg], slow=True)
            k.dma("sp", cbb[:], conv_ln_b[l].rearrange("(j p) -> p j", p=128), w=[cbb], slow=True)
            L.update(dg=dg, cb=cb, cg=cg, cbb=cbb, ptmp=ptmp)
            sk = sb("sk%d" % l, [64, 8], F32)
            esink = sb("esink%d" % l, [64, 8, 64], F32)
            k.dma("sp", sk[:], sinks[l].partition_broadcast(64), w=[sk])
            k.act(sk[:], sk[:], AF.Exp, r=[], w=[sk])
            for h in range(8):
                k.ts(esink[:, h, :], onesb[0:64, 0:64], sk[:, h:h + 1], None, ALU.mult, None, r=[sk, onesb], w=[esink])
            L["esink"] = esink
            P.append(L)

        NB = ST // 128
        H = [sb("H%d" % i, [128, D], F32) for i in range(NB)]
        xT = sb("xT", [128, 8, ST], BF16)
        mixT = sb("mixT", [128, 4, ST], BF16)
        ocT = sb("ocT", [64, 8, ST], BF16)
        aT = sb("aT", [128, 22, ST], BF16)
        pT = sb("pT", [128, 2, ST], BF16)
        WIN = sb("WIN", [128, 8, INC], BF16)
        WINr = [Tl(None) for _ in range(5)]
        NPAN = 3
        PAN = [sb("pan%d" % i, [128, 8, 512], BF16) for i in range(NPAN)]
        pan_i = [0]
        PSF = [ps("psf%d" % i, [128, 512], F32) for i in range(6)]
        PSB = [ps("psb%d" % i, [128, 1024], BF16) for i in range(2)]
        psf_i = [0]
        psb_i = [0]

        def psf():
            t = PSF[psf_i[0] % 6]
            psf_i[0] += 1
            return t

        def psb():
            t = PSB[psb_i[0] % 2]
            psb_i[0] += 1
            return t

        cA = [0]
        cC = [0]

        def psfA():
            cA[0] += 1
            return PSF[cA[0] % 2]

        def psfC():
            cC[0] += 1
            return PSF[3 + cC[0] % 3]

        def hiloc(src_ap, f, r):
            k.cp(hi_c[:, 0:f], src_ap, r=r, w=[hi_c])
            k.tt(lo_c[:, 0:f], src_ap, hi_c[:, 0:f], ALU.subtract, r=r + [hi_c], w=[lo_c])

        def next_pan():
            t = PAN[pan_i[0] % NPAN]
            pan_i[0] += 1
            return t

        xn = sb("xn", [128, D], BF16)
        ssq = sb("ssq", [128, 8], F32)
        rst = sb("rst", [128, 8], F32)
        MB = 64
        qT = sb("qT", [64, 4, MB], F32)
        kinT = sb("kinT", [64, 4, MB], F32)
        sgT = sb("sgT", [128, MB], F32)
        logf = sb("logf", [64, 256], F32)
        kin = sb("kin", [64, 256], F32)
        ftm = sb("ftm", [64, 256], F32)
        va = sb("va", [64, 256], BF16)
        sg = sb("sg", [64, 256], F32)
        GT = sb("GT", [64, 4, MB], F32)
        ER = sb("ER", [64, 256], F32)
        kd = sb("kd", [64, 256], BF16)
        gm = sb("gm", [64, 4], F32)
        ngm = sb("ngm", [64, 4], F32)
        E1 = sb("E1", [64, 4, MB], F32)
        E2 = sb("E2", [64, 4, MB], F32)
        E3 = sb("E3", [64, 4, MB], F32)
        qeT = sb("qeT", [64, 4, MB], BF16)
        keT = sb("keT", [64, 4, MB], BF16)
        qgT = sb("qgT", [64, 4, MB], BF16)
        attT = sb("attT", [64, 4, MB], BF16)
        oaf = sb("oaf", [64, 256], BF16)
        UF = sb("UF", [128, 2, 32 + MB], F32)
        UB = sb("UB", [128, 2, 32 + MB], BF16)
        UBo = sb("UBo", [128, 2, 32 + MB], BF16)
        yb = sb("yb", [128, MB], F32)
        ysq = sb("ysq", [128, MB], F32)
        cmean = sb("cmean", [128, MB], F32)
        cvar = sb("cvar", [128, MB], F32)
        zq = sb("zq", [64, 512], F32)
        zkv = sb("zkv", [64, 256], F32)
        qnf = sb("qnf", [64, 512], F32)
        qr = sb("qr", [64, 512], BF16)
        rt = [sb("rt%d" % i, [64, 8, 32], F32) for i in range(4)]
        QT = sb("QT", [64, 8, MB], BF16)
        knf = sb("knf", [64, 128], F32)
        krf = sb("krf", [64, 128], F32)
        krb = sb("krb", [64, 128], BF16)
        vf = sb("vf", [64, 128], F32)
        Pm8 = [sb("Pm%d" % i, [64, 512], BF16) for i in range(3)]
        Pm = [TlView(t8, t8.t[:, 0:256].rearrange("p (m t) -> p m t", m=4)) for t8 in Pm8]
        dent8 = sb("dent8", [64, 512], F32)
        dent = TlView(dent8, dent8.t[:, 0:256].rearrange("p (m t) -> p m t", m=4))
        cosT = sb("cosT", [64, 256], F32)
        sinT = sb("sinT", [64, 256], F32)
        pb = sb("pb", [128, 256], BF16)
        tok30 = sb("tok30", [32, 256], F32)
        halo_tok = tok30
        osq = sb("osq", [64, 256], F32)
        ssqA = sb("ssqA", [64, 4], F32)
        rstA = sb("rstA", [64, 4], F32)
        hi_c = sb("hi_c", [128, MB], BF16)
        lo_c = sb("lo_c", [128, MB], BF16)
        hi_t = sb("hi_t", [128, 256], BF16)
        lo_t = sb("lo_t", [128, 256], BF16)

        def new_state(tag):
            st = dict(S=sb("S" + tag, [64, 4, 64], F32), Sb=sb("Sb" + tag, [64, 4, 64], BF16),
                      UFh=sb("UFh" + tag, [128, 2, 30], F32))
            kts = [sb("KT%s_%d" % (tag, i), [64, 2, 64], BF16) for i in range(3)]
            vbs = [sb("VB%s_%d" % (tag, i), [64, 128], BF16) for i in range(3)]
            st.update(KT_old=kts[0], KT_mid=kts[1], KT_free=kts[2], VB_old=vbs[0], VB_mid=vbs[1], VB_free=vbs[2],
                      ones_old=zerosb, ones_mid=zerosb)
            return st

        def sigmoid_el(out, in_, p0, p1, r, w):
            k.act(out, in_, AF.Exp, r=r, w=w, scale=-1.0)
            k.act(out, out, AF.Ln, r=[one_t], w=w, bias=one_t[p0:p1, :], scale=1.0)
            k.act(out, out, AF.Exp, r=[], w=w, scale=-1.0)

        def hilo(src_ap, p, f, r):
            k.cp(hi_t[0:p, 0:f], src_ap, r=r, w=[hi_t])
            k.tt(lo_t[0:p, 0:f], src_ap, hi_t[0:p, 0:f], ALU.subtract, r=r + [hi_t], w=[lo_t])

        def tr32(out_ps_ap, src_ap, p, f, r, w):
            hilo(src_ap, p, f, r)
            k.mm(out_ps_ap, lhsT=hi_t[0:p, 0:f], rhs=identb[0:p, 0:p], start=True, stop=False, r=[hi_t, identb], w=w)
            k.mm(out_ps_ap, lhsT=lo_t[0:p, 0:f], rhs=identb[0:p, 0:p], start=False, stop=True, r=[lo_t, identb], w=w)

        def rmsnorm_T(h_t, n, grow, off):
            k.act(xn[0:n, :], h_t[0:n, :], AF.Square, r=[h_t], w=[xn, ssq], accum=ssq[0:n, 0:1])
            k.act(rst[0:n, 0:1], ssq[0:n, 0:1], AF.Ln, r=[ssq, eps_t], w=[rst], bias=eps_t[0:n, :], scale=1.0 / D)
            k.act(rst[0:n, 0:1], rst[0:n, 0:1], AF.Exp, r=[], w=[rst], scale=-0.5)
            k.ts(xn[0:n, :], h_t[0:n, :], rst[0:n, 0:1], None, ALU.mult, None, r=[h_t, rst], w=[xn])
            pt = psb()
            for c in range(8):
                k.tr(pt[:, c * 128:c * 128 + n], xn[0:n, c * 128:(c + 1) * 128], identb[0:n, 0:n], r=[xn, identb], w=[pt])
            for c in range(8):
                k.ts(xT[:, c, off:off + n], pt[:, c * 128:c * 128 + n], grow[:, c:c + 1], None, ALU.mult, None, r=[pt, grow], w=[xT])

        def lin_tok(out_ps, n, off, c0, c1, wt, wr, act_t, act_r, nk=8, kp=128, first=True, last=True):
            for kc in range(nk):
                k.mm(out_ps[0:n, 0:c1 - c0], lhsT=act_t[0:kp, kc, off:off + n], rhs=wt[0:kp, kc, c0:c1],
                     start=(first and kc == 0), stop=(last and kc == nk - 1), r=[act_r] + wr, w=[out_ps],
                     inc=(kc == nk - 1))

        def lin_feat(out_ps, n, off, c0, wt, wr, act_t, act_r, width=128):
            for kc in range(8):
                k.mm(out_ps[0:width, 0:n], lhsT=wt[:, kc, c0:c0 + width], rhs=act_t[:, kc, off:off + n],
                     start=(kc == 0), stop=(kc == 7), r=[act_r] + wr, w=[out_ps])

        def mixer(l, n, off, st, pos0):
            L = P[l]
            S, Sb = st["S"], st["Sb"]
            for h in range(4):
                p_ = psf()
                lin_feat(p_, n, off, h * 64, WIN, [WINr[0]], xT, xT, width=64)
                k.cp(qT[:, h, 0:n], p_[0:64, 0:n], r=[p_], w=[qT], eng="act")
            for h in range(4):
                p_ = psf()
                lin_feat(p_, n, off, 256 + h * 64, WIN, [WINr[0]], xT, xT, width=64)
                k.act(sgT[0:64, 0:n], p_[0:64, 0:n], AF.Sigmoid, r=[p_], w=[sgT])
                k.ts(kinT[:, h, 0:n], sgT[0:64, 0:n], L["nomlp"][:, h:h + 1], L["omlp"][:, h:h + 1], ALU.mult, ALU.add,
                     r=[sgT, L["nomlp"], L["omlp"]], w=[kinT])
            p_ = psf()
            lin_tok(p_, n, off, 256, 512, WIN, [WINr[0]], xT, xT)
            k.act(ftm[0:n, :], p_[0:n, 0:256], AF.Sigmoid, r=[p_], w=[ftm])
            k.tt(ftm[0:n, :], ftm[0:n, :], L["omlrow"][0:n, :], ALU.mult, r=[L["omlrow"]], w=[ftm])
            k.tt(ftm[0:n, :], ftm[0:n, :], L["lbrow"][0:n, :], ALU.add, r=[L["lbrow"]], w=[ftm])
            k.act(logf[0:n, :], ftm[0:n, :], AF.Ln, r=[ftm], w=[logf])
            k.ts(kin[0:n, :], ftm[0:n, :], -1.0, 1.0, ALU.mult, ALU.add, r=[ftm], w=[kin])
            p_ = psf()
            lin_tok(p_, n, off, 512, 1024, WIN, [WINr[1]], xT, xT)
            k.cp(va[0:n, :], p_[0:n, 0:256], r=[p_], w=[va])
            k.act(sg[0:n, :], p_[0:n, 256:512], AF.Silu, r=[p_], w=[sg])
            k.tt(sg[0:n, :], sg[0:n, :], L["aon"][0:n, :], ALU.mult, r=[L["aon"]], w=[sg])
            for j in range(2):
                pu = psf()
                lin_feat(pu, n, off, 1024 + j * 128, WIN, [WINr[2]], xT, xT)
                pg = psf()
                lin_feat(pg, n, off, 1280 + j * 128, WIN, [WINr[2]], xT, xT)
                k.act(sgT[:, 0:n], pg[:, 0:n], AF.Sigmoid, r=[pg], w=[sgT])
                k.tt(UF[:, j, 30:30 + n], pu[:, 0:n], sgT[:, 0:n], ALU.mult, r=[pu, sgT], w=[UF])
            k.cp(UF[:, :, 0:30], st["UFh"][:], r=[st["UFh"]], w=[UF])
            k.cp(UB[:, :, 0:30 + n], UF[:, :, 0:30 + n], r=[UF], w=[UB])
            k.cp(UBo[:, :, 0:29 + n], UF[:, :, 1:30 + n], r=[UF], w=[UBo])
            p_ = psf()
            lin_tok(p_, n, off, 1536, 2048, WIN, [WINr[3]], xT, xT)
            k.cp(zq[0:n, :], p_[0:n, :], r=[p_], w=[zq], eng="act")
            p_ = psf()
            lin_tok(p_, n, off, 2048, 2304, WIN, [WINr[4]], xT, xT)
            k.cp(zkv[0:n, :], p_[0:n, 0:256], r=[p_], w=[zkv], eng="act")

            KTc, VBc = st["KT_free"], st["VB_free"]

            def chainA():
                yield
                hilo(logf[0:n, :], n, 256, [logf])
                for h in range(4):
                    p_ = psfA()
                    yield
                    k.mm(p_[0:64, 0:n], lhsT=hi_t[0:n, h * 64:(h + 1) * 64], rhs=Ub[0:n, 0:n], start=True, stop=False,
                         r=[hi_t, Ub], w=[p_])
                    yield
                    k.mm(p_[0:64, 0:n], lhsT=lo_t[0:n, h * 64:(h + 1) * 64], rhs=Ub[0:n, 0:n], start=False, stop=True,
                         r=[lo_t, Ub], w=[p_])
                    yield
                    k.cp(GT[:, h, 0:n], p_[0:64, 0:n], r=[p_], w=[GT], eng="act")
                p_ = psfA()
                yield
                k.mm(p_[0:n, 0:256], lhsT=Wb[0:n, 0:n], rhs=hi_t[0:n, :], start=True, stop=False, r=[hi_t, Wb], w=[p_])
                yield
                k.mm(p_[0:n, 0:256], lhsT=Wb[0:n, 0:n], rhs=lo_t[0:n, :], start=False, stop=True, r=[lo_t, Wb], w=[p_])
                yield
                k.act(ER[0:n, :], p_[0:n, 0:256], AF.Exp, r=[p_], w=[ER])
                yield
                k.tt(kd[0:n, :], kin[0:n, :], ER[0:n, :], ALU.mult, r=[kin, ER], w=[kd])
                rc = max(n // 2 - 1, 0)
                yield
                k.tt(E1[:, :, 0:n], GT[:, :, 0:n], GT[:, :, rc:rc + 1].to_broadcast([64, 4, n]), ALU.subtract, r=[GT], w=[E1])
                yield
                k.act(E2[:, :, 0:n], E1[:, :, 0:n], AF.Exp, r=[E1], w=[E2], scale=-1.0)
                yield
                k.act(E1[:, :, 0:n], E1[:, :, 0:n], AF.Exp, r=[], w=[E1])
                yield
                k.act(E3[:, :, 0:n], GT[:, :, 0:n], AF.Exp, r=[GT], w=[E3])
                yield
                k.tt(qeT[:, :, 0:n], qT[:, :, 0:n], E1[:, :, 0:n], ALU.mult, r=[qT, E1], w=[qeT])
                yield
                k.tt(keT[:, :, 0:n], kinT[:, :, 0:n], E2[:, :, 0:n], ALU.mult, r=[kinT, E2], w=[keT])
                yield
                k.tt(qgT[:, :, 0:n], qT[:, :, 0:n], E3[:, :, 0:n], ALU.mult, r=[qT, E3], w=[qgT])
                pA = psfA()
                for h in range(4):
                    yield
                    k.mm(pA[0:n, h * 64:h * 64 + n], lhsT=keT[:, h, 0:n], rhs=qeT[:, h, 0:n], start=True, stop=True,
                         r=[keT, qeT], w=[pA])
                yield
                k.tt(attT[0:n, :, 0:n], pA[0:n, 0:256].rearrange("p (h t) -> p h t", h=4)[:, :, 0:n],
                     Uf[0:n, 0:n].unsqueeze(1).to_broadcast([n, 4, n]), ALU.mult, r=[pA, Uf], w=[attT])
                pO = psfA()
                for h in range(4):
                    yield
                    k.mm(pO[0:n, h * 64:(h + 1) * 64], lhsT=attT[0:n, h, 0:n], rhs=va[0:n, h * 64:(h + 1) * 64],
                         start=True, stop=False, r=[attT, va], w=[pO])
                    yield
                    k.mm(pO[0:n, h * 64:(h + 1) * 64], lhsT=qgT[:, h, 0:n], rhs=Sb[:, h, :],
                         start=False, stop=True, r=[qgT, Sb], w=[pO])
                pU = psfA()
                for h in range(4):
                    yield
                    k.mm(pU[0:64, h * 64:(h + 1) * 64], lhsT=kd[0:n, h * 64:(h + 1) * 64], rhs=va[0:n, h * 64:(h + 1) * 64],
                         start=True, stop=True, r=[kd, va], w=[pU])
                yield
                k.tt(S[:], S[:], E3[:, :, n - 1:n].to_broadcast([64, 4, 64]), ALU.mult, r=[E3], w=[S])
                yield
                k.tt(S[:], S[:], pU[0:64, 0:256].rearrange("p (h v) -> p h v", h=4), ALU.add, r=[pU], w=[S])
                yield
                k.cp(Sb[:], S[:], r=[S], w=[Sb])
                yield
                k.act(osq[0:n, :], pO[0:n, 0:256], AF.Square, r=[pO], w=[osq])
                yield
                k.red(ssqA[0:n, 0:4], osq[0:n, :].rearrange("p (h d) -> p h d", h=4), r=[osq], w=[ssqA])
                yield
                k.act(rstA[0:n, 0:4], ssqA[0:n, 0:4], AF.Ln, r=[ssqA, eps_t], w=[rstA], bias=eps_t[0:n, :], scale=1.0 / 64)
                yield
                k.act(rstA[0:n, 0:4], rstA[0:n, 0:4], AF.Exp, r=[], w=[rstA], scale=-0.5)
                yield
                k.tt(osq[0:n, :].rearrange("p (h d) -> p h d", h=4), pO[0:n, 0:256].rearrange("p (h d) -> p h d", h=4),
                     rstA[0:n, 0:4].unsqueeze(2).to_broadcast([n, 4, 64]), ALU.mult, r=[pO, rstA], w=[osq])
                yield
                k.tt(oaf[0:n, :], osq[0:n, :], sg[0:n, :], ALU.mult, r=[osq, sg], w=[oaf])
                pt = PSB[0]
                for m in range(2):
                    yield
                    k.tr(pt[:, m * 128:m * 128 + n], oaf[0:n, m * 128:(m + 1) * 128], identb[0:n, 0:n], r=[oaf, identb], w=[pt])
                for m in range(2):
                    yield
                    k.cp(mixT[:, m, off:off + n], pt[:, m * 128:m * 128 + n], r=[pt], w=[mixT])


                yield
            def chainB():
                for j in range(2):
                    pY = PSF[2]
                    for i in range(31):
                        src_ = UB[:, j, i:i + n] if i % 2 == 0 else UBo[:, j, i - 1:i - 1 + n]
                        yield
                        k.mm(pY[:, 0:n], lhsT=L["dg"][:, j, i, :], rhs=src_, start=(i == 0), stop=(i == 30),
                             r=[L["dg"], UB, UBo], w=[pY])
                    yield
                    k.ts(yb[:, 0:n], pY[:, 0:n], L["cb"][:, j:j + 1], None, ALU.add, None, r=[pY, L["cb"]], w=[yb])
                    yield
                    k.tt(ysq[:, 0:n], yb[:, 0:n], yb[:, 0:n], ALU.mult, r=[yb], w=[ysq])
                    pM = PSF[2]
                    yield
                    hiloc(yb[:, 0:n], n, [yb])
                    yield
                    k.mm(pM[:, 0:n], lhsT=bonesb[:], rhs=hi_c[:, 0:n], start=True, stop=False, r=[bonesb, hi_c], w=[pM])
                    yield
                    k.mm(pM[:, 0:n], lhsT=bonesb[:], rhs=lo_c[:, 0:n], start=False, stop=True, r=[bonesb, lo_c], w=[pM])
                    pQ = PSF[2]
                    yield
                    hiloc(ysq[:, 0:n], n, [ysq])
                    yield
                    k.mm(pQ[:, 256:256 + n], lhsT=bonesb[:], rhs=hi_c[:, 0:n], start=True, stop=False, r=[bonesb, hi_c], w=[pQ])
                    yield
                    k.mm(pQ[:, 256:256 + n], lhsT=bonesb[:], rhs=lo_c[:, 0:n], start=False, stop=True, r=[bonesb, lo_c], w=[pQ])
                    yield
                    k.cp(cmean[:, 0:n], pM[:, 0:n], r=[pM], w=[cmean], eng="act")
                    yield
                    k.tt(ysq[:, 0:n], cmean[:, 0:n], cmean[:, 0:n], ALU.mult, r=[cmean], w=[ysq])
                    yield
                    k.tt(cvar[:, 0:n], pQ[:, 256:256 + n], ysq[:, 0:n], ALU.subtract, r=[pQ, ysq], w=[cvar])
                    yield
                    k.act(cvar[:, 0:n], cvar[:, 0:n], AF.Ln, r=[eps_t], w=[cvar], bias=eps_t[:], scale=1.0)
                    yield
                    k.act(cvar[:, 0:n], cvar[:, 0:n], AF.Exp, r=[], w=[cvar], scale=-0.5)
                    yield
                    k.tt(yb[:, 0:n], yb[:, 0:n], cmean[:, 0:n], ALU.subtract, r=[yb, cmean], w=[yb])
                    yield
                    k.tt(yb[:, 0:n], yb[:, 0:n], cvar[:, 0:n], ALU.mult, r=[cvar], w=[yb])
                    yield
                    k.ts(yb[:, 0:n], yb[:, 0:n], L["cg"][:, j:j + 1], L["cbb"][:, j:j + 1], ALU.mult, ALU.add, r=[L["cg"], L["cbb"]], w=[yb])
                    yield
                    k.act(cvar[:, 0:n], yb[:, 0:n], AF.Exp, r=[yb], w=[cvar], scale=-1.0)
                    yield
                    k.act(cvar[:, 0:n], cvar[:, 0:n], AF.Ln, r=[one_t], w=[cvar], bias=one_t[:], scale=1.0)
                    yield
                    k.act(cvar[:, 0:n], cvar[:, 0:n], AF.Exp, r=[], w=[cvar], scale=-1.0)
                    yield
                    k.tt(mixT[:, 2 + j, off:off + n], yb[:, 0:n], cvar[:, 0:n], ALU.mult, r=[yb, cvar], w=[mixT])
                yield
                k.cp(st["UFh"][:], UF[:, :, n:n + 30], r=[UF], w=[st["UFh"]])


                yield
            def chainC():
                yield
                k.dma("sp", cosT[0:n, :], c_cos[pos0:pos0 + n, :], w=[cosT])
                yield
                k.dma("sp", sinT[0:n, :], c_sin[pos0:pos0 + n, :], w=[sinT])
                yield
                k.act(qnf[0:n, :], zq[0:n, :], AF.Square, r=[zq], w=[qnf])
                yield
                k.red(ssq[0:n, 0:8], qnf[0:n, :].rearrange("p (h d) -> p h d", h=8), r=[qnf], w=[ssq])
                yield
                k.act(rst[0:n, 0:8], ssq[0:n, 0:8], AF.Ln, r=[ssq, eps_t], w=[rst], bias=eps_t[0:n, :], scale=1.0 / 64)
                yield
                k.act(rst[0:n, 0:8], rst[0:n, 0:8], AF.Exp, r=[], w=[rst], scale=-0.5)
                yield
                k.tt(qnf[0:n, :].rearrange("p (h d) -> p h d", h=8), zq[0:n, :].rearrange("p (h d) -> p h d", h=8),
                     rst[0:n, 0:8].unsqueeze(2).to_broadcast([n, 8, 64]), ALU.mult, r=[zq, rst], w=[qnf])
                yield
                k.tt(qnf[0:n, :].rearrange("p (h d) -> p h d", h=8), qnf[0:n, :].rearrange("p (h d) -> p h d", h=8),
                     L["qn"][0:n, :].unsqueeze(1).to_broadcast([n, 8, 64]), ALU.mult, r=[L["qn"]], w=[qnf])
                cosv = cosT[0:n, :].rearrange("p (m d) -> p m d", m=8)
                sinv = sinT[0:n, :].rearrange("p (m d) -> p m d", m=8)
                src = qnf[0:n, :].rearrange("p (m d) -> p m d", m=8)
                dst = qr[0:n, :].rearrange("p (m d) -> p m d", m=8)
                x1, x2 = src[:, :, 0:32], src[:, :, 32:64]
                a, b, c, d_ = [t[0:n] for t in rt]
                yield
                k.tt(a, x1, cosv, ALU.mult, r=[qnf, cosT], w=[rt[0]])
                yield
                k.tt(b, x2, sinv, ALU.mult, r=[qnf, sinT], w=[rt[1]])
                yield
                k.tt(dst[:, :, 0:32], a, b, ALU.subtract, r=[rt[0], rt[1]], w=[qr])
                yield
                k.tt(c, x2, cosv, ALU.mult, r=[qnf, cosT], w=[rt[2]])
                yield
                k.tt(d_, x1, sinv, ALU.mult, r=[qnf, sinT], w=[rt[3]])
                yield
                k.tt(dst[:, :, 32:64], c, d_, ALU.add, r=[rt[2], rt[3]], w=[qr])
                pt = PSB[1]
                for h in range(8):
                    yield
                    k.tr(pt[0:64, h * 64:h * 64 + n], qr[0:n, h * 64:(h + 1) * 64], identb[0:n, 0:n], r=[qr, identb], w=[pt])
                for h in range(8):
                    yield
                    k.cp(QT[:, h, 0:n], pt[0:64, h * 64:h * 64 + n], r=[pt], w=[QT])
                yield
                k.act(knf[0:n, :], zkv[0:n, 0:128], AF.Square, r=[zkv], w=[knf])
                yield
                k.red(ssq[0:n, 0:2], knf[0:n, :].rearrange("p (h d) -> p h d", h=2), r=[knf], w=[ssq])
                yield
                k.act(rst[0:n, 0:2], ssq[0:n, 0:2], AF.Ln, r=[ssq, eps_t], w=[rst], bias=eps_t[0:n, :], scale=1.0 / 64)
                yield
                k.act(rst[0:n, 0:2], rst[0:n, 0:2], AF.Exp, r=[], w=[rst], scale=-0.5)
                for h in range(2):
                    cs = slice(h * 64, (h + 1) * 64)
                    yield
                    k.stt(knf[0:n, cs], zkv[0:n, cs], rst[0:n, h:h + 1], L["kn"][0:n, :], ALU.mult, ALU.mult, r=[zkv, rst, L["kn"]], w=[knf])
                src = knf[0:n, :].rearrange("p (m d) -> p m d", m=2)
                x1, x2 = src[:, :, 0:32], src[:, :, 32:64]
                dst = krf[0:n, :].rearrange("p (m d) -> p m d", m=2)
                cos2 = cosT[0:n, 0:64].rearrange("p (m d) -> p m d", m=2)
                sin2 = sinT[0:n, 0:64].rearrange("p (m d) -> p m d", m=2)
                a, b, c, d_ = [t[0:n, 0:2, :] for t in rt]
                yield
                k.tt(a, x1, cos2, ALU.mult, r=[knf, cosT], w=[rt[0]])
                yield
                k.tt(b, x2, sin2, ALU.mult, r=[knf, sinT], w=[rt[1]])
                yield
                k.tt(dst[:, :, 0:32], a, b, ALU.subtract, r=[rt[0], rt[1]], w=[krf])
                yield
                k.tt(c, x2, cos2, ALU.mult, r=[knf, cosT], w=[rt[2]])
                yield
                k.tt(d_, x1, sin2, ALU.mult, r=[knf, sinT], w=[rt[3]])
                yield
                k.tt(dst[:, :, 32:64], c, d_, ALU.add, r=[rt[2], rt[3]], w=[krf])
                yield
                k.cp(krb[0:n, :], krf[0:n, :], r=[krf], w=[krb])
                KTc, VBc = st["KT_free"], st["VB_free"]
                pt = PSB[1]
                for kv in range(2):
                    yield
                    k.tr(pt[0:64, kv * 64:kv * 64 + n], krb[0:n, kv * 64:(kv + 1) * 64], identb[0:n, 0:n], r=[krb, identb], w=[pt])
                for kv in range(2):
                    yield
                    k.cp(KTc[:, kv, 0:n], pt[0:64, kv * 64:kv * 64 + n], r=[pt], w=[KTc])
                yield
                k.cp(vf[0:n, :], zkv[0:n, 128:256], r=[zkv], w=[vf], eng="act")
                yield
                k.cp(VBc[0:n, :], zkv[0:n, 128:256], r=[zkv], w=[VBc])
                kblocks = ((st["KT_old"], st["VB_old"], st["ones_old"], 64, mprev4),
                           (st["KT_mid"], st["VB_mid"], st["ones_mid"], 64, None),
                           (KTc, VBc, onesb, n, mdiag4))
                if n == 64:
                    pNs = (PSF[3], PSF[4])
                    pS = PSF[5]
                    valid = []
                    for bi, (KTk, VBk, ones_k, nk, mask) in enumerate(kblocks):
                        if ones_k is zerosb:
                            continue
                        valid.append(bi)
                    for vi, bi in enumerate(valid):
                        KTk, VBk, ones_k, nk, mask = kblocks[bi]
                        for kv in range(2):
                            yield
                            k.mm(pS[0:nk, kv * 256:kv * 256 + 256], lhsT=KTk[:, kv, 0:nk], rhs=QT[:, kv * 4:(kv + 1) * 4, 0:n],
                                 start=True, stop=True, r=[KTk, QT], w=[pS])
                        Pt = Pm8[bi]
                        yield
                        k.act(Pt[0:nk, :], pS[0:nk, :], AF.Exp, r=[pS], w=[Pt], scale=0.125)
                        if mask is not None:
                            yield
                            k.tt(Pt[0:nk, :].rearrange("p (g x) -> p g x", g=2), Pt[0:nk, :].rearrange("p (g x) -> p g x", g=2),
                                 mask[0:nk, :, :].rearrange("p m t -> p (m t)").unsqueeze(1).to_broadcast([nk, 2, 256]),
                                 ALU.mult, r=[mask], w=[Pt])
                        for kv in range(2):
                            yield
                            k.mm(pNs[kv][0:64, 0:256], lhsT=VBk[0:nk, kv * 64:(kv + 1) * 64], rhs=Pt[0:nk, kv * 256:(kv + 1) * 256],
                                 start=(vi == 0), stop=(vi == len(valid) - 1), r=[VBk, Pt], w=[pNs[kv]], inc=True)
                    Pl = Pm8[valid[-1]]
                    for bi in valid[:-1]:
                        yield
                        k.tt(Pl[:, :], Pl[:, :], Pm8[bi][:, :], ALU.add, r=[Pm8[bi]], w=[Pl])
                    yield
                    k.mm(pS[0:64, :], lhsT=onesb[0:64, 0:64], rhs=Pl[:, :], start=True, stop=True, r=[onesb, Pl], w=[pS])
                    yield
                    k.tt(dent8[:, :], pS[0:64, :], L["esink"][:, :, :].rearrange("p h t -> p (h t)"), ALU.add, r=[pS, L["esink"]], w=[dent8])
                    yield
                    k.recip(dent8[:, :], dent8[:, :], r=[], w=[dent8])
                    for kv in range(2):
                        yield
                        k.tt(ocT[:, kv * 4:(kv + 1) * 4, off:off + n], pNs[kv][0:64, 0:256].rearrange("p (m t) -> p m t", m=4),
                             dent8[:, kv * 256:(kv + 1) * 256].rearrange("p (m t) -> p m t", m=4), ALU.mult, r=[pNs[kv], dent8], w=[ocT])
                else:
                    for kv in range(2):
                        pN = PSF[3]
                        pD = PSF[4]
                        for bi, (KTk, VBk, ones_k, nk, mask) in enumerate(kblocks):
                            pS = PSF[5]
                            yield
                            k.mm(pS[0:nk, 0:4 * n], lhsT=KTk[:, kv, 0:nk], rhs=QT[:, kv * 4:(kv + 1) * 4, 0:n], start=True, stop=True,
                                 r=[KTk, QT], w=[pS])
                            Pt = Pm[bi]
                            yield
                            k.act(Pt[0:nk, :, 0:n], pS[0:nk, 0:4 * n].rearrange("p (m t) -> p m t", m=4), AF.Exp, r=[pS], w=[Pt], scale=0.125)
                            if mask is not None:
                                yield
                                k.tt(Pt[0:nk, :, 0:n], Pt[0:nk, :, 0:n], mask[0:nk, :, 0:n], ALU.mult, r=[mask], w=[Pt])
                            yield
                            k.mm(pN[0:64, 0:4 * n], lhsT=VBk[0:nk, kv * 64:(kv + 1) * 64], rhs=Pt[0:nk, :, 0:n], start=(bi == 0), stop=(bi == 2),
                                 r=[VBk, Pt], w=[pN], inc=True)
                            yield
                            k.mm(pD[0:64, 0:4 * n], lhsT=ones_k[0:nk, 0:64], rhs=Pt[0:nk, :, 0:n], start=(bi == 0), stop=(bi == 2),
                                 r=[ones_k, Pt], w=[pD], inc=True)
                        yield
                        k.tt(dent[:, :, 0:n], pD[0:64, 0:4 * n].rearrange("p (m t) -> p m t", m=4), L["esink"][:, kv * 4:(kv + 1) * 4, 0:n],
                             ALU.add, r=[pD, L["esink"]], w=[dent])
                        yield
                        k.recip(dent[:, :, 0:n], dent[:, :, 0:n], r=[], w=[dent])
                        yield
                        k.tt(ocT[:, kv * 4:(kv + 1) * 4, off:off + n], pN[0:64, 0:4 * n].rearrange("p (m t) -> p m t", m=4), dent[:, :, 0:n],
                             ALU.mult, r=[pN, dent], w=[ocT])

                    yield
            chains = [chainA(), chainB(), chainC()]
            while chains:
                for g_ in list(chains):
                    try:
                        next(g_)
                    except StopIteration:
                        chains.remove(g_)
            if n == 64:
                st["KT_free"], st["KT_old"], st["KT_mid"] = st["KT_old"], st["KT_mid"], KTc
                st["VB_free"], st["VB_old"], st["VB_mid"] = st["VB_old"], st["VB_mid"], VBc
                st["ones_old"], st["ones_mid"] = st["ones_mid"], onesb

        def load_win(l):
            wv = w_in[l].rearrange("(c p) n -> p c n", p=128)
            for pi in range(5):
                c0, c1 = pi * 512, min((pi + 1) * 512, INC)
                k.dma("pool", WIN[:, :, c0:c1], wv[:, :, c0:c1], w=[WINr[pi]])

        def load_pan(src_ap, nk, ncol, p0=0, rows=slice(0, 128), t=None):
            if t is None:
                t = next_pan()
            k.dma("pool", t[rows, p0:p0 + nk, 0:ncol], src_ap, w=[t])
            return t

        def layer(l, blocks, mix_blocks, p_src):
            L = P[l]
            load_win(l)
            for j in range(2):
                for i in range(31):
                    k.ts(dg[:, j, i, :], identf[:], L["ptmp"][:, j, i:i + 1], None, ALU.mult, None, r=[L["ptmp"], identf], w=[dg])
            for off, n, Ht in blocks:
                rmsnorm_T(Ht, n, L["g_mix"], off)
            for mb in mix_blocks:
                mb(l)
            wo = w_out[l]
            for ch in range(2):
                cs = slice(ch * 512, (ch + 1) * 512)
                t1 = load_pan(wo[0:512, cs].rearrange("(c p) n -> p c n", p=128), 4, 512)
                t2 = load_pan(wo[512:1024, cs].rearrange("(h d) n -> d h n", d=64), 8, 512, rows=slice(0, 64))
                for off, n, Ht in blocks:
                    p_ = psf()
                    lin_tok(p_, n, off, 0, 512, t1, [t1], mixT, mixT, nk=4, last=False)
                    lin_tok(p_, n, off, 0, 512, t2, [t2], ocT, ocT, nk=8, kp=64, first=False)
                    k.tt(Ht[0:n, cs], Ht[0:n, cs], p_[0:n, :], ALU.add, r=[p_], w=[Ht])
            for off, n, Ht in blocks:
                rmsnorm_T(Ht, n, L["g_ffn"], off)
            wg = w_gate[l].rearrange("(c p) n -> p c n", p=128)
            wu = w_up[l].rearrange("(c p) n -> p c n", p=128)
            for pi in range(6):
                c0, c1 = pi * 512, min((pi + 1) * 512, DFF)
                tg = load_pan(wg[:, :, c0:c1], 8, c1 - c0)
                tu = load_pan(wu[:, :, c0:c1], 8, c1 - c0)
                ntok = blocks[-1][0] + blocks[-1][1]
                for j in range((c1 - c0) // 128):
                    fc = (c0 // 128) + j
                    pg_ = psf()
                    lin_feat(pg_, ntok, 0, j * 128, tg, [tg], xT, xT)
                    pu_ = psf()
                    lin_feat(pu_, ntok, 0, j * 128, tu, [tu], xT, xT)
                    k.act(aT[:, fc, 0:ntok], pg_[:, 0:ntok], AF.Silu, r=[pg_], w=[aT])
                    k.tt(aT[:, fc, 0:ntok], pu_[:, 0:ntok], aT[:, fc, 0:ntok], ALU.mult, r=[pu_], w=[aT])
            wd = w_down[l].rearrange("(c p) n -> p c n", p=128)
            for ch in range(2):
                cs = slice(ch * 512, (ch + 1) * 512)
                accs = [psf() for _ in blocks]
                for rg, (k0, nk) in enumerate(((0, 8), (8, 8), (16, 6))):
                    t = load_pan(wd[:, k0:k0 + nk, cs], nk, 512)
                    for bi, (off, n, Ht) in enumerate(blocks):
                        for kc in range(nk):
                            k.mm(accs[bi][0:n, :], lhsT=aT[:, k0 + kc, off:off + n], rhs=t[:, kc, :],
                                 start=(k0 + kc == 0), stop=(k0 + kc == 21), r=[aT, t], w=[accs[bi]],
                                 inc=(kc == nk - 1))
                for bi, (off, n, Ht) in enumerate(blocks):
                    k.tt(Ht[0:n, cs], Ht[0:n, cs], accs[bi][0:n, :], ALU.add, r=[accs[bi]], w=[Ht])
            for off, n, Ht in blocks:
                rmsnorm_T(Ht, n, L["g_ple"], off)
                k.dma("pool", pb[0:n, :], p_src(l, off, n), w=[pb])
                pt = psb()
                for c in range(2):
                    k.tr(pt[:, c * 128:c * 128 + n], pb[0:n, c * 128:(c + 1) * 128], identb[0:n, 0:n], r=[pb, identb], w=[pt])
                for c in range(2):
                    k.cp(pT[:, c, off:off + n], pt[:, c * 128:c * 128 + n], r=[pt], w=[pT])
            wpg = w_ple_gate[l].rearrange("(c p) n -> p c n", p=128)
            wpp = w_ple_proj[l].rearrange("(c p) n -> p c n", p=128)
            for ch in range(2):
                cs = slice(ch * 512, (ch + 1) * 512)
                tg = load_pan(wpg[:, :, cs], 8, 512)
                tp = load_pan(wpp[:, :, cs], 2, 512)
                for off, n, Ht in blocks:
                    p1 = psf()
                    lin_tok(p1, n, off, 0, 512, tg, [tg], xT, xT)
                    p2 = psf()
                    lin_tok(p2, n, off, 0, 512, tp, [tp], pT, pT, nk=2)
                    k.act(gate_sb[0:n, :], p1[0:n, :], AF.Sigmoid, r=[p1], w=[gate_sb])
                    k.tt(gate_sb[0:n, :], p2[0:n, :], gate_sb[0:n, :], ALU.mult, r=[p2], w=[gate_sb])
                    k.tt(Ht[0:n, cs], Ht[0:n, cs], gate_sb[0:n, :], ALU.add, r=[gate_sb], w=[Ht])

        sst = new_state("s")
        kc_old = sb("kc_old", [64, 128], BF16)
        kc_mid = sb("kc_mid", [64, 128], BF16)

        def sample_mix_block(b):
            def run(l):
                st = sst
                st["ones_old"], st["ones_mid"] = onesb, onesb
                S = st["S"]
                k.dma("sp", S[:], state_hgrn[l, b].rearrange("h k v -> k h v"), w=[S])
                k.cp(st["Sb"][:], S[:], r=[S], w=[st["Sb"]])
                k.dma("sp", halo_tok[0:30, :], state_conv[l, b], w=[halo_tok])
                for j in range(2):
                    p_ = psf()
                    tr32(p_[:, 0:30], halo_tok[0:30, j * 128:(j + 1) * 128], 30, 128, [halo_tok], [p_])
                    k.cp(st["UFh"][:, j, :], p_[:, 0:30], r=[p_], w=[st["UFh"]], eng="act")
                k.dma("pool", kc_old[:], cache_k[l, b, 0:64, :], w=[kc_old])
                k.dma("pool", kc_mid[:], cache_k[l, b, 64:128, :], w=[kc_mid])
                k.dma("pool", st["VB_old"][:], cache_v[l, b, 0:64, :], w=[st["VB_old"]])
                k.dma("pool", st["VB_mid"][:], cache_v[l, b, 64:128, :], w=[st["VB_mid"]])
                for src_t, dst_t in ((kc_old, st["KT_old"]), (kc_mid, st["KT_mid"])):
                    pt = psb()
                    for kv in range(2):
                        k.tr(pt[0:64, kv * 64:(kv + 1) * 64], src_t[:, kv * 64:(kv + 1) * 64], identb[0:64, 0:64], r=[src_t, identb], w=[pt])
                    for kv in range(2):
                        k.cp(dst_t[:, kv, :], pt[0:64, kv * 64:(kv + 1) * 64], r=[pt], w=[dst_t])
                mixer(l, DSEQ, b * DSEQ, st, SEQ)
                k.dma("sp", o_hs[l, b].rearrange("h k v -> k h v"), S[:], r=[S])
                k.dma("sp", o_cs[l, b, 0:26, :], state_conv[l, b, 4:30, :])
                for j in range(2):
                    p_ = psf()
                    tr32(p_[0:DSEQ, 0:128], UF[:, j, 30:30 + DSEQ], 128, DSEQ, [UF], [p_])
                    k.cp(tok30[0:DSEQ, j * 128:(j + 1) * 128], p_[0:DSEQ, 0:128], r=[p_], w=[tok30], eng="act")
                k.dma("sp", o_cs[l, b, 26:30, :], tok30[0:DSEQ, :], r=[tok30])
                k.dma("sp", o_ks[l, b, 0:124, :], cache_k[l, b, 4:128, :])
                k.dma("sp", o_vs[l, b, 0:124, :], cache_v[l, b, 4:128, :])
                k.dma("sp", o_ks[l, b, 124:128, :], krf[0:DSEQ, :], r=[krf])
                k.dma("sp", o_vs[l, b, 124:128, :], vf[0:DSEQ, :], r=[vf])
            return run

        NS = SPC * DSEQ
        k.dma("sp", H[0][0:NS, :], x_sample, w=[H[0]])
        sblocks = [(0, NS, H[0])]
        for l in range(DEPTH):
            layer(l, sblocks, [sample_mix_block(b) for b in range(SPC)], lambda l_, off, n: p_sample[l_, off:off + n, :])
        k.dma("sp", y_sample, H[0][0:NS, :], r=[H[0]])
        if STAGE <= 4:
            k.final_wait()
            return nc

        pst = []
        for l in range(DEPTH):
            st = new_state("p%d" % l)
            for nm in ("S", "Sb", "UFh", "KT_old", "KT_mid", "VB_old", "VB_mid"):
                k.memset(st[nm][:], 0.0, w=[st[nm]])
            pst.append(st)
        nst = SEQ // ST
        NMB = ST // 64
        for si in range(nst):
            t0 = si * ST
            blocks = []
            for bi in range(NB):
                k.dma("sp", H[bi][:], x_prompt[t0 + bi * 128:t0 + (bi + 1) * 128, :], w=[H[bi]])
                blocks.append((bi * 128, 128, H[bi]))
            for l in range(DEPTH):
                def mk(mi):
                    def run(l_):
                        mixer(l_, 64, mi * 64, pst[l_], t0 + mi * 64)
                        if si == nst - 1 and mi >= NMB - 2:
                            half = mi - (NMB - 2)
                            k.dma("sp", o_kp[l_, half * 64:(half + 1) * 64, :], krf[0:64, :], r=[krf])
                            k.dma("sp", o_vp[l_, half * 64:(half + 1) * 64, :], vf[0:64, :], r=[vf])
                        if si == nst - 1 and mi == NMB - 1:
                            stt_ = pst[l_]
                            k.dma("sp", o_hp[l_].rearrange("h k v -> k h v"), stt_["S"][:], r=[stt_["S"]])
                            for j in range(2):
                                p_ = psf()
                                tr32(p_[0:30, 0:128], UF[:, j, 64:94], 128, 30, [UF], [p_])
                                k.cp(tok30[0:30, j * 128:(j + 1) * 128], p_[0:30, 0:128], r=[p_], w=[tok30], eng="act")
                            k.dma("sp", o_cp[l_], tok30[0:30, :], r=[tok30])
                    return run
                layer(l, blocks, [mk(mi) for mi in range(NMB)], lambda l_, off, n: p_prompt[l_, t0 + off:t0 + off + n, :])
            for bi in range(NB):
                k.dma("sp", y_prompt[t0 + bi * 128:t0 + (bi + 1) * 128, :], H[bi][:], r=[H[bi]])
        k.final_wait()
    return nc


_CONSTS = None


def _consts():
    global _CONSTS
    if _CONSTS is None:
        i = np.arange(128)
        same = (i[:, None] // 64) == (i[None, :] // 64)
        U = ((i[:, None] <= i[None, :]) & same).astype(np.float32)
        W = ((i[:, None] > i[None, :]) & same).astype(np.float32)
        mprev = (i[:, None] >= i[None, :]).astype(np.float32)
        mdiag = (i[:, None] <= i[None, :]).astype(np.float32)
        bones = (same.astype(np.float32) / 64.0).astype(np.float32)
        half = 32
        inv = (10000.0 ** (-np.arange(half, dtype=np.float32) / half)).astype(np.float32)
        pos = np.concatenate([np.arange(SEQ, dtype=np.float32), PAST + np.arange(DSEQ, dtype=np.float32)])
        ang = (pos[:, None] * inv[None, :]).astype(np.float32)
        cos = np.tile(np.cos(ang).astype(np.float32), (1, 8))
        sin = np.tile(np.sin(ang).astype(np.float32), (1, 8))
        _CONSTS = dict(c_ident=np.eye(128, dtype=np.float32), c_U=U, c_W=W, c_mprev=mprev, c_mdiag=mdiag,
                       c_bones=bones, c_cos=np.ascontiguousarray(cos), c_sin=np.ascontiguousarray(sin))
    return _CONSTS


def kernel(x_prompt, x_sample, state_hgrn, state_conv, cache_swa_k, cache_swa_v, p_prompt, p_sample,
           a_lower, w_in, a_onorm, conv_w, conv_b, conv_ln_g, conv_ln_b, q_norm, k_norm, sinks, w_out,
           norm_mix, norm_ffn, w_gate, w_up, w_down, ple_norm, w_ple_gate, w_ple_proj):
    f = lambda a: np.ascontiguousarray(np.asarray(a, dtype=np.float32))
    shared = dict(x_prompt=f(x_prompt).reshape(SEQ, D), p_prompt=f(p_prompt).reshape(DEPTH, SEQ, 256),
                  a_lower=f(a_lower), w_in=f(w_in), a_onorm=f(a_onorm), conv_w=f(conv_w), conv_b=f(conv_b),
                  conv_ln_g=f(conv_ln_g), conv_ln_b=f(conv_ln_b), q_norm=f(q_norm), k_norm=f(k_norm), sinks=f(sinks),
                  w_out=f(w_out), norm_mix=f(norm_mix), norm_ffn=f(norm_ffn), w_gate=f(w_gate), w_up=f(w_up),
                  w_down=f(w_down), ple_norm=f(ple_norm), w_ple_gate=f(w_ple_gate), w_ple_proj=f(w_ple_proj))
    shared.update(_consts())
    xs, sh, sc = f(x_sample), f(state_hgrn), f(state_conv)
    ck, cv, ps_ = f(cache_swa_k), f(cache_swa_v), f(p_sample)
    in_maps = []
    for c in range(NCORE):
        b = slice(c * SPC, (c + 1) * SPC)
        m = dict(shared)
        m.update(x_sample=np.ascontiguousarray(xs[b]).reshape(SPC * DSEQ, D),
                 state_hgrn=np.ascontiguousarray(sh[:, b]),
                 state_conv=np.ascontiguousarray(sc[:, b]),
                 cache_k=np.ascontiguousarray(ck[:, b]).reshape(DEPTH, SPC, 128, 128),
                 cache_v=np.ascontiguousarray(cv[:, b]).reshape(DEPTH, SPC, 128, 128),
                 p_sample=np.ascontiguousarray(ps_[:, b]).reshape(DEPTH, SPC * DSEQ, 256))
        in_maps.append(m)
    nc = build_program()
    res = run_bass_kernel_spmd(nc, in_maps, core_ids=list(range(NCORE)))
    R = res.results
    cat = lambda name, ax: np.concatenate([R[c][name] for c in range(NCORE)], axis=ax)
    y_p = R[0]["y_prompt"].reshape(1, SEQ, D)
    y_s = cat("y_sample", 0).reshape(NSEQ, DSEQ, D)
    return (y_p.astype(np.float32), y_s.astype(np.float32),
            R[0]["o_hp"].reshape(DEPTH, 1, 4, 64, 64), R[0]["o_cp"].reshape(DEPTH, 1, 30, 256),
            R[0]["o_kp"].reshape(DEPTH, 1, 128, 2, 64), R[0]["o_vp"].reshape(DEPTH, 1, 128, 2, 64),
            cat("o_hs", 1), cat("o_cs", 1),
            cat("o_ks", 1).reshape(DEPTH, NSEQ, 128, 2, 64), cat("o_vs", 1).reshape(DEPTH, NSEQ, 128, 2, 64))
```

```python
import numpy as np
import ml_dtypes
from contextlib import ExitStack
import concourse.bass as bass
import concourse.mybir as mybir
from concourse.bass_utils import run_bass_kernel_spmd

F32 = mybir.dt.float32
BF16 = mybir.dt.bfloat16
AF = mybir.ActivationFunctionType
ALU = mybir.AluOpType
AX = mybir.AxisListType

NCORE = 8
D = 1024
SEQ = 16384
DEPTH = 2
NSEQ = 128
SPC = NSEQ // NCORE
DSEQ = 4
PAST = 16384
DFF = 2816
INC = 2304
EPS = 1e-6
ST = 512
SAME_SYNC = True
STAGE = 99
MSTAGE = 99
MBSEL = [0]
HSEL = [0, 1, 2, 3]


class Tl:
    def __init__(self, t):
        self.t = t
        self.w = None
        self.r = {}

    def __getitem__(self, idx):
        return self.t[idx]


class TlView(Tl):
    def __init__(self, parent, ap):
        self.t = ap
        self.p = parent

    w = property(lambda self: self.p.w, lambda self, v: setattr(self.p, "w", v))
    r = property(lambda self: self.p.r, lambda self, v: setattr(self.p, "r", v))


class KB:
    def __init__(self, nc, es):
        self.nc = nc
        self.es = es
        self.E = {}
        for name, h in (("pe", nc.tensor), ("act", nc.scalar), ("dve", nc.vector), ("pool", nc.gpsimd), ("sp", nc.sync)):
            self.E[name] = dict(h=h, sem=es.enter_context(nc.semaphore("sem_" + name)), count=0, waited={})
        self.dsem = {q: [[es.enter_context(nc.semaphore("dsem%s%d" % (q, i))), 0] for i in range(24)] for q in ("sp", "pool")}
        self.dnext = {"sp": 0, "pool": 0}
        self.n_inst = 0

    def sb(self, name, shape, dt):
        return Tl(self.es.enter_context(self.nc.sbuf_tensor(name, shape, dt)))

    def ps(self, name, shape, dt):
        return Tl(self.es.enter_context(self.nc.psum_tensor(name, shape, dt)))

    def _wait(self, eng, r, w):
        E = self.E[eng]
        deps = {}

        def add(ev):
            if ev is None:
                return
            s, v = ev
            k = id(s)
            if k not in deps or deps[k][1] < v:
                deps[k] = (s, v)
        for t in r:
            add(t.w)
        for t in w:
            add(t.w)
            for e in t.r.values():
                add(e)
        for k, (s, v) in deps.items():
            if s is E["sem"] and (eng == "pe" or not SAME_SYNC):
                continue
            if E["waited"].get(k, 0) >= v:
                continue
            E["h"].wait_ge(s, v)
            E["waited"][k] = v

    def _note(self, ev, r, w):
        for t in r:
            t.r[id(ev[0])] = ev
        for t in w:
            t.w = ev
            t.r = {}

    def op(self, eng, fn, r=(), w=(), inc=True):
        E = self.E[eng]
        self._wait(eng, r, w)
        inst = fn(E["h"])
        self.n_inst += 1
        if inc:
            E["count"] += 1
            inst.then_inc(E["sem"], 1)
            ev = (E["sem"], E["count"])
        else:
            ev = (E["sem"], E["count"] + 1)
        self._note(ev, r, w)

    def dma(self, eng, out, in_, r=(), w=(), slow=False):
        E = self.E[eng]
        self._wait(eng, r, w)
        slot = self.dsem[eng][self.dnext[eng]]
        self.dnext[eng] = (self.dnext[eng] + 1) % len(self.dsem[eng])
        s, v = slot
        if v > 0 and E["waited"].get(id(s), 0) < v:
            E["h"].wait_ge(s, v)
            E["waited"][id(s)] = v
        if slow:
            E["h"].dma_start(out=out, in_=in_, allow_slow_non_contiguous=True).then_inc(s, 16)
        else:
            E["h"].dma_start(out=out, in_=in_).then_inc(s, 16)
        slot[1] = v + 16
        self.n_inst += 1
        self._note((s, v + 16), r, w)

    def final_wait(self):
        E = self.E["sp"]
        for q in self.dsem:
            for s, v in self.dsem[q]:
                if v > 0:
                    E["h"].wait_ge(s, v)
        for name in ("pe", "act", "dve", "pool"):
            e = self.E[name]
            if e["count"] > 0:
                E["h"].wait_ge(e["sem"], e["count"])

    def mm(self, out, lhsT, rhs, start, stop, r, w, inc=None):
        if inc is None:
            inc = stop
        self.op("pe", lambda e: e.matmul(out, lhsT=lhsT, rhs=rhs, start=start, stop=stop), r=r, w=w, inc=inc)

    def tr(self, out, in_, ident, r, w):
        self.op("pe", lambda e: e.transpose(out, in_, ident), r=r, w=w)

    def act(self, out, in_, func, r, w, bias=None, scale=None, accum=None):
        kw = {}
        if bias is not None:
            kw["bias"] = bias
        if scale is not None:
            kw["scale"] = scale
        if accum is not None:
            kw["accum_out"] = accum
        self.op("act", lambda e: e.activation(out=out, in_=in_, func=func, **kw), r=r, w=w)

    def tt(self, out, in0, in1, op, r, w, eng="dve"):
        self.op(eng, lambda e: e.tensor_tensor(out=out, in0=in0, in1=in1, op=op), r=r, w=w)

    def ts(self, out, in0, s1, s2, op0, op1, r, w, eng="dve"):
        if op1 is None:
            self.op(eng, lambda e: e.tensor_scalar(out=out, in0=in0, scalar1=s1, scalar2=None, op0=op0), r=r, w=w)
        else:
            self.op(eng, lambda e: e.tensor_scalar(out=out, in0=in0, scalar1=s1, scalar2=s2, op0=op0, op1=op1), r=r, w=w)

    def stt(self, out, in0, scalar, in1, op0, op1, r, w):
        self.op("dve", lambda e: e.scalar_tensor_tensor(out=out, in0=in0, scalar=scalar, in1=in1, op0=op0, op1=op1), r=r, w=w)

    def cp(self, out, in_, r, w, eng="dve"):
        if eng == "act":
            self.act(out, in_, AF.Copy, r, w)
        else:
            self.op(eng, lambda e: e.tensor_copy(out=out, in_=in_), r=r, w=w)

    def recip(self, out, in_, r, w):
        self.op("dve", lambda e: e.reciprocal(out=out, in_=in_), r=r, w=w)

    def red(self, out, in_, r, w):
        self.op("dve", lambda e: e.tensor_reduce(out=out, in_=in_, axis=AX.X, op=ALU.add), r=r, w=w)

    def memset(self, ap, val, w, eng="dve"):
        self.op(eng, lambda e: e.memset(ap, val), r=(), w=w)


def build_program():
    nc = bass.Bass("TRN2", target_bir_lowering=False)

    def din(name, shape):
        return nc.dram_tensor(name, list(shape), F32, kind="ExternalInput").ap()

    def dout(name, shape):
        return nc.dram_tensor(name, list(shape), F32, kind="ExternalOutput").ap()

    x_prompt = din("x_prompt", (SEQ, D))
    x_sample = din("x_sample", (SPC * DSEQ, D))
    state_hgrn = din("state_hgrn", (DEPTH, SPC, 4, 64, 64))
    state_conv = din("state_conv", (DEPTH, SPC, 30, 256))
    cache_k = din("cache_k", (DEPTH, SPC, 128, 128))
    cache_v = din("cache_v", (DEPTH, SPC, 128, 128))
    p_prompt = din("p_prompt", (DEPTH, SEQ, 256))
    p_sample = din("p_sample", (DEPTH, SPC * DSEQ, 256))
    a_lower = din("a_lower", (DEPTH, 256))
    w_in = din("w_in", (DEPTH, D, INC))
    a_onorm = din("a_onorm", (DEPTH, 64))
    conv_w = din("conv_w", (DEPTH, 31, 256))
    conv_b = din("conv_b", (DEPTH, 256))
    conv_ln_g = din("conv_ln_g", (DEPTH, 256))
    conv_ln_b = din("conv_ln_b", (DEPTH, 256))
    q_norm = din("q_norm", (DEPTH, 64))
    k_norm = din("k_norm", (DEPTH, 64))
    sinks = din("sinks", (DEPTH, 8))
    w_out = din("w_out", (DEPTH, D, D))
    norm_mix = din("norm_mix", (DEPTH, D))
    norm_ffn = din("norm_ffn", (DEPTH, D))
    w_gate = din("w_gate", (DEPTH, D, DFF))
    w_up = din("w_up", (DEPTH, D, DFF))
    w_down = din("w_down", (DEPTH, DFF, D))
    ple_norm = din("ple_norm", (DEPTH, D))
    w_ple_gate = din("w_ple_gate", (DEPTH, D, D))
    w_ple_proj = din("w_ple_proj", (DEPTH, 256, D))
    c_ident = din("c_ident", (128, 128))
    c_U = din("c_U", (128, 128))
    c_W = din("c_W", (128, 128))
    c_mprev = din("c_mprev", (128, 128))
    c_mdiag = din("c_mdiag", (128, 128))
    c_bones = din("c_bones", (128, 128))
    c_cos = din("c_cos", (SEQ + DSEQ, 256))
    c_sin = din("c_sin", (SEQ + DSEQ, 256))

    y_prompt = dout("y_prompt", (SEQ, D))
    y_sample = dout("y_sample", (SPC * DSEQ, D))
    o_hp = dout("o_hp", (DEPTH, 4, 64, 64))
    o_cp = dout("o_cp", (DEPTH, 30, 256))
    o_kp = dout("o_kp", (DEPTH, 128, 128))
    o_vp = dout("o_vp", (DEPTH, 128, 128))
    o_hs = dout("o_hs", (DEPTH, SPC, 4, 64, 64))
    o_cs = dout("o_cs", (DEPTH, SPC, 30, 256))
    o_ks = dout("o_ks", (DEPTH, SPC, 128, 128))
    o_vs = dout("o_vs", (DEPTH, SPC, 128, 128))

    es = ExitStack()
    with es:
        k = KB(nc, es)
        sb, ps = k.sb, k.ps

        identf = sb("identf", [128, 128], F32)
        identb = sb("identb", [128, 128], BF16)
        Uf = sb("Uf", [128, 128], F32)
        mprev4 = sb("mprev4", [64, 4, 64], BF16)
        mdiag4 = sb("mdiag4", [64, 4, 64], BF16)
        onesb = sb("onesb", [128, 128], BF16)
        zerosb = sb("zerosb", [128, 128], BF16)
        eps_t = sb("eps_t", [128, 1], F32)
        gate_sb = sb("gate_sb", [128, 512], F32)
        for t, src in ((identf, c_ident), (Uf, c_U)):
            k.dma("sp", t[:], src, w=[t])
        for i_, src in enumerate((c_W, c_bones, c_mprev, c_mdiag)):
            k.dma("sp", gate_sb[:, i_ * 128:(i_ + 1) * 128], src, w=[gate_sb])
        k.cp(identb[:], identf[:], r=[identf], w=[identb])
        Ub = sb("Ub", [128, 128], BF16)
        Wb = sb("Wb", [128, 128], BF16)
        bonesb = sb("bonesb", [128, 128], BF16)
        k.cp(Ub[:], Uf[:], r=[Uf], w=[Ub])
        k.cp(Wb[:], gate_sb[:, 0:128], r=[gate_sb], w=[Wb])
        k.cp(bonesb[:], gate_sb[:, 128:256], r=[gate_sb], w=[bonesb])
        for g in range(4):
            k.cp(mprev4[:, g, :], gate_sb[0:64, 256:320], r=[gate_sb], w=[mprev4])
            k.cp(mdiag4[:, g, :], gate_sb[0:64, 384:448], r=[gate_sb], w=[mdiag4])
        k.memset(onesb[:], 1.0, w=[onesb])
        k.memset(zerosb[:], 0.0, w=[zerosb])
        k.memset(eps_t[:], EPS, w=[eps_t])
        one_t = sb("one_t", [128, 1], F32)
        k.memset(one_t[:], 1.0, w=[one_t])

        P = []
        dg = sb("dg", [128, 2, 31, 128], BF16)
        for l in range(DEPTH):
            L = {}
            for nm, src in (("g_mix", norm_mix), ("g_ffn", norm_ffn), ("g_ple", ple_norm)):
                t = sb("%s%d" % (nm, l), [128, 8], F32)
                k.dma("sp", t[:], src[l].rearrange("(c p) -> p c", p=128), w=[t], slow=True)
                L[nm] = t
            aon = sb("aon%d" % l, [64, 256], F32)
            for h in range(4):
                k.dma("sp", aon[:, h * 64:(h + 1) * 64], a_onorm[l].partition_broadcast(64), w=[aon])
            L["aon"] = aon
            qn = sb("qnr%d" % l, [64, 64], F32)
            kn = sb("knr%d" % l, [64, 64], F32)
            k.dma("sp", qn[:], q_norm[l].partition_broadcast(64), w=[qn])
            k.dma("sp", kn[:], k_norm[l].partition_broadcast(64), w=[kn])
            L["qn"], L["kn"] = qn, kn
            lbrow = sb("lbrow%d" % l, [64, 256], F32)
            omlrow = sb("omlrow%d" % l, [64, 256], F32)
            lbp = sb("lbp%d" % l, [64, 4], F32)
            omlp = sb("omlp%d" % l, [64, 4], F32)
            nomlp = sb("nomlp%d" % l, [64, 4], F32)
            if l == 0:
                k.memset(lbrow[:], 0.0, w=[lbrow])
                k.memset(lbp[:], 0.0, w=[lbp])
            else:
                a0 = sb("a0row", [64, 256], F32)
                a0p = sb("a0p", [64, 4], F32)
                k.dma("sp", lbrow[:], a_lower[1].partition_broadcast(64), w=[lbrow])
                k.dma("sp", a0[:], a_lower[0].partition_broadcast(64), w=[a0])
                k.dma("sp", lbp[:], a_lower[1].rearrange("(h p) -> p h", p=64), w=[lbp], slow=True)
                k.dma("sp", a0p[:], a_lower[0].rearrange("(h p) -> p h", p=64), w=[a0p], slow=True)
                k.tt(lbrow[:], lbrow[:], a0[:], ALU.subtract, r=[a0], w=[lbrow])
                k.tt(lbp[:], lbp[:], a0p[:], ALU.subtract, r=[a0p], w=[lbp])
                k.act(lbrow[:], lbrow[:], AF.Sigmoid, r=[], w=[lbrow])
                k.act(lbp[:], lbp[:], AF.Sigmoid, r=[], w=[lbp])
            k.ts(omlrow[:], lbrow[:], -1.0, 1.0, ALU.mult, ALU.add, r=[lbrow], w=[omlrow])
            k.ts(omlp[:], lbp[:], -1.0, 1.0, ALU.mult, ALU.add, r=[lbp], w=[omlp])
            k.ts(nomlp[:], omlp[:], -1.0, None, ALU.mult, None, r=[omlp], w=[nomlp])
            L.update(lbrow=lbrow, omlrow=omlrow, lbp=lbp, omlp=omlp, nomlp=nomlp)
            ptmp = sb("ptmp%d" % l, [128, 2, 31], F32)
            for j in range(2):
                k.dma("sp", ptmp[:, j, :], conv_w[l][:, j * 128:(j + 1) * 128].rearrange("i p -> p i"), w=[ptmp], slow=True)
            cb = sb("cb%d" % l, [128, 2], F32)
            cg = sb("cg%d" % l, [128, 2], F32)
            cbb = sb("cbb%d" % l, [128, 2], F32)
            k.dma("sp", cb[:], conv_b[l].rearrange("(j p) -> p j", p=128), w=[cb], slow=True)
            k.dma("sp", cg[:], conv_ln_g[l].rearrange("(j p) -> p j", p=128), w=[cg], slow=True)
            k.dma("sp", cbb[:], conv_ln_b[l].rearrange("(j p) -> p j", p=128), w=[cbb], slow=True)
            L.update(dg=dg, cb=cb, cg=cg, cbb=cbb, ptmp=ptmp)
            sk = sb("sk%d" % l, [64, 8], F32)
            esink = sb("esink%d" % l, [64, 8, 64], F32)
            k.dma("sp", sk[:], sinks[l].partition_broadcast(64), w=[sk])
            k.act(sk[:], sk[:], AF.Exp, r=[], w=[sk])
            for h in range(8):
                k.ts(esink[:, h, :], onesb[0:64, 0:64], sk[:, h:h + 1], None, ALU.mult, None, r=[sk, onesb], w=[esink])
            L["esink"] = esink
            P.append(L)

        NB = ST // 128
        H = [sb("H%d" % i, [128, D], F32) for i in range(NB)]
        xT = sb("xT", [128, 8, ST], BF16)
        mixT = sb("mixT", [128, 4, ST], BF16)
        ocT = sb("ocT", [64, 8, ST], BF16)
        aT = sb("aT", [128, 22, ST], BF16)
        pT = sb("pT", [128, 2, ST], BF16)
        WIN = sb("WIN", [128, 8, INC], BF16)
        WINr = [Tl(None) for _ in range(5)]
        NPAN = 3
        PAN = [sb("pan%d" % i, [128, 8, 512], BF16) for i in range(NPAN)]
        pan_i = [0]
        PSF = [ps("psf%d" % i, [128, 512], F32) for i in range(6)]
        PSB = [ps("psb%d" % i, [128, 1024], BF16) for i in range(2)]
        psf_i = [0]
        psb_i = [0]

        def psf():
            t = PSF[psf_i[0] % 6]
            psf_i[0] += 1
            return t

        def psb():
            t = PSB[psb_i[0] % 2]
            psb_i[0] += 1
            return t

        cA = [0]
        cC = [0]

        def psfA():
            cA[0] += 1
            return PSF[cA[0] % 2]

        def psfC():
            cC[0] += 1
            return PSF[3 + cC[0] % 3]

        def hiloc(src_ap, f, r):
            k.cp(hi_c[:, 0:f], src_ap, r=r, w=[hi_c])
            k.tt(lo_c[:, 0:f], src_ap, hi_c[:, 0:f], ALU.subtract, r=r + [hi_c], w=[lo_c])

        def next_pan():
            t = PAN[pan_i[0] % NPAN]
            pan_i[0] += 1
            return t

        xn = sb("xn", [128, D], BF16)
        ssq = sb("ssq", [128, 8], F32)
        rst = sb("rst", [128, 8], F32)
        MB = 64
        qT = sb("qT", [64, 4, MB], F32)
        kinT = sb("kinT", [64, 4, MB], F32)
        sgT = sb("sgT", [128, MB], F32)
        logf = sb("logf", [64, 256], F32)
        kin = sb("kin", [64, 256], F32)
        ftm = sb("ftm", [64, 256], F32)
        va = sb("va", [64, 256], BF16)
        sg = sb("sg", [64, 256], F32)
        GT = sb("GT", [64, 4, MB], F32)
        ER = sb("ER", [64, 256], F32)
        kd = sb("kd", [64, 256], BF16)
        gm = sb("gm", [64, 4], F32)
        ngm = sb("ngm", [64, 4], F32)
        E1 = sb("E1", [64, 4, MB], F32)
        E2 = sb("E2", [64, 4, MB], F32)
        E3 = sb("E3", [64, 4, MB], F32)
        qeT = sb("qeT", [64, 4, MB], BF16)
        keT = sb("keT", [64, 4, MB], BF16)
        qgT = sb("qgT", [64, 4, MB], BF16)
        attT = sb("attT", [64, 4, MB], BF16)
        oaf = sb("oaf", [64, 256], BF16)
        UF = sb("UF", [128, 2, 32 + MB], F32)
        UB = sb("UB", [128, 2, 32 + MB], BF16)
        UBo = sb("UBo", [128, 2, 32 + MB], BF16)
        yb = sb("yb", [128, MB], F32)
        ysq = sb("ysq", [128, MB], F32)
        cmean = sb("cmean", [128, MB], F32)
        cvar = sb("cvar", [128, MB], F32)
        zq = sb("zq", [64, 512], F32)
        zkv = sb("zkv", [64, 256], F32)
        qnf = sb("qnf", [64, 512], F32)
        qr = sb("qr", [64, 512], BF16)
        rt = [sb("rt%d" % i, [64, 8, 32], F32) for i in range(4)]
        QT = sb("QT", [64, 8, MB], BF16)
        knf = sb("knf", [64, 128], F32)
        krf = sb("krf", [64, 128], F32)
        krb = sb("krb", [64, 128], BF16)
        vf = sb("vf", [64, 128], F32)
        Pm8 = [sb("Pm%d" % i, [64, 512], BF16) for i in range(3)]
        Pm = [TlView(t8, t8.t[:, 0:256].rearrange("p (m t) -> p m t", m=4)) for t8 in Pm8]
        dent8 = sb("dent8", [64, 512], F32)
        dent = TlView(dent8, dent8.t[:, 0:256].rearrange("p (m t) -> p m t", m=4))
        cosT = sb("cosT", [64, 256], F32)
        sinT = sb("sinT", [64, 256], F32)
        pb = sb("pb", [128, 256], BF16)
        tok30 = sb("tok30", [32, 256], F32)
        halo_tok = tok30
        osq = sb("osq", [64, 256], F32)
        ssqA = sb("ssqA", [64, 4], F32)
        rstA = sb("rstA", [64, 4], F32)
        hi_c = sb("hi_c", [128, MB], BF16)
        lo_c = sb("lo_c", [128, MB], BF16)
        hi_t = sb("hi_t", [128, 256], BF16)
        lo_t = sb("lo_t", [128, 256], BF16)

        def new_state(tag):
            st = dict(S=sb("S" + tag, [64, 4, 64], F32), Sb=sb("Sb" + tag, [64, 4, 64], BF16),
                      UFh=sb("UFh" + tag, [128, 2, 30], F32))
            kts = [sb("KT%s_%d" % (tag, i), [64, 2, 64], BF16) for i in range(3)]
            vbs = [sb("VB%s_%d" % (tag, i), [64, 128], BF16) for i in range(3)]
            st.update(KT_old=kts[0], KT_mid=kts[1], KT_free=kts[2], VB_old=vbs[0], VB_mid=vbs[1], VB_free=vbs[2],
                      ones_old=zerosb, ones_mid=zerosb)
            return st

        def sigmoid_el(out, in_, p0, p1, r, w):
            k.act(out, in_, AF.Exp, r=r, w=w, scale=-1.0)
            k.act(out, out, AF.Ln, r=[one_t], w=w, bias=one_t[p0:p1, :], scale=1.0)
            k.act(out, out, AF.Exp, r=[], w=w, scale=-1.0)

        def hilo(src_ap, p, f, r):
            k.cp(hi_t[0:p, 0:f], src_ap, r=r, w=[hi_t])
            k.tt(lo_t[0:p, 0:f], src_ap, hi_t[0:p, 0:f], ALU.subtract, r=r + [hi_t], w=[lo_t])

        def tr32(out_ps_ap, src_ap, p, f, r, w):
            hilo(src_ap, p, f, r)
            k.mm(out_ps_ap, lhsT=hi_t[0:p, 0:f], rhs=identb[0:p, 0:p], start=True, stop=False, r=[hi_t, identb], w=w)
            k.mm(out_ps_ap, lhsT=lo_t[0:p, 0:f], rhs=identb[0:p, 0:p], start=False, stop=True, r=[lo_t, identb], w=w)

        def rmsnorm_T(h_t, n, grow, off):
            k.act(xn[0:n, :], h_t[0:n, :], AF.Square, r=[h_t], w=[xn, ssq], accum=ssq[0:n, 0:1])
            k.act(rst[0:n, 0:1], ssq[0:n, 0:1], AF.Ln, r=[ssq, eps_t], w=[rst], bias=eps_t[0:n, :], scale=1.0 / D)
            k.act(rst[0:n, 0:1], rst[0:n, 0:1], AF.Exp, r=[], w=[rst], scale=-0.5)
            k.ts(xn[0:n, :], h_t[0:n, :], rst[0:n, 0:1], None, ALU.mult, None, r=[h_t, rst], w=[xn])
            pt = psb()
            for c in range(8):
                k.tr(pt[:, c * 128:c * 128 + n], xn[0:n, c * 128:(c + 1) * 128], identb[0:n, 0:n], r=[xn, identb], w=[pt])
            for c in range(8):
                k.ts(xT[:, c, off:off + n], pt[:, c * 128:c * 128 + n], grow[:, c:c + 1], None, ALU.mult, None, r=[pt, grow], w=[xT])

        def lin_tok(out_ps, n, off, c0, c1, wt, wr, act_t, act_r, nk=8, kp=128, first=True, last=True):
            for kc in range(nk):
                k.mm(out_ps[0:n, 0:c1 - c0], lhsT=act_t[0:kp, kc, off:off + n], rhs=wt[0:kp, kc, c0:c1],
                     start=(first and kc == 0), stop=(last and kc == nk - 1), r=[act_r] + wr, w=[out_ps],
                     inc=(kc == nk - 1))

        def lin_feat(out_ps, n, off, c0, wt, wr, act_t, act_r, width=128):
            for kc in range(8):
                k.mm(out_ps[0:width, 0:n], lhsT=wt[:, kc, c0:c0 + width], rhs=act_t[:, kc, off:off + n],
                     start=(kc == 0), stop=(kc == 7), r=[act_r] + wr, w=[out_ps])

        def mixer(l, n, off, st, pos0):
            L = P[l]
            S, Sb = st["S"], st["Sb"]
            for h in range(4):
                p_ = psf()
                lin_feat(p_, n, off, h * 64, WIN, [WINr[0]], xT, xT, width=64)
                k.cp(qT[:, h, 0:n], p_[0:64, 0:n], r=[p_], w=[qT], eng="act")
            for h in range(4):
                p_ = psf()
                lin_feat(p_, n, off, 256 + h * 64, WIN, [WINr[0]], xT, xT, width=64)
                k.act(sgT[0:64, 0:n], p_[0:64, 0:n], AF.Sigmoid, r=[p_], w=[sgT])
                k.ts(kinT[:, h, 0:n], sgT[0:64, 0:n], L["nomlp"][:, h:h + 1], L["omlp"][:, h:h + 1], ALU.mult, ALU.add,
                     r=[sgT, L["nomlp"], L["omlp"]], w=[kinT])
            p_ = psf()
            lin_tok(p_, n, off, 256, 512, WIN, [WINr[0]], xT, xT)
            k.act(ftm[0:n, :], p_[0:n, 0:256], AF.Sigmoid, r=[p_], w=[ftm])
            k.tt(ftm[0:n, :], ftm[0:n, :], L["omlrow"][0:n, :], ALU.mult, r=[L["omlrow"]], w=[ftm])
            k.tt(ftm[0:n, :], ftm[0:n, :], L["lbrow"][0:n, :], ALU.add, r=[L["lbrow"]], w=[ftm])
            k.act(logf[0:n, :], ftm[0:n, :], AF.Ln, r=[ftm], w=[logf])
            k.ts(kin[0:n, :], ftm[0:n, :], -1.0, 1.0, ALU.mult, ALU.add, r=[ftm], w=[kin])
            p_ = psf()
            lin_tok(p_, n, off, 512, 1024, WIN, [WINr[1]], xT, xT)
            k.cp(va[0:n, :], p_[0:n, 0:256], r=[p_], w=[va])
            k.act(sg[0:n, :], p_[0:n, 256:512], AF.Sigmoid, r=[p_], w=[sg])
            k.tt(sg[0:n, :], sg[0:n, :], p_[0:n, 256:512], ALU.mult, r=[p_], w=[sg])
            k.tt(sg[0:n, :], sg[0:n, :], L["aon"][0:n, :], ALU.mult, r=[L["aon"]], w=[sg])
            for j in range(2):
                pu = psf()
                lin_feat(pu, n, off, 1024 + j * 128, WIN, [WINr[2]], xT, xT)
                pg = psf()
                lin_feat(pg, n, off, 1280 + j * 128, WIN, [WINr[2]], xT, xT)
                k.act(sgT[:, 0:n], pg[:, 0:n], AF.Sigmoid, r=[pg], w=[sgT])
                k.tt(UF[:, j, 30:30 + n], pu[:, 0:n], sgT[:, 0:n], ALU.mult, r=[pu, sgT], w=[UF])
            k.cp(UF[:, :, 0:30], st["UFh"][:], r=[st["UFh"]], w=[UF])
            k.cp(UB[:, :, 0:30 + n], UF[:, :, 0:30 + n], r=[UF], w=[UB])
            k.cp(UBo[:, :, 0:29 + n], UF[:, :, 1:30 + n], r=[UF], w=[UBo])
            p_ = psf()
            lin_tok(p_, n, off, 1536, 2048, WIN, [WINr[3]], xT, xT)
            k.cp(zq[0:n, :], p_[0:n, :], r=[p_], w=[zq], eng="act")
            p_ = psf()
            lin_tok(p_, n, off, 2048, 2304, WIN, [WINr[4]], xT, xT)
            k.cp(zkv[0:n, :], p_[0:n, 0:256], r=[p_], w=[zkv], eng="act")

            KTc, VBc = st["KT_free"], st["VB_free"]

            def chainA():
                yield
                hilo(logf[0:n, :], n, 256, [logf])
                for h in range(4):
                    p_ = psfA()
                    yield
                    k.mm(p_[0:64, 0:n], lhsT=hi_t[0:n, h * 64:(h + 1) * 64], rhs=Ub[0:n, 0:n], start=True, stop=False,
                         r=[hi_t, Ub], w=[p_])
                    yield
                    k.mm(p_[0:64, 0:n], lhsT=lo_t[0:n, h * 64:(h + 1) * 64], rhs=Ub[0:n, 0:n], start=False, stop=True,
                         r=[lo_t, Ub], w=[p_])
                    yield
                    k.cp(GT[:, h, 0:n], p_[0:64, 0:n], r=[p_], w=[GT], eng="act")
                p_ = psfA()
                yield
                k.mm(p_[0:n, 0:256], lhsT=Wb[0:n, 0:n], rhs=hi_t[0:n, :], start=True, stop=False, r=[hi_t, Wb], w=[p_])
                yield
                k.mm(p_[0:n, 0:256], lhsT=Wb[0:n, 0:n], rhs=lo_t[0:n, :], start=False, stop=True, r=[lo_t, Wb], w=[p_])
                yield
                k.act(ER[0:n, :], p_[0:n, 0:256], AF.Exp, r=[p_], w=[ER])
                yield
                k.tt(kd[0:n, :], kin[0:n, :], ER[0:n, :], ALU.mult, r=[kin, ER], w=[kd])
                rc = max(n // 2 - 1, 0)
                yield
                k.tt(E1[:, :, 0:n], GT[:, :, 0:n], GT[:, :, rc:rc + 1].to_broadcast([64, 4, n]), ALU.subtract, r=[GT], w=[E1])
                yield
                k.act(E2[:, :, 0:n], E1[:, :, 0:n], AF.Exp, r=[E1], w=[E2], scale=-1.0)
                yield
                k.act(E1[:, :, 0:n], E1[:, :, 0:n], AF.Exp, r=[], w=[E1])
                yield
                k.act(E3[:, :, 0:n], GT[:, :, 0:n], AF.Exp, r=[GT], w=[E3])
                yield
                k.tt(qeT[:, :, 0:n], qT[:, :, 0:n], E1[:, :, 0:n], ALU.mult, r=[qT, E1], w=[qeT])
                yield
                k.tt(keT[:, :, 0:n], kinT[:, :, 0:n], E2[:, :, 0:n], ALU.mult, r=[kinT, E2], w=[keT])
                yield
                k.tt(qgT[:, :, 0:n], qT[:, :, 0:n], E3[:, :, 0:n], ALU.mult, r=[qT, E3], w=[qgT])
                pA = psfA()
                for h in range(4):
                    yield
                    k.mm(pA[0:n, h * 64:h * 64 + n], lhsT=keT[:, h, 0:n], rhs=qeT[:, h, 0:n], start=True, stop=True,
                         r=[keT, qeT], w=[pA])
                yield
                k.tt(attT[0:n, :, 0:n], pA[0:n, 0:256].rearrange("p (h t) -> p h t", h=4)[:, :, 0:n],
                     Uf[0:n, 0:n].unsqueeze(1).to_broadcast([n, 4, n]), ALU.mult, r=[pA, Uf], w=[attT])
                pO = psfA()
                for h in range(4):
                    yield
                    k.mm(pO[0:n, h * 64:(h + 1) * 64], lhsT=attT[0:n, h, 0:n], rhs=va[0:n, h * 64:(h + 1) * 64],
                         start=True, stop=False, r=[attT, va], w=[pO])
                    yield
                    k.mm(pO[0:n, h * 64:(h + 1) * 64], lhsT=qgT[:, h, 0:n], rhs=Sb[:, h, :],
                         start=False, stop=True, r=[qgT, Sb], w=[pO])
                pU = psfA()
                for h in range(4):
                    yield
                    k.mm(pU[0:64, h * 64:(h + 1) * 64], lhsT=kd[0:n, h * 64:(h + 1) * 64], rhs=va[0:n, h * 64:(h + 1) * 64],
                         start=True, stop=True, r=[kd, va], w=[pU])
                yield
                k.tt(S[:], S[:], E3[:, :, n - 1:n].to_broadcast([64, 4, 64]), ALU.mult, r=[E3], w=[S])
                yield
                k.tt(S[:], S[:], pU[0:64, 0:256].rearrange("p (h v) -> p h v", h=4), ALU.add, r=[pU], w=[S])
                yield
                k.cp(Sb[:], S[:], r=[S], w=[Sb])
                yield
                k.act(osq[0:n, :], pO[0:n, 0:256], AF.Square, r=[pO], w=[osq])
                yield
                k.red(ssqA[0:n, 0:4], osq[0:n, :].rearrange("p (h d) -> p h d", h=4), r=[osq], w=[ssqA])
                yield
                k.act(rstA[0:n, 0:4], ssqA[0:n, 0:4], AF.Ln, r=[ssqA, eps_t], w=[rstA], bias=eps_t[0:n, :], scale=1.0 / 64)
                yield
                k.act(rstA[0:n, 0:4], rstA[0:n, 0:4], AF.Exp, r=[], w=[rstA], scale=-0.5)
                yield
                k.tt(osq[0:n, :].rearrange("p (h d) -> p h d", h=4), pO[0:n, 0:256].rearrange("p (h d) -> p h d", h=4),
                     rstA[0:n, 0:4].unsqueeze(2).to_broadcast([n, 4, 64]), ALU.mult, r=[pO, rstA], w=[osq])
                yield
                k.tt(oaf[0:n, :], osq[0:n, :], sg[0:n, :], ALU.mult, r=[osq, sg], w=[oaf])
                pt = PSB[0]
                for m in range(2):
                    yield
                    k.tr(pt[:, m * 128:m * 128 + n], oaf[0:n, m * 128:(m + 1) * 128], identb[0:n, 0:n], r=[oaf, identb], w=[pt])
                for m in range(2):
                    yield
                    k.cp(mixT[:, m, off:off + n], pt[:, m * 128:m * 128 + n], r=[pt], w=[mixT])


                yield
            def chainB():
                for j in range(2):
                    pY = PSF[2]
                    for i in range(31):
                        src_ = UB[:, j, i:i + n] if i % 2 == 0 else UBo[:, j, i - 1:i - 1 + n]
                        yield
                        k.mm(pY[:, 0:n], lhsT=L["dg"][:, j, i, :], rhs=src_, start=(i == 0), stop=(i == 30),
                             r=[L["dg"], UB, UBo], w=[pY])
                    yield
                    k.ts(yb[:, 0:n], pY[:, 0:n], L["cb"][:, j:j + 1], None, ALU.add, None, r=[pY, L["cb"]], w=[yb])
                    yield
                    k.tt(ysq[:, 0:n], yb[:, 0:n], yb[:, 0:n], ALU.mult, r=[yb], w=[ysq])
                    pM = PSF[2]
                    yield
                    hiloc(yb[:, 0:n], n, [yb])
                    yield
                    k.mm(pM[:, 0:n], lhsT=bonesb[:], rhs=hi_c[:, 0:n], start=True, stop=False, r=[bonesb, hi_c], w=[pM])
                    yield
                    k.mm(pM[:, 0:n], lhsT=bonesb[:], rhs=lo_c[:, 0:n], start=False, stop=True, r=[bonesb, lo_c], w=[pM])
                    pQ = PSF[2]
                    yield
                    hiloc(ysq[:, 0:n], n, [ysq])
                    yield
                    k.mm(pQ[:, 256:256 + n], lhsT=bonesb[:], rhs=hi_c[:, 0:n], start=True, stop=False, r=[bonesb, hi_c], w=[pQ])
                    yield
                    k.mm(pQ[:, 256:256 + n], lhsT=bonesb[:], rhs=lo_c[:, 0:n], start=False, stop=True, r=[bonesb, lo_c], w=[pQ])
                    yield
                    k.cp(cmean[:, 0:n], pM[:, 0:n], r=[pM], w=[cmean], eng="act")
                    yield
                    k.tt(ysq[:, 0:n], cmean[:, 0:n], cmean[:, 0:n], ALU.mult, r=[cmean], w=[ysq])
                    yield
                    k.tt(cvar[:, 0:n], pQ[:, 256:256 + n], ysq[:, 0:n], ALU.subtract, r=[pQ, ysq], w=[cvar])
                    yield
                    k.act(cvar[:, 0:n], cvar[:, 0:n], AF.Ln, r=[eps_t], w=[cvar], bias=eps_t[:], scale=1.0)
                    yield
                    k.act(cvar[:, 0:n], cvar[:, 0:n], AF.Exp, r=[], w=[cvar], scale=-0.5)
                    yield
                    k.tt(yb[:, 0:n], yb[:, 0:n], cmean[:, 0:n], ALU.subtract, r=[yb, cmean], w=[yb])
                    yield
                    k.tt(yb[:, 0:n], yb[:, 0:n], cvar[:, 0:n], ALU.mult, r=[cvar], w=[yb])
                    yield
                    k.ts(yb[:, 0:n], yb[:, 0:n], L["cg"][:, j:j + 1], L["cbb"][:, j:j + 1], ALU.mult, ALU.add, r=[L["cg"], L["cbb"]], w=[yb])
                    yield
                    k.act(cvar[:, 0:n], yb[:, 0:n], AF.Exp, r=[yb], w=[cvar], scale=-1.0)
                    yield
                    k.act(cvar[:, 0:n], cvar[:, 0:n], AF.Ln, r=[one_t], w=[cvar], bias=one_t[:], scale=1.0)
                    yield
                    k.act(cvar[:, 0:n], cvar[:, 0:n], AF.Exp, r=[], w=[cvar], scale=-1.0)
                    yield
                    k.tt(mixT[:, 2 + j, off:off + n], yb[:, 0:n], cvar[:, 0:n], ALU.mult, r=[yb, cvar], w=[mixT])
                yield
                k.cp(st["UFh"][:], UF[:, :, n:n + 30], r=[UF], w=[st["UFh"]])


                yield
            def chainC():
                yield
                k.dma("sp", cosT[0:n, :], c_cos[pos0:pos0 + n, :], w=[cosT])
                yield
                k.dma("sp", sinT[0:n, :], c_sin[pos0:pos0 + n, :], w=[sinT])
                yield
                k.act(qnf[0:n, :], zq[0:n, :], AF.Square, r=[zq], w=[qnf])
                yield
                k.red(ssq[0:n, 0:8], qnf[0:n, :].rearrange("p (h d) -> p h d", h=8), r=[qnf], w=[ssq])
                yield
                k.act(rst[0:n, 0:8], ssq[0:n, 0:8], AF.Ln, r=[ssq, eps_t], w=[rst], bias=eps_t[0:n, :], scale=1.0 / 64)
                yield
                k.act(rst[0:n, 0:8], rst[0:n, 0:8], AF.Exp, r=[], w=[rst], scale=-0.5)
                yield
                k.tt(qnf[0:n, :].rearrange("p (h d) -> p h d", h=8), zq[0:n, :].rearrange("p (h d) -> p h d", h=8),
                     rst[0:n, 0:8].unsqueeze(2).to_broadcast([n, 8, 64]), ALU.mult, r=[zq, rst], w=[qnf])
                yield
                k.tt(qnf[0:n, :].rearrange("p (h d) -> p h d", h=8), qnf[0:n, :].rearrange("p (h d) -> p h d", h=8),
                     L["qn"][0:n, :].unsqueeze(1).to_broadcast([n, 8, 64]), ALU.mult, r=[L["qn"]], w=[qnf])
                cosv = cosT[0:n, :].rearrange("p (m d) -> p m d", m=8)
                sinv = sinT[0:n, :].rearrange("p (m d) -> p m d", m=8)
                src = qnf[0:n, :].rearrange("p (m d) -> p m d", m=8)
                dst = qr[0:n, :].rearrange("p (m d) -> p m d", m=8)
                x1, x2 = src[:, :, 0:32], src[:, :, 32:64]
                a, b, c, d_ = [t[0:n] for t in rt]
                yield
                k.tt(a, x1, cosv, ALU.mult, r=[qnf, cosT], w=[rt[0]])
                yield
                k.tt(b, x2, sinv, ALU.mult, r=[qnf, sinT], w=[rt[1]])
                yield
                k.tt(dst[:, :, 0:32], a, b, ALU.subtract, r=[rt[0], rt[1]], w=[qr])
                yield
                k.tt(c, x2, cosv, ALU.mult, r=[qnf, cosT], w=[rt[2]])
                yield
                k.tt(d_, x1, sinv, ALU.mult, r=[qnf, sinT], w=[rt[3]])
                yield
                k.tt(dst[:, :, 32:64], c, d_, ALU.add, r=[rt[2], rt[3]], w=[qr])
                pt = PSB[1]
                for h in range(8):
                    yield
                    k.tr(pt[0:64, h * 64:h * 64 + n], qr[0:n, h * 64:(h + 1) * 64], identb[0:n, 0:n], r=[qr, identb], w=[pt])
                for h in range(8):
                    yield
                    k.cp(QT[:, h, 0:n], pt[0:64, h * 64:h * 64 + n], r=[pt], w=[QT])
                yield
                k.act(knf[0:n, :], zkv[0:n, 0:128], AF.Square, r=[zkv], w=[knf])
                yield
                k.red(ssq[0:n, 0:2], knf[0:n, :].rearrange("p (h d) -> p h d", h=2), r=[knf], w=[ssq])
                yield
                k.act(rst[0:n, 0:2], ssq[0:n, 0:2], AF.Ln, r=[ssq, eps_t], w=[rst], bias=eps_t[0:n, :], scale=1.0 / 64)
                yield
                k.act(rst[0:n, 0:2], rst[0:n, 0:2], AF.Exp, r=[], w=[rst], scale=-0.5)
                for h in range(2):
                    cs = slice(h * 64, (h + 1) * 64)
                    yield
                    k.stt(knf[0:n, cs], zkv[0:n, cs], rst[0:n, h:h + 1], L["kn"][0:n, :], ALU.mult, ALU.mult, r=[zkv, rst, L["kn"]], w=[knf])
                src = knf[0:n, :].rearrange("p (m d) -> p m d", m=2)
                x1, x2 = src[:, :, 0:32], src[:, :, 32:64]
                dst = krf[0:n, :].rearrange("p (m d) -> p m d", m=2)
                cos2 = cosT[0:n, 0:64].rearrange("p (m d) -> p m d", m=2)
                sin2 = sinT[0:n, 0:64].rearrange("p (m d) -> p m d", m=2)
                a, b, c, d_ = [t[0:n, 0:2, :] for t in rt]
                yield
                k.tt(a, x1, cos2, ALU.mult, r=[knf, cosT], w=[rt[0]])
                yield
                k.tt(b, x2, sin2, ALU.mult, r=[knf, sinT], w=[rt[1]])
                yield
                k.tt(dst[:, :, 0:32], a, b, ALU.subtract, r=[rt[0], rt[1]], w=[krf])
                yield
                k.tt(c, x2, cos2, ALU.mult, r=[knf, cosT], w=[rt[2]])
                yield
                k.tt(d_, x1, sin2, ALU.mult, r=[knf, sinT], w=[rt[3]])
                yield
                k.tt(dst[:, :, 32:64], c, d_, ALU.add, r=[rt[2], rt[3]], w=[krf])
                yield
                k.cp(krb[0:n, :], krf[0:n, :], r=[krf], w=[krb])
                KTc, VBc = st["KT_free"], st["VB_free"]
                pt = PSB[1]
                for kv in range(2):
                    yield
                    k.tr(pt[0:64, kv * 64:kv * 64 + n], krb[0:n, kv * 64:(kv + 1) * 64], identb[0:n, 0:n], r=[krb, identb], w=[pt])
                for kv in range(2):
                    yield
                    k.cp(KTc[:, kv, 0:n], pt[0:64, kv * 64:kv * 64 + n], r=[pt], w=[KTc])
                yield
                k.cp(vf[0:n, :], zkv[0:n, 128:256], r=[zkv], w=[vf], eng="act")
                yield
                k.cp(VBc[0:n, :], zkv[0:n, 128:256], r=[zkv], w=[VBc])
                kblocks = ((st["KT_old"], st["VB_old"], st["ones_old"], 64, mprev4),
                           (st["KT_mid"], st["VB_mid"], st["ones_mid"], 64, None),
                           (KTc, VBc, onesb, n, mdiag4))
                if n == 64:
                    pNs = (PSF[3], PSF[4])
                    pS = PSF[5]
                    valid = []
                    for bi, (KTk, VBk, ones_k, nk, mask) in enumerate(kblocks):
                        if ones_k is zerosb:
                            continue
                        valid.append(bi)
                    for vi, bi in enumerate(valid):
                        KTk, VBk, ones_k, nk, mask = kblocks[bi]
                        for kv in range(2):
                            yield
                            k.mm(pS[0:nk, kv * 256:kv * 256 + 256], lhsT=KTk[:, kv, 0:nk], rhs=QT[:, kv * 4:(kv + 1) * 4, 0:n],
                                 start=True, stop=True, r=[KTk, QT], w=[pS])
                        Pt = Pm8[bi]
                        yield
                        k.act(Pt[0:nk, :], pS[0:nk, :], AF.Exp, r=[pS], w=[Pt], scale=0.125)
                        if mask is not None:
                            yield
                            k.tt(Pt[0:nk, :].rearrange("p (g x) -> p g x", g=2), Pt[0:nk, :].rearrange("p (g x) -> p g x", g=2),
                                 mask[0:nk, :, :].rearrange("p m t -> p (m t)").unsqueeze(1).to_broadcast([nk, 2, 256]),
                                 ALU.mult, r=[mask], w=[Pt])
                        for kv in range(2):
                            yield
                            k.mm(pNs[kv][0:64, 0:256], lhsT=VBk[0:nk, kv * 64:(kv + 1) * 64], rhs=Pt[0:nk, kv * 256:(kv + 1) * 256],
                                 start=(vi == 0), stop=(vi == len(valid) - 1), r=[VBk, Pt], w=[pNs[kv]], inc=True)
                    Pl = Pm8[valid[-1]]
                    for bi in valid[:-1]:
                        yield
                        k.tt(Pl[:, :], Pl[:, :], Pm8[bi][:, :], ALU.add, r=[Pm8[bi]], w=[Pl])
                    yield
                    k.mm(pS[0:64, :], lhsT=onesb[0:64, 0:64], rhs=Pl[:, :], start=True, stop=True, r=[onesb, Pl], w=[pS])
                    yield
                    k.tt(dent8[:, :], pS[0:64, :], L["esink"][:, :, :].rearrange("p h t -> p (h t)"), ALU.add, r=[pS, L["esink"]], w=[dent8])
                    yield
                    k.act(dent8[:, :], dent8[:, :], AF.Ln, r=[], w=[dent8])
                    yield
                    k.act(dent8[:, :], dent8[:, :], AF.Exp, r=[], w=[dent8], scale=-1.0)
                    for kv in range(2):
                        yield
                        k.tt(ocT[:, kv * 4:(kv + 1) * 4, off:off + n], pNs[kv][0:64, 0:256].rearrange("p (m t) -> p m t", m=4),
                             dent8[:, kv * 256:(kv + 1) * 256].rearrange("p (m t) -> p m t", m=4), ALU.mult, r=[pNs[kv], dent8], w=[ocT])
                else:
                    for kv in range(2):
                        pN = PSF[3]
                        pD = PSF[4]
                        for bi, (KTk, VBk, ones_k, nk, mask) in enumerate(kblocks):
                            pS = PSF[5]
                            yield
                            k.mm(pS[0:nk, 0:4 * n], lhsT=KTk[:, kv, 0:nk], rhs=QT[:, kv * 4:(kv + 1) * 4, 0:n], start=True, stop=True,
                                 r=[KTk, QT], w=[pS])
                            Pt = Pm[bi]
                            yield
                            k.act(Pt[0:nk, :, 0:n], pS[0:nk, 0:4 * n].rearrange("p (m t) -> p m t", m=4), AF.Exp, r=[pS], w=[Pt], scale=0.125)
                            if mask is not None:
                                yield
                                k.tt(Pt[0:nk, :, 0:n], Pt[0:nk, :, 0:n], mask[0:nk, :, 0:n], ALU.mult, r=[mask], w=[Pt])
                            yield
                            k.mm(pN[0:64, 0:4 * n], lhsT=VBk[0:nk, kv * 64:(kv + 1) * 64], rhs=Pt[0:nk, :, 0:n], start=(bi == 0), stop=(bi == 2),
                                 r=[VBk, Pt], w=[pN], inc=True)
                            yield
                            k.mm(pD[0:64, 0:4 * n], lhsT=ones_k[0:nk, 0:64], rhs=Pt[0:nk, :, 0:n], start=(bi == 0), stop=(bi == 2),
                                 r=[ones_k, Pt], w=[pD], inc=True)
                        yield
                        k.tt(dent[:, :, 0:n], pD[0:64, 0:4 * n].rearrange("p (m t) -> p m t", m=4), L["esink"][:, kv * 4:(kv + 1) * 4, 0:n],
                             ALU.add, r=[pD, L["esink"]], w=[dent])
                        yield
                        k.recip(dent[:, :, 0:n], dent[:, :, 0:n], r=[], w=[dent])
                        yield
                        k.tt(ocT[:, kv * 4:(kv + 1) * 4, off:off + n], pN[0:64, 0:4 * n].rearrange("p (m t) -> p m t", m=4), dent[:, :, 0:n],
                             ALU.mult, r=[pN, dent], w=[ocT])

                    yield
            chains = [chainA(), chainB(), chainC()]
            while chains:
                for g_ in list(chains):
                    try:
                        next(g_)
                    except StopIteration:
                        chains.remove(g_)
            if n == 64:
                st["KT_free"], st["KT_old"], st["KT_mid"] = st["KT_old"], st["KT_mid"], KTc
                st["VB_free"], st["VB_old"], st["VB_mid"] = st["VB_old"], st["VB_mid"], VBc
                st["ones_old"], st["ones_mid"] = st["ones_mid"], onesb

        def load_win(l):
            wv = w_in[l].rearrange("(c p) n -> p c n", p=128)
            for pi in range(5):
                c0, c1 = pi * 512, min((pi + 1) * 512, INC)
                k.dma("pool", WIN[:, :, c0:c1], wv[:, :, c0:c1], w=[WINr[pi]])

        def load_pan(src_ap, nk, ncol, p0=0, rows=slice(0, 128), t=None):
            if t is None:
                t = next_pan()
            k.dma("pool", t[rows, p0:p0 + nk, 0:ncol], src_ap, w=[t])
            return t

        def layer(l, blocks, mix_blocks, p_src):
            L = P[l]
            load_win(l)
            for j in range(2):
                for i in range(31):
                    k.ts(dg[:, j, i, :], identf[:], L["ptmp"][:, j, i:i + 1], None, ALU.mult, None, r=[L["ptmp"], identf], w=[dg])
            for off, n, Ht in blocks:
                rmsnorm_T(Ht, n, L["g_mix"], off)
            for mb in mix_blocks:
                mb(l)
            wo = w_out[l]
            for ch in range(2):
                cs = slice(ch * 512, (ch + 1) * 512)
                t1 = load_pan(wo[0:512, cs].rearrange("(c p) n -> p c n", p=128), 4, 512)
                t2 = load_pan(wo[512:1024, cs].rearrange("(h d) n -> d h n", d=64), 8, 512, rows=slice(0, 64))
                for off, n, Ht in blocks:
                    p_ = psf()
                    lin_tok(p_, n, off, 0, 512, t1, [t1], mixT, mixT, nk=4, last=False)
                    lin_tok(p_, n, off, 0, 512, t2, [t2], ocT, ocT, nk=8, kp=64, first=False)
                    k.tt(Ht[0:n, cs], Ht[0:n, cs], p_[0:n, :], ALU.add, r=[p_], w=[Ht])
            for off, n, Ht in blocks:
                rmsnorm_T(Ht, n, L["g_ffn"], off)
            wg = w_gate[l].rearrange("(c p) n -> p c n", p=128)
            wu = w_up[l].rearrange("(c p) n -> p c n", p=128)
            for pi in range(6):
                c0, c1 = pi * 512, min((pi + 1) * 512, DFF)
                tg = load_pan(wg[:, :, c0:c1], 8, c1 - c0)
                tu = load_pan(wu[:, :, c0:c1], 8, c1 - c0)
                ntok = blocks[-1][0] + blocks[-1][1]
                for j in range((c1 - c0) // 128):
                    fc = (c0 // 128) + j
                    pg_ = psf()
                    lin_feat(pg_, ntok, 0, j * 128, tg, [tg], xT, xT)
                    pu_ = psf()
                    lin_feat(pu_, ntok, 0, j * 128, tu, [tu], xT, xT)
                    k.act(aT[:, fc, 0:ntok], pg_[:, 0:ntok], AF.Silu, r=[pg_], w=[aT])
                    k.tt(aT[:, fc, 0:ntok], pu_[:, 0:ntok], aT[:, fc, 0:ntok], ALU.mult, r=[pu_], w=[aT])
            wd = w_down[l].rearrange("(c p) n -> p c n", p=128)
            for ch in range(2):
                cs = slice(ch * 512, (ch + 1) * 512)
                accs = [psf() for _ in blocks]
                for rg, (k0, nk) in enumerate(((0, 8), (8, 8), (16, 6))):
                    t = load_pan(wd[:, k0:k0 + nk, cs], nk, 512)
                    for bi, (off, n, Ht) in enumerate(blocks):
                        for kc in range(nk):
                            k.mm(accs[bi][0:n, :], lhsT=aT[:, k0 + kc, off:off + n], rhs=t[:, kc, :],
                                 start=(k0 + kc == 0), stop=(k0 + kc == 21), r=[aT, t], w=[accs[bi]],
                                 inc=(kc == nk - 1))
                for bi, (off, n, Ht) in enumerate(blocks):
                    k.tt(Ht[0:n, cs], Ht[0:n, cs], accs[bi][0:n, :], ALU.add, r=[accs[bi]], w=[Ht])
            for off, n, Ht in blocks:
                rmsnorm_T(Ht, n, L["g_ple"], off)
                k.dma("pool", pb[0:n, :], p_src(l, off, n), w=[pb])
                pt = psb()
                for c in range(2):
                    k.tr(pt[:, c * 128:c * 128 + n], pb[0:n, c * 128:(c + 1) * 128], identb[0:n, 0:n], r=[pb, identb], w=[pt])
                for c in range(2):
                    k.cp(pT[:, c, off:off + n], pt[:, c * 128:c * 128 + n], r=[pt], w=[pT])
            wpg = w_ple_gate[l].rearrange("(c p) n -> p c n", p=128)
            wpp = w_ple_proj[l].rearrange("(c p) n -> p c n", p=128)
            for ch in range(2):
                cs = slice(ch * 512, (ch + 1) * 512)
                tg = load_pan(wpg[:, :, cs], 8, 512)
                tp = load_pan(wpp[:, :, cs], 2, 512)
                for off, n, Ht in blocks:
                    p1 = psf()
                    lin_tok(p1, n, off, 0, 512, tg, [tg], xT, xT)
                    p2 = psf()
                    lin_tok(p2, n, off, 0, 512, tp, [tp], pT, pT, nk=2)
                    k.act(gate_sb[0:n, :], p1[0:n, :], AF.Sigmoid, r=[p1], w=[gate_sb])
                    k.tt(gate_sb[0:n, :], p2[0:n, :], gate_sb[0:n, :], ALU.mult, r=[p2], w=[gate_sb])
                    k.tt(Ht[0:n, cs], Ht[0:n, cs], gate_sb[0:n, :], ALU.add, r=[gate_sb], w=[Ht])

        sst = new_state("s")
        kc_old = sb("kc_old", [64, 128], BF16)
        kc_mid = sb("kc_mid", [64, 128], BF16)

        def sample_mix_block(b):
            def run(l):
                st = sst
                st["ones_old"], st["ones_mid"] = onesb, onesb
                S = st["S"]
                k.dma("sp", S[:], state_hgrn[l, b].rearrange("h k v -> k h v"), w=[S])
                k.cp(st["Sb"][:], S[:], r=[S], w=[st["Sb"]])
                k.dma("sp", halo_tok[0:30, :], state_conv[l, b], w=[halo_tok])
                for j in range(2):
                    p_ = psf()
                    tr32(p_[:, 0:30], halo_tok[0:30, j * 128:(j + 1) * 128], 30, 128, [halo_tok], [p_])
                    k.cp(st["UFh"][:, j, :], p_[:, 0:30], r=[p_], w=[st["UFh"]], eng="act")
                k.dma("pool", kc_old[:], cache_k[l, b, 0:64, :], w=[kc_old])
                k.dma("pool", kc_mid[:], cache_k[l, b, 64:128, :], w=[kc_mid])
                k.dma("pool", st["VB_old"][:], cache_v[l, b, 0:64, :], w=[st["VB_old"]])
                k.dma("pool", st["VB_mid"][:], cache_v[l, b, 64:128, :], w=[st["VB_mid"]])
                for src_t, dst_t in ((kc_old, st["KT_old"]), (kc_mid, st["KT_mid"])):
                    pt = psb()
                    for kv in range(2):
                        k.tr(pt[0:64, kv * 64:(kv + 1) * 64], src_t[:, kv * 64:(kv + 1) * 64], identb[0:64, 0:64], r=[src_t, identb], w=[pt])
                    for kv in range(2):
                        k.cp(dst_t[:, kv, :], pt[0:64, kv * 64:(kv + 1) * 64], r=[pt], w=[dst_t])
                mixer(l, DSEQ, b * DSEQ, st, SEQ)
                k.dma("sp", o_hs[l, b].rearrange("h k v -> k h v"), S[:], r=[S])
                k.dma("sp", o_cs[l, b, 0:26, :], state_conv[l, b, 4:30, :])
                for j in range(2):
                    p_ = psf()
                    tr32(p_[0:DSEQ, 0:128], UF[:, j, 30:30 + DSEQ], 128, DSEQ, [UF], [p_])
                    k.cp(tok30[0:DSEQ, j * 128:(j + 1) * 128], p_[0:DSEQ, 0:128], r=[p_], w=[tok30], eng="act")
                k.dma("sp", o_cs[l, b, 26:30, :], tok30[0:DSEQ, :], r=[tok30])
                k.dma("sp", o_ks[l, b, 0:124, :], cache_k[l, b, 4:128, :])
                k.dma("sp", o_vs[l, b, 0:124, :], cache_v[l, b, 4:128, :])
                k.dma("sp", o_ks[l, b, 124:128, :], krf[0:DSEQ, :], r=[krf])
                k.dma("sp", o_vs[l, b, 124:128, :], vf[0:DSEQ, :], r=[vf])
            return run

        NS = SPC * DSEQ
        k.dma("sp", H[0][0:NS, :], x_sample, w=[H[0]])
        sblocks = [(0, NS, H[0])]
        for l in range(DEPTH):
            layer(l, sblocks, [sample_mix_block(b) for b in range(SPC)], lambda l_, off, n: p_sample[l_, off:off + n, :])
        k.dma("sp", y_sample, H[0][0:NS, :], r=[H[0]])
        if STAGE <= 4:
            k.final_wait()
            return nc

        pst = []
        for l in range(DEPTH):
            st = new_state("p%d" % l)
            for nm in ("S", "Sb", "UFh", "KT_old", "KT_mid", "VB_old", "VB_mid"):
                k.memset(st[nm][:], 0.0, w=[st[nm]])
            pst.append(st)
        nst = SEQ // ST
        NMB = ST // 64
        for si in range(nst):
            t0 = si * ST
            blocks = []
            for bi in range(NB):
                k.dma("sp", H[bi][:], x_prompt[t0 + bi * 128:t0 + (bi + 1) * 128, :], w=[H[bi]])
                blocks.append((bi * 128, 128, H[bi]))
            for l in range(DEPTH):
                def mk(mi):
                    def run(l_):
                        mixer(l_, 64, mi * 64, pst[l_], t0 + mi * 64)
                        if si == nst - 1 and mi >= NMB - 2:
                            half = mi - (NMB - 2)
                            k.dma("sp", o_kp[l_, half * 64:(half + 1) * 64, :], krf[0:64, :], r=[krf])
                            k.dma("sp", o_vp[l_, half * 64:(half + 1) * 64, :], vf[0:64, :], r=[vf])
                        if si == nst - 1 and mi == NMB - 1:
                            stt_ = pst[l_]
                            k.dma("sp", o_hp[l_].rearrange("h k v -> k h v"), stt_["S"][:], r=[stt_["S"]])
                            for j in range(2):
                                p_ = psf()
                                tr32(p_[0:30, 0:128], UF[:, j, 64:94], 128, 30, [UF], [p_])
                                k.cp(tok30[0:30, j * 128:(j + 1) * 128], p_[0:30, 0:128], r=[p_], w=[tok30], eng="act")
                            k.dma("sp", o_cp[l_], tok30[0:30, :], r=[tok30])
                    return run
                layer(l, blocks, [mk(mi) for mi in range(NMB)], lambda l_, off, n: p_prompt[l_, t0 + off:t0 + off + n, :])
            for bi in range(NB):
                k.dma("sp", y_prompt[t0 + bi * 128:t0 + (bi + 1) * 128, :], H[bi][:], r=[H[bi]])
        k.final_wait()
    return nc


_CONSTS = None


def _consts():
    global _CONSTS
    if _CONSTS is None:
        i = np.arange(128)
        same = (i[:, None] // 64) == (i[None, :] // 64)
        U = ((i[:, None] <= i[None, :]) & same).astype(np.float32)
        W = ((i[:, None] > i[None, :]) & same).astype(np.float32)
        mprev = (i[:, None] >= i[None, :]).astype(np.float32)
        mdiag = (i[:, None] <= i[None, :]).astype(np.float32)
        bones = (same.astype(np.float32) / 64.0).astype(np.float32)
        half = 32
        inv = (10000.0 ** (-np.arange(half, dtype=np.float32) / half)).astype(np.float32)
        pos = np.concatenate([np.arange(SEQ, dtype=np.float32), PAST + np.arange(DSEQ, dtype=np.float32)])
        ang = (pos[:, None] * inv[None, :]).astype(np.float32)
        cos = np.tile(np.cos(ang).astype(np.float32), (1, 8))
        sin = np.tile(np.sin(ang).astype(np.float32), (1, 8))
        _CONSTS = dict(c_ident=np.eye(128, dtype=np.float32), c_U=U, c_W=W, c_mprev=mprev, c_mdiag=mdiag,
                       c_bones=bones, c_cos=np.ascontiguousarray(cos), c_sin=np.ascontiguousarray(sin))
    return _CONSTS


def kernel(x_prompt, x_sample, state_hgrn, state_conv, cache_swa_k, cache_swa_v, p_prompt, p_sample,
           a_lower, w_in, a_onorm, conv_w, conv_b, conv_ln_g, conv_ln_b, q_norm, k_norm, sinks, w_out,
           norm_mix, norm_ffn, w_gate, w_up, w_down, ple_norm, w_ple_gate, w_ple_proj):
    f = lambda a: np.ascontiguousarray(np.asarray(a, dtype=np.float32))
    shared = dict(x_prompt=f(x_prompt).reshape(SEQ, D), p_prompt=f(p_prompt).reshape(DEPTH, SEQ, 256),
                  a_lower=f(a_lower), w_in=f(w_in), a_onorm=f(a_onorm), conv_w=f(conv_w), conv_b=f(conv_b),
                  conv_ln_g=f(conv_ln_g), conv_ln_b=f(conv_ln_b), q_norm=f(q_norm), k_norm=f(k_norm), sinks=f(sinks),
                  w_out=f(w_out), norm_mix=f(norm_mix), norm_ffn=f(norm_ffn), w_gate=f(w_gate), w_up=f(w_up),
                  w_down=f(w_down), ple_norm=f(ple_norm), w_ple_gate=f(w_ple_gate), w_ple_proj=f(w_ple_proj))
    shared.update(_consts())
    xs, sh, sc = f(x_sample), f(state_hgrn), f(state_conv)
    ck, cv, ps_ = f(cache_swa_k), f(cache_swa_v), f(p_sample)
    in_maps = []
    for c in range(NCORE):
        b = slice(c * SPC, (c + 1) * SPC)
        m = dict(shared)
        m.update(x_sample=np.ascontiguousarray(xs[b]).reshape(SPC * DSEQ, D),
                 state_hgrn=np.ascontiguousarray(sh[:, b]),
                 state_conv=np.ascontiguousarray(sc[:, b]),
                 cache_k=np.ascontiguousarray(ck[:, b]).reshape(DEPTH, SPC, 128, 128),
                 cache_v=np.ascontiguousarray(cv[:, b]).reshape(DEPTH, SPC, 128, 128),
                 p_sample=np.ascontiguousarray(ps_[:, b]).reshape(DEPTH, SPC * DSEQ, 256))
        in_maps.append(m)
    nc = build_program()
    res = run_bass_kernel_spmd(nc, in_maps, core_ids=list(range(NCORE)))
    R = res.results
    cat = lambda name, ax: np.concatenate([R[c][name] for c in range(NCORE)], axis=ax)
    y_p = R[0]["y_prompt"].reshape(1, SEQ, D)
    y_s = cat("y_sample", 0).reshape(NSEQ, DSEQ, D)
    return (y_p.astype(np.float32), y_s.astype(np.float32),
            R[0]["o_hp"].reshape(DEPTH, 1, 4, 64, 64), R[0]["o_cp"].reshape(DEPTH, 1, 30, 256),
            R[0]["o_kp"].reshape(DEPTH, 1, 128, 2, 64), R[0]["o_vp"].reshape(DEPTH, 1, 128, 2, 64),
            cat("o_hs", 1), cat("o_cs", 1),
            cat("o_ks", 1).reshape(DEPTH, NSEQ, 128, 2, 64), cat("o_vs", 1).reshape(DEPTH, NSEQ, 128, 2, 64))
```

```python
import numpy as np
import ml_dtypes
from contextlib import ExitStack
import concourse.bass as bass
import concourse.mybir as mybir
from concourse.bass_utils import run_bass_kernel_spmd

F32 = mybir.dt.float32
BF16 = mybir.dt.bfloat16
AF = mybir.ActivationFunctionType
ALU = mybir.AluOpType
AX = mybir.AxisListType

NCORE = 8
D = 1024
SEQ = 16384
DEPTH = 2
NSEQ = 128
SPC = NSEQ // NCORE
DSEQ = 4
PAST = 16384
DFF = 2816
INC = 2304
EPS = 1e-6
ST = 512
SAME_SYNC = True
STAGE = 99
MSTAGE = 99
MBSEL = [0]
HSEL = [0, 1, 2, 3]


class Tl:
    def __init__(self, t):
        self.t = t
        self.w = None
        self.r = {}

    def __getitem__(self, idx):
        return self.t[idx]


class TlView(Tl):
    def __init__(self, parent, ap):
        self.t = ap
        self.p = parent

    w = property(lambda self: self.p.w, lambda self, v: setattr(self.p, "w", v))
    r = property(lambda self: self.p.r, lambda self, v: setattr(self.p, "r", v))


class KB:
    def __init__(self, nc, es):
        self.nc = nc
        self.es = es
        self.E = {}
        for name, h in (("pe", nc.tensor), ("act", nc.scalar), ("dve", nc.vector), ("pool", nc.gpsimd), ("sp", nc.sync)):
            self.E[name] = dict(h=h, sem=es.enter_context(nc.semaphore("sem_" + name)), count=0, waited={})
        self.dsem = {q: [[es.enter_context(nc.semaphore("dsem%s%d" % (q, i))), 0] for i in range(24)] for q in ("sp", "pool")}
        self.dnext = {"sp": 0, "pool": 0}
        self.n_inst = 0

    def sb(self, name, shape, dt):
        return Tl(self.es.enter_context(self.nc.sbuf_tensor(name, shape, dt)))

    def ps(self, name, shape, dt):
        return Tl(self.es.enter_context(self.nc.psum_tensor(name, shape, dt)))

    def _wait(self, eng, r, w):
        E = self.E[eng]
        deps = {}

        def add(ev):
            if ev is None:
                return
            s, v = ev
            k = id(s)
            if k not in deps or deps[k][1] < v:
                deps[k] = (s, v)
        for t in r:
            add(t.w)
        for t in w:
            add(t.w)
            for e in t.r.values():
                add(e)
        for k, (s, v) in deps.items():
            if s is E["sem"] and (eng == "pe" or not SAME_SYNC):
                continue
            if E["waited"].get(k, 0) >= v:
                continue
            E["h"].wait_ge(s, v)
            E["waited"][k] = v

    def _note(self, ev, r, w):
        for t in r:
            t.r[id(ev[0])] = ev
        for t in w:
            t.w = ev
            t.r = {}

    def op(self, eng, fn, r=(), w=(), inc=True):
        E = self.E[eng]
        self._wait(eng, r, w)
        inst = fn(E["h"])
        self.n_inst += 1
        if inc:
            E["count"] += 1
            inst.then_inc(E["sem"], 1)
            ev = (E["sem"], E["count"])
        else:
            ev = (E["sem"], E["count"] + 1)
        self._note(ev, r, w)

    def dma(self, eng, out, in_, r=(), w=(), slow=False):
        E = self.E[eng]
        self._wait(eng, r, w)
        slot = self.dsem[eng][self.dnext[eng]]
        self.dnext[eng] = (self.dnext[eng] + 1) % len(self.dsem[eng])
        s, v = slot
        if v > 0 and E["waited"].get(id(s), 0) < v:
            E["h"].wait_ge(s, v)
            E["waited"][id(s)] = v
        if slow:
            E["h"].dma_start(out=out, in_=in_, allow_slow_non_contiguous=True).then_inc(s, 16)
        else:
            E["h"].dma_start(out=out, in_=in_).then_inc(s, 16)
        slot[1] = v + 16
        self.n_inst += 1
        self._note((s, v + 16), r, w)

    def final_wait(self):
        E = self.E["sp"]
        for q in self.dsem:
            for s, v in self.dsem[q]:
                if v > 0:
                    E["h"].wait_ge(s, v)
        for name in ("pe", "act", "dve", "pool"):
            e = self.E[name]
            if e["count"] > 0:
                E["h"].wait_ge(e["sem"], e["count"])

    def mm(self, out, lhsT, rhs, start, stop, r, w, inc=None):
        if inc is None:
            inc = stop
        self.op("pe", lambda e: e.matmul(out, lhsT=lhsT, rhs=rhs, start=start, stop=stop), r=r, w=w, inc=inc)

    def tr(self, out, in_, ident, r, w):
        self.op("pe", lambda e: e.transpose(out, in_, ident), r=r, w=w)

    def act(self, out, in_, func, r, w, bias=None, scale=None, accum=None):
        kw = {}
        if bias is not None:
            kw["bias"] = bias
        if scale is not None:
            kw["scale"] = scale
        if accum is not None:
            kw["accum_out"] = accum
        self.op("act", lambda e: e.activation(out=out, in_=in_, func=func, **kw), r=r, w=w)

    def tt(self, out, in0, in1, op, r, w, eng="dve"):
        self.op(eng, lambda e: e.tensor_tensor(out=out, in0=in0, in1=in1, op=op), r=r, w=w)

    def ts(self, out, in0, s1, s2, op0, op1, r, w, eng="dve"):
        if op1 is None:
            self.op(eng, lambda e: e.tensor_scalar(out=out, in0=in0, scalar1=s1, scalar2=None, op0=op0), r=r, w=w)
        else:
            self.op(eng, lambda e: e.tensor_scalar(out=out, in0=in0, scalar1=s1, scalar2=s2, op0=op0, op1=op1), r=r, w=w)

    def stt(self, out, in0, scalar, in1, op0, op1, r, w):
        self.op("dve", lambda e: e.scalar_tensor_tensor(out=out, in0=in0, scalar=scalar, in1=in1, op0=op0, op1=op1), r=r, w=w)

    def cp(self, out, in_, r, w, eng="dve"):
        if eng == "act":
            self.act(out, in_, AF.Copy, r, w)
        else:
            self.op(eng, lambda e: e.tensor_copy(out=out, in_=in_), r=r, w=w)

    def recip(self, out, in_, r, w):
        self.op("dve", lambda e: e.reciprocal(out=out, in_=in_), r=r, w=w)

    def red(self, out, in_, r, w):
        self.op("dve", lambda e: e.tensor_reduce(out=out, in_=in_, axis=AX.X, op=ALU.add), r=r, w=w)

    def memset(self, ap, val, w, eng="dve"):
        self.op(eng, lambda e: e.memset(ap, val), r=(), w=w)


def build_program():
    nc = bass.Bass("TRN2", target_bir_lowering=False)

    def din(name, shape):
        return nc.dram_tensor(name, list(shape), F32, kind="ExternalInput").ap()

    def dout(name, shape):
        return nc.dram_tensor(name, list(shape), F32, kind="ExternalOutput").ap()

    x_prompt = din("x_prompt", (SEQ, D))
    x_sample = din("x_sample", (SPC * DSEQ, D))
    state_hgrn = din("state_hgrn", (DEPTH, SPC, 4, 64, 64))
    state_conv = din("state_conv", (DEPTH, SPC, 30, 256))
    cache_k = din("cache_k", (DEPTH, SPC, 128, 128))
    cache_v = din("cache_v", (DEPTH, SPC, 128, 128))
    p_prompt = din("p_prompt", (DEPTH, SEQ, 256))
    p_sample = din("p_sample", (DEPTH, SPC * DSEQ, 256))
    a_lower = din("a_lower", (DEPTH, 256))
    w_in = din("w_in", (DEPTH, D, INC))
    a_onorm = din("a_onorm", (DEPTH, 64))
    conv_w = din("conv_w", (DEPTH, 31, 256))
    conv_b = din("conv_b", (DEPTH, 256))
    conv_ln_g = din("conv_ln_g", (DEPTH, 256))
    conv_ln_b = din("conv_ln_b", (DEPTH, 256))
    q_norm = din("q_norm", (DEPTH, 64))
    k_norm = din("k_norm", (DEPTH, 64))
    sinks = din("sinks", (DEPTH, 8))
    w_out = din("w_out", (DEPTH, D, D))
    norm_mix = din("norm_mix", (DEPTH, D))
    norm_ffn = din("norm_ffn", (DEPTH, D))
    w_gate = din("w_gate", (DEPTH, D, DFF))
    w_up = din("w_up", (DEPTH, D, DFF))
    w_down = din("w_down", (DEPTH, DFF, D))
    ple_norm = din("ple_norm", (DEPTH, D))
    w_ple_gate = din("w_ple_gate", (DEPTH, D, D))
    w_ple_proj = din("w_ple_proj", (DEPTH, 256, D))
    c_ident = din("c_ident", (128, 128))
    c_U = din("c_U", (128, 128))
    c_W = din("c_W", (128, 128))
    c_mprev = din("c_mprev", (128, 128))
    c_mdiag = din("c_mdiag", (128, 128))
    c_bones = din("c_bones", (128, 128))
    c_cos = din("c_cos", (SEQ + DSEQ, 256))
    c_sin = din("c_sin", (SEQ + DSEQ, 256))

    y_prompt = dout("y_prompt", (SEQ, D))
    y_sample = dout("y_sample", (SPC * DSEQ, D))
    o_hp = dout("o_hp", (DEPTH, 4, 64, 64))
    o_cp = dout("o_cp", (DEPTH, 30, 256))
    o_kp = dout("o_kp", (DEPTH, 128, 128))
    o_vp = dout("o_vp", (DEPTH, 128, 128))
    o_hs = dout("o_hs", (DEPTH, SPC, 4, 64, 64))
    o_cs = dout("o_cs", (DEPTH, SPC, 30, 256))
    o_ks = dout("o_ks", (DEPTH, SPC, 128, 128))
    o_vs = dout("o_vs", (DEPTH, SPC, 128, 128))

    es = ExitStack()
    with es:
        k = KB(nc, es)
        sb, ps = k.sb, k.ps

        identf = sb("identf", [128, 128], F32)
        identb = sb("identb", [128, 128], BF16)
        Uf = sb("Uf", [128, 128], F32)
        mprev4 = sb("mprev4", [64, 4, 64], BF16)
        mdiag4 = sb("mdiag4", [64, 4, 64], BF16)
        onesb = sb("onesb", [128, 128], BF16)
        zerosb = sb("zerosb", [128, 128], BF16)
        eps_t = sb("eps_t", [128, 1], F32)
        gate_sb = sb("gate_sb", [128, 512], F32)
        for t, src in ((identf, c_ident), (Uf, c_U)):
            k.dma("sp", t[:], src, w=[t])
        for i_, src in enumerate((c_W, c_bones, c_mprev, c_mdiag)):
            k.dma("sp", gate_sb[:, i_ * 128:(i_ + 1) * 128], src, w=[gate_sb])
        k.cp(identb[:], identf[:], r=[identf], w=[identb])
        Ub = sb("Ub", [128, 128], BF16)
        Wb = sb("Wb", [128, 128], BF16)
        bonesb = sb("bonesb", [128, 128], BF16)
        k.cp(Ub[:], Uf[:], r=[Uf], w=[Ub])
        k.cp(Wb[:], gate_sb[:, 0:128], r=[gate_sb], w=[Wb])
        k.cp(bonesb[:], gate_sb[:, 128:256], r=[gate_sb], w=[bonesb])
        for g in range(4):
            k.cp(mprev4[:, g, :], gate_sb[0:64, 256:320], r=[gate_sb], w=[mprev4])
            k.cp(mdiag4[:, g, :], gate_sb[0:64, 384:448], r=[gate_sb], w=[mdiag4])
        k.memset(onesb[:], 1.0, w=[onesb])
        k.memset(zerosb[:], 0.0, w=[zerosb])
        k.memset(eps_t[:], EPS, w=[eps_t])
        one_t = sb("one_t", [128, 1], F32)
        k.memset(one_t[:], 1.0, w=[one_t])

        P = []
        dg = sb("dg", [128, 2, 31, 128], BF16)
        for l in range(DEPTH):
            L = {}
            for nm, src in (("g_mix", norm_mix), ("g_ffn", norm_ffn), ("g_ple", ple_norm)):
                t = sb("%s%d" % (nm, l), [128, 8], F32)
                k.dma("sp", t[:], src[l].rearrange("(c p) -> p c", p=128), w=[t], slow=True)
                L[nm] = t
            aon = sb("aon%d" % l, [64, 256], F32)
            for h in range(4):
                k.dma("sp", aon[:, h * 64:(h + 1) * 64], a_onorm[l].partition_broadcast(64), w=[aon])
            L["aon"] = aon
            qn = sb("qnr%d" % l, [64, 64], F32)
            kn = sb("knr%d" % l, [64, 64], F32)
            k.dma("sp", qn[:], q_norm[l].partition_broadcast(64), w=[qn])
            k.dma("sp", kn[:], k_norm[l].partition_broadcast(64), w=[kn])
            L["qn"], L["kn"] = qn, kn
            lbrow = sb("lbrow%d" % l, [64, 256], F32)
            omlrow = sb("omlrow%d" % l, [64, 256], F32)
            lbp = sb("lbp%d" % l, [64, 4], F32)
            omlp = sb("omlp%d" % l, [64, 4], F32)
            nomlp = sb("nomlp%d" % l, [64, 4], F32)
            if l == 0:
                k.memset(lbrow[:], 0.0, w=[lbrow])
                k.memset(lbp[:], 0.0, w=[lbp])
            else:
                a0 = sb("a0row", [64, 256], F32)
                a0p = sb("a0p", [64, 4], F32)
                k.dma("sp", lbrow[:], a_lower[1].partition_broadcast(64), w=[lbrow])
                k.dma("sp", a0[:], a_lower[0].partition_broadcast(64), w=[a0])
                k.dma("sp", lbp[:], a_lower[1].rearrange("(h p) -> p h", p=64), w=[lbp], slow=True)
                k.dma("sp", a0p[:], a_lower[0].rearrange("(h p) -> p h", p=64), w=[a0p], slow=True)
                k.tt(lbrow[:], lbrow[:], a0[:], ALU.subtract, r=[a0], w=[lbrow])
                k.tt(lbp[:], lbp[:], a0p[:], ALU.subtract, r=[a0p], w=[lbp])
                k.act(lbrow[:], lbrow[:], AF.Sigmoid, r=[], w=[lbrow])
                k.act(lbp[:], lbp[:], AF.Sigmoid, r=[], w=[lbp])
            k.ts(omlrow[:], lbrow[:], -1.0, 1.0, ALU.mult, ALU.add, r=[lbrow], w=[omlrow])
            k.ts(omlp[:], lbp[:], -1.0, 1.0, ALU.mult, ALU.add, r=[lbp], w=[omlp])
            k.ts(nomlp[:], omlp[:], -1.0, None, ALU.mult, None, r=[omlp], w=[nomlp])
            L.update(lbrow=lbrow, omlrow=omlrow, lbp=lbp, omlp=omlp, nomlp=nomlp)
            ptmp = sb("ptmp%d" % l, [128, 2, 31], F32)
            for j in range(2):
                k.dma("sp", ptmp[:, j, :], conv_w[l][:, j * 128:(j + 1) * 128].rearrange("i p -> p i"), w=[ptmp], slow=True)
            cb = sb("cb%d" % l, [128, 2], F32)
            cg = sb("cg%d" % l, [128, 2], F32)
            cbb = sb("cbb%d" % l, [128, 2], F32)
            k.dma("sp", cb[:], conv_b[l].rearrange("(j p) -> p j", p=128), w=[cb], slow=True)
            k.dma("sp", cg[:], conv_ln_g[l].rearrange("(j p) -> p j", p=128), w=[cg], slow=True)
            k.dma("sp", cbb[:], conv_ln_b[l].rearrange("(j p) -> p j", p=128), w=[cbb], slow=True)
            L.update(dg=dg, cb=cb, cg=cg, cbb=cbb, ptmp=ptmp)
            sk = sb("sk%d" % l, [64, 8], F32)
            esink = sb("esink%d" % l, [64, 8, 64], F32)
            k.dma("sp", sk[:], sinks[l].partition_broadcast(64), w=[sk])
            k.act(sk[:], sk[:], AF.Exp, r=[], w=[sk])
            for h in range(8):
                k.ts(esink[:, h, :], onesb[0:64, 0:64], sk[:, h:h + 1], None, ALU.mult, None, r=[sk, onesb], w=[esink])
            L["esink"] = esink
            P.append(L)

        NB = ST // 128
        H = [sb("H%d" % i, [128, D], F32) for i in range(NB)]
        xT = sb("xT", [128, 8, ST], BF16)
        mixT = sb("mixT", [128, 4, ST], BF16)
        ocT = sb("ocT", [64, 8, ST], BF16)
        aT = sb("aT", [128, 22, ST], BF16)
        pT = sb("pT", [128, 2, ST], BF16)
        WIN = sb("WIN", [128, 8, INC], BF16)
        WINr = [Tl(None) for _ in range(5)]
        NPAN = 3
        PAN = [sb("pan%d" % i, [128, 8, 512], BF16) for i in range(NPAN)]
        pan_i = [0]
        PSF = [ps("psf%d" % i, [128, 512], F32) for i in range(6)]
        PSB = [ps("psb%d" % i, [128, 1024], BF16) for i in range(2)]
        psf_i = [0]
        psb_i = [0]

        def psf():
            t = PSF[psf_i[0] % 6]
            psf_i[0] += 1
            return t

        def psb():
            t = PSB[psb_i[0] % 2]
            psb_i[0] += 1
            return t

        cA = [0]
        cC = [0]

        def psfA():
            cA[0] += 1
            return PSF[cA[0] % 2]

        def psfC():
            cC[0] += 1
            return PSF[3 + cC[0] % 3]

        def hiloc(src_ap, f, r):
            k.cp(hi_c[:, 0:f], src_ap, r=r, w=[hi_c])
            k.tt(lo_c[:, 0:f], src_ap, hi_c[:, 0:f], ALU.subtract, r=r + [hi_c], w=[lo_c])

        def next_pan():
            t = PAN[pan_i[0] % NPAN]
            pan_i[0] += 1
            return t

        xn = sb("xn", [128, D], BF16)
        ssq = sb("ssq", [128, 8], F32)
        rst = sb("rst", [128, 8], F32)
        MB = 64
        qT = sb("qT", [64, 4, MB], F32)
        kinT = sb("kinT", [64, 4, MB], F32)
        sgT = sb("sgT", [128, MB], F32)
        logf = sb("logf", [64, 256], F32)
        kin = sb("kin", [64, 256], F32)
        ftm = sb("ftm", [64, 256], F32)
        va = sb("va", [64, 256], BF16)
        sg = sb("sg", [64, 256], F32)
        GT = sb("GT", [64, 4, MB], F32)
        ER = sb("ER", [64, 256], F32)
        kd = sb("kd", [64, 256], BF16)
        gm = sb("gm", [64, 4], F32)
        ngm = sb("ngm", [64, 4], F32)
        E1 = sb("E1", [64, 4, MB], F32)
        E2 = sb("E2", [64, 4, MB], F32)
        E3 = sb("E3", [64, 4, MB], F32)
        qeT = sb("qeT", [64, 4, MB], BF16)
        keT = sb("keT", [64, 4, MB], BF16)
        qgT = sb("qgT", [64, 4, MB], BF16)
        attT = sb("attT", [64, 4, MB], BF16)
        oaf = sb("oaf", [64, 256], BF16)
        UF = sb("UF", [128, 2, 32 + MB], F32)
        UB = sb("UB", [128, 2, 32 + MB], BF16)
        UBo = sb("UBo", [128, 2, 32 + MB], BF16)
        yb = sb("yb", [128, MB], F32)
        ysq = sb("ysq", [128, MB], F32)
        cmean = sb("cmean", [128, MB], F32)
        cvar = sb("cvar", [128, MB], F32)
        zq = sb("zq", [64, 512], F32)
        zkv = sb("zkv", [64, 256], F32)
        qnf = sb("qnf", [64, 512], F32)
        qr = sb("qr", [64, 512], BF16)
        rt = [sb("rt%d" % i, [64, 8, 32], F32) for i in range(4)]
        QT = sb("QT", [64, 8, MB], BF16)
        knf = sb("knf", [64, 128], F32)
        krf = sb("krf", [64, 128], F32)
        krb = sb("krb", [64, 128], BF16)
        vf = sb("vf", [64, 128], F32)
        Pm8 = [sb("Pm%d" % i, [64, 512], BF16) for i in range(3)]
        Pm = [TlView(t8, t8.t[:, 0:256].rearrange("p (m t) -> p m t", m=4)) for t8 in Pm8]
        dent8 = sb("dent8", [64, 512], F32)
        dent = TlView(dent8, dent8.t[:, 0:256].rearrange("p (m t) -> p m t", m=4))
        cosT = sb("cosT", [64, 256], F32)
        sinT = sb("sinT", [64, 256], F32)
        pb = sb("pb", [128, 256], BF16)
        tok30 = sb("tok30", [32, 256], F32)
        halo_tok = tok30
        osq = sb("osq", [64, 256], F32)
        ssqA = sb("ssqA", [64, 4], F32)
        rstA = sb("rstA", [64, 4], F32)
        hi_c = sb("hi_c", [128, MB], BF16)
        lo_c = sb("lo_c", [128, MB], BF16)
        hi_t = sb("hi_t", [128, 256], BF16)
        lo_t = sb("lo_t", [128, 256], BF16)

        def new_state(tag):
            st = dict(S=sb("S" + tag, [64, 4, 64], F32), Sb=sb("Sb" + tag, [64, 4, 64], BF16),
                      UFh=sb("UFh" + tag, [128, 2, 30], F32))
            kts = [sb("KT%s_%d" % (tag, i), [64, 2, 64], BF16) for i in range(3)]
            vbs = [sb("VB%s_%d" % (tag, i), [64, 128], BF16) for i in range(3)]
            st.update(KT_old=kts[0], KT_mid=kts[1], KT_free=kts[2], VB_old=vbs[0], VB_mid=vbs[1], VB_free=vbs[2],
                      ones_old=zerosb, ones_mid=zerosb)
            return st

        def sigmoid_el(out, in_, p0, p1, r, w):
            k.act(out, in_, AF.Exp, r=r, w=w, scale=-1.0)
            k.act(out, out, AF.Ln, r=[one_t], w=w, bias=one_t[p0:p1, :], scale=1.0)
            k.act(out, out, AF.Exp, r=[], w=w, scale=-1.0)

        def hilo(src_ap, p, f, r):
            k.cp(hi_t[0:p, 0:f], src_ap, r=r, w=[hi_t])
            k.tt(lo_t[0:p, 0:f], src_ap, hi_t[0:p, 0:f], ALU.subtract, r=r + [hi_t], w=[lo_t])

        def tr32(out_ps_ap, src_ap, p, f, r, w):
            hilo(src_ap, p, f, r)
            k.mm(out_ps_ap, lhsT=hi_t[0:p, 0:f], rhs=identb[0:p, 0:p], start=True, stop=False, r=[hi_t, identb], w=w)
            k.mm(out_ps_ap, lhsT=lo_t[0:p, 0:f], rhs=identb[0:p, 0:p], start=False, stop=True, r=[lo_t, identb], w=w)

        def rmsnorm_T(h_t, n, grow, off):
            k.act(xn[0:n, :], h_t[0:n, :], AF.Square, r=[h_t], w=[xn, ssq], accum=ssq[0:n, 0:1])
            k.act(rst[0:n, 0:1], ssq[0:n, 0:1], AF.Ln, r=[ssq, eps_t], w=[rst], bias=eps_t[0:n, :], scale=1.0 / D)
            k.act(rst[0:n, 0:1], rst[0:n, 0:1], AF.Exp, r=[], w=[rst], scale=-0.5)
            k.ts(xn[0:n, :], h_t[0:n, :], rst[0:n, 0:1], None, ALU.mult, None, r=[h_t, rst], w=[xn])
            pt = psb()
            for c in range(8):
                k.tr(pt[:, c * 128:c * 128 + n], xn[0:n, c * 128:(c + 1) * 128], identb[0:n, 0:n], r=[xn, identb], w=[pt])
            for c in range(8):
                k.ts(xT[:, c, off:off + n], pt[:, c * 128:c * 128 + n], grow[:, c:c + 1], None, ALU.mult, None, r=[pt, grow], w=[xT])

        def lin_tok(out_ps, n, off, c0, c1, wt, wr, act_t, act_r, nk=8, kp=128, first=True, last=True):
            for kc in range(nk):
                k.mm(out_ps[0:n, 0:c1 - c0], lhsT=act_t[0:kp, kc, off:off + n], rhs=wt[0:kp, kc, c0:c1],
                     start=(first and kc == 0), stop=(last and kc == nk - 1), r=[act_r] + wr, w=[out_ps],
                     inc=(kc == nk - 1))

        def lin_feat(out_ps, n, off, c0, wt, wr, act_t, act_r, width=128):
            for kc in range(8):
                k.mm(out_ps[0:width, 0:n], lhsT=wt[:, kc, c0:c0 + width], rhs=act_t[:, kc, off:off + n],
                     start=(kc == 0), stop=(kc == 7), r=[act_r] + wr, w=[out_ps])

        def mixer(l, n, off, st, pos0):
            L = P[l]
            S, Sb = st["S"], st["Sb"]
            for h in range(4):
                p_ = psf()
                lin_feat(p_, n, off, h * 64, WIN, [WINr[0]], xT, xT, width=64)
                k.cp(qT[:, h, 0:n], p_[0:64, 0:n], r=[p_], w=[qT], eng="act")
            for h in range(4):
                p_ = psf()
                lin_feat(p_, n, off, 256 + h * 64, WIN, [WINr[0]], xT, xT, width=64)
                k.act(sgT[0:64, 0:n], p_[0:64, 0:n], AF.Sigmoid, r=[p_], w=[sgT])
                k.ts(kinT[:, h, 0:n], sgT[0:64, 0:n], L["nomlp"][:, h:h + 1], L["omlp"][:, h:h + 1], ALU.mult, ALU.add,
                     r=[sgT, L["nomlp"], L["omlp"]], w=[kinT])
            p_ = psf()
            lin_tok(p_, n, off, 256, 512, WIN, [WINr[0]], xT, xT)
            k.act(ftm[0:n, :], p_[0:n, 0:256], AF.Sigmoid, r=[p_], w=[ftm])
            k.tt(ftm[0:n, :], ftm[0:n, :], L["omlrow"][0:n, :], ALU.mult, r=[L["omlrow"]], w=[ftm])
            k.tt(ftm[0:n, :], ftm[0:n, :], L["lbrow"][0:n, :], ALU.add, r=[L["lbrow"]], w=[ftm])
            k.act(logf[0:n, :], ftm[0:n, :], AF.Ln, r=[ftm], w=[logf])
            k.ts(kin[0:n, :], ftm[0:n, :], -1.0, 1.0, ALU.mult, ALU.add, r=[ftm], w=[kin])
            p_ = psf()
            lin_tok(p_, n, off, 512, 1024, WIN, [WINr[1]], xT, xT)
            k.cp(va[0:n, :], p_[0:n, 0:256], r=[p_], w=[va])
            k.act(sg[0:n, :], p_[0:n, 256:512], AF.Sigmoid, r=[p_], w=[sg])
            k.tt(sg[0:n, :], sg[0:n, :], p_[0:n, 256:512], ALU.mult, r=[p_], w=[sg])
            k.tt(sg[0:n, :], sg[0:n, :], L["aon"][0:n, :], ALU.mult, r=[L["aon"]], w=[sg])
            for j in range(2):
                pu = psf()
                lin_feat(pu, n, off, 1024 + j * 128, WIN, [WINr[2]], xT, xT)
                pg = psf()
                lin_feat(pg, n, off, 1280 + j * 128, WIN, [WINr[2]], xT, xT)
                k.act(sgT[:, 0:n], pg[:, 0:n], AF.Sigmoid, r=[pg], w=[sgT])
                k.tt(UF[:, j, 30:30 + n], pu[:, 0:n], sgT[:, 0:n], ALU.mult, r=[pu, sgT], w=[UF])
            k.cp(UF[:, :, 0:30], st["UFh"][:], r=[st["UFh"]], w=[UF])
            k.cp(UB[:, :, 0:30 + n], UF[:, :, 0:30 + n], r=[UF], w=[UB])
            k.cp(UBo[:, :, 0:29 + n], UF[:, :, 1:30 + n], r=[UF], w=[UBo])
            p_ = psf()
            lin_tok(p_, n, off, 1536, 2048, WIN, [WINr[3]], xT, xT)
            k.cp(zq[0:n, :], p_[0:n, :], r=[p_], w=[zq], eng="act")
            p_ = psf()
            lin_tok(p_, n, off, 2048, 2304, WIN, [WINr[4]], xT, xT)
            k.cp(zkv[0:n, :], p_[0:n, 0:256], r=[p_], w=[zkv], eng="act")

            KTc, VBc = st["KT_free"], st["VB_free"]

            def chainA():
                yield
                hilo(logf[0:n, :], n, 256, [logf])
                for h in range(4):
                    p_ = psfA()
                    yield
                    k.mm(p_[0:64, 0:n], lhsT=hi_t[0:n, h * 64:(h + 1) * 64], rhs=Ub[0:n, 0:n], start=True, stop=False,
                         r=[hi_t, Ub], w=[p_])
                    yield
                    k.mm(p_[0:64, 0:n], lhsT=lo_t[0:n, h * 64:(h + 1) * 64], rhs=Ub[0:n, 0:n], start=False, stop=True,
                         r=[lo_t, Ub], w=[p_])
                    yield
                    k.cp(GT[:, h, 0:n], p_[0:64, 0:n], r=[p_], w=[GT], eng="act")
                p_ = psfA()
                yield
                k.mm(p_[0:n, 0:256], lhsT=Wb[0:n, 0:n], rhs=hi_t[0:n, :], start=True, stop=False, r=[hi_t, Wb], w=[p_])
                yield
                k.mm(p_[0:n, 0:256], lhsT=Wb[0:n, 0:n], rhs=lo_t[0:n, :], start=False, stop=True, r=[lo_t, Wb], w=[p_])
                yield
                k.act(ER[0:n, :], p_[0:n, 0:256], AF.Exp, r=[p_], w=[ER])
                yield
                k.tt(kd[0:n, :], kin[0:n, :], ER[0:n, :], ALU.mult, r=[kin, ER], w=[kd])
                rc = max(n // 2 - 1, 0)
                yield
                k.tt(E1[:, :, 0:n], GT[:, :, 0:n], GT[:, :, rc:rc + 1].to_broadcast([64, 4, n]), ALU.subtract, r=[GT], w=[E1])
                yield
                k.act(E2[:, :, 0:n], E1[:, :, 0:n], AF.Exp, r=[E1], w=[E2], scale=-1.0)
                yield
                k.act(E1[:, :, 0:n], E1[:, :, 0:n], AF.Exp, r=[], w=[E1])
                yield
                k.act(E3[:, :, 0:n], GT[:, :, 0:n], AF.Exp, r=[GT], w=[E3])
                yield
                k.tt(qeT[:, :, 0:n], qT[:, :, 0:n], E1[:, :, 0:n], ALU.mult, r=[qT, E1], w=[qeT])
                yield
                k.tt(keT[:, :, 0:n], kinT[:, :, 0:n], E2[:, :, 0:n], ALU.mult, r=[kinT, E2], w=[keT])
                yield
                k.tt(qgT[:, :, 0:n], qT[:, :, 0:n], E3[:, :, 0:n], ALU.mult, r=[qT, E3], w=[qgT])
                pA = psfA()
                for h in range(4):
                    yield
                    k.mm(pA[0:n, h * 64:h * 64 + n], lhsT=keT[:, h, 0:n], rhs=qeT[:, h, 0:n], start=True, stop=True,
                         r=[keT, qeT], w=[pA])
                yield
                k.tt(attT[0:n, :, 0:n], pA[0:n, 0:256].rearrange("p (h t) -> p h t", h=4)[:, :, 0:n],
                     Uf[0:n, 0:n].unsqueeze(1).to_broadcast([n, 4, n]), ALU.mult, r=[pA, Uf], w=[attT])
                pO = psfA()
                for h in range(4):
                    yield
                    k.mm(pO[0:n, h * 64:(h + 1) * 64], lhsT=attT[0:n, h, 0:n], rhs=va[0:n, h * 64:(h + 1) * 64],
                         start=True, stop=False, r=[attT, va], w=[pO])
                    yield
                    k.mm(pO[0:n, h * 64:(h + 1) * 64], lhsT=qgT[:, h, 0:n], rhs=Sb[:, h, :],
                         start=False, stop=True, r=[qgT, Sb], w=[pO])
                pU = psfA()
                for h in range(4):
                    yield
                    k.mm(pU[0:64, h * 64:(h + 1) * 64], lhsT=kd[0:n, h * 64:(h + 1) * 64], rhs=va[0:n, h * 64:(h + 1) * 64],
                         start=True, stop=True, r=[kd, va], w=[pU])
                yield
                k.tt(S[:], S[:], E3[:, :, n - 1:n].to_broadcast([64, 4, 64]), ALU.mult, r=[E3], w=[S])
                yield
                k.tt(S[:], S[:], pU[0:64, 0:256].rearrange("p (h v) -> p h v", h=4), ALU.add, r=[pU], w=[S])
                yield
                k.cp(Sb[:], S[:], r=[S], w=[Sb])
                yield
                k.act(osq[0:n, :], pO[0:n, 0:256], AF.Square, r=[pO], w=[osq])
                yield
                k.red(ssqA[0:n, 0:4], osq[0:n, :].rearrange("p (h d) -> p h d", h=4), r=[osq], w=[ssqA])
                yield
                k.act(rstA[0:n, 0:4], ssqA[0:n, 0:4], AF.Ln, r=[ssqA, eps_t], w=[rstA], bias=eps_t[0:n, :], scale=1.0 / 64)
                yield
                k.act(rstA[0:n, 0:4], rstA[0:n, 0:4], AF.Exp, r=[], w=[rstA], scale=-0.5)
                yield
                k.tt(osq[0:n, :].rearrange("p (h d) -> p h d", h=4), pO[0:n, 0:256].rearrange("p (h d) -> p h d", h=4),
                     rstA[0:n, 0:4].unsqueeze(2).to_broadcast([n, 4, 64]), ALU.mult, r=[pO, rstA], w=[osq])
                yield
                k.tt(oaf[0:n, :], osq[0:n, :], sg[0:n, :], ALU.mult, r=[osq, sg], w=[oaf])
                pt = PSB[0]
                for m in range(2):
                    yield
                    k.tr(pt[:, m * 128:m * 128 + n], oaf[0:n, m * 128:(m + 1) * 128], identb[0:n, 0:n], r=[oaf, identb], w=[pt])
                for m in range(2):
                    yield
                    k.cp(mixT[:, m, off:off + n], pt[:, m * 128:m * 128 + n], r=[pt], w=[mixT])


                yield
            def chainB():
                for j in range(2):
                    pY = PSF[2]
                    for i in range(31):
                        src_ = UB[:, j, i:i + n] if i % 2 == 0 else UBo[:, j, i - 1:i - 1 + n]
                        yield
                        k.mm(pY[:, 0:n], lhsT=L["dg"][:, j, i, :], rhs=src_, start=(i == 0), stop=(i == 30),
                             r=[L["dg"], UB, UBo], w=[pY])
                    yield
                    k.ts(yb[:, 0:n], pY[:, 0:n], L["cb"][:, j:j + 1], None, ALU.add, None, r=[pY, L["cb"]], w=[yb])
                    yield
                    k.tt(ysq[:, 0:n], yb[:, 0:n], yb[:, 0:n], ALU.mult, r=[yb], w=[ysq])
                    pM = PSF[2]
                    yield
                    hiloc(yb[:, 0:n], n, [yb])
                    yield
                    k.mm(pM[:, 0:n], lhsT=bonesb[:], rhs=hi_c[:, 0:n], start=True, stop=False, r=[bonesb, hi_c], w=[pM])
                    yield
                    k.mm(pM[:, 0:n], lhsT=bonesb[:], rhs=lo_c[:, 0:n], start=False, stop=True, r=[bonesb, lo_c], w=[pM])
                    pQ = PSF[2]
                    yield
                    hiloc(ysq[:, 0:n], n, [ysq])
                    yield
                    k.mm(pQ[:, 256:256 + n], lhsT=bonesb[:], rhs=hi_c[:, 0:n], start=True, stop=False, r=[bonesb, hi_c], w=[pQ])
                    yield
                    k.mm(pQ[:, 256:256 + n], lhsT=bonesb[:], rhs=lo_c[:, 0:n], start=False, stop=True, r=[bonesb, lo_c], w=[pQ])
                    yield
                    k.cp(cmean[:, 0:n], pM[:, 0:n], r=[pM], w=[cmean], eng="act")
                    yield
                    k.tt(ysq[:, 0:n], cmean[:, 0:n], cmean[:, 0:n], ALU.mult, r=[cmean], w=[ysq])
                    yield
                    k.tt(cvar[:, 0:n], pQ[:, 256:256 + n], ysq[:, 0:n], ALU.subtract, r=[pQ, ysq], w=[cvar])
                    yield
                    k.act(cvar[:, 0:n], cvar[:, 0:n], AF.Ln, r=[eps_t], w=[cvar], bias=eps_t[:], scale=1.0)
                    yield
                    k.act(cvar[:, 0:n], cvar[:, 0:n], AF.Exp, r=[], w=[cvar], scale=-0.5)
                    yield
                    k.tt(yb[:, 0:n], yb[:, 0:n], cmean[:, 0:n], ALU.subtract, r=[yb, cmean], w=[yb])
                    yield
                    k.tt(yb[:, 0:n], yb[:, 0:n], cvar[:, 0:n], ALU.mult, r=[cvar], w=[yb])
                    yield
                    k.ts(yb[:, 0:n], yb[:, 0:n], L["cg"][:, j:j + 1], L["cbb"][:, j:j + 1], ALU.mult, ALU.add, r=[L["cg"], L["cbb"]], w=[yb])
                    yield
                    k.act(cvar[:, 0:n], yb[:, 0:n], AF.Exp, r=[yb], w=[cvar], scale=-1.0)
                    yield
                    k.act(cvar[:, 0:n], cvar[:, 0:n], AF.Ln, r=[one_t], w=[cvar], bias=one_t[:], scale=1.0)
                    yield
                    k.act(cvar[:, 0:n], cvar[:, 0:n], AF.Exp, r=[], w=[cvar], scale=-1.0)
                    yield
                    k.tt(mixT[:, 2 + j, off:off + n], yb[:, 0:n], cvar[:, 0:n], ALU.mult, r=[yb, cvar], w=[mixT])
                yield
                k.cp(st["UFh"][:], UF[:, :, n:n + 30], r=[UF], w=[st["UFh"]])


                yield
            def chainC():
                yield
                k.dma("sp", cosT[0:n, :], c_cos[pos0:pos0 + n, :], w=[cosT])
                yield
                k.dma("sp", sinT[0:n, :], c_sin[pos0:pos0 + n, :], w=[sinT])
                yield
                k.act(qnf[0:n, :], zq[0:n, :], AF.Square, r=[zq], w=[qnf])
                yield
                k.red(ssq[0:n, 0:8], qnf[0:n, :].rearrange("p (h d) -> p h d", h=8), r=[qnf], w=[ssq])
                yield
                k.act(rst[0:n, 0:8], ssq[0:n, 0:8], AF.Ln, r=[ssq, eps_t], w=[rst], bias=eps_t[0:n, :], scale=1.0 / 64)
                yield
                k.act(rst[0:n, 0:8], rst[0:n, 0:8], AF.Exp, r=[], w=[rst], scale=-0.5)
                yield
                k.tt(qnf[0:n, :].rearrange("p (h d) -> p h d", h=8), zq[0:n, :].rearrange("p (h d) -> p h d", h=8),
                     rst[0:n, 0:8].unsqueeze(2).to_broadcast([n, 8, 64]), ALU.mult, r=[zq, rst], w=[qnf])
                yield
                k.tt(qnf[0:n, :].rearrange("p (h d) -> p h d", h=8), qnf[0:n, :].rearrange("p (h d) -> p h d", h=8),
                     L["qn"][0:n, :].unsqueeze(1).to_broadcast([n, 8, 64]), ALU.mult, r=[L["qn"]], w=[qnf])
                cosv = cosT[0:n, :].rearrange("p (m d) -> p m d", m=8)
                sinv = sinT[0:n, :].rearrange("p (m d) -> p m d", m=8)
                src = qnf[0:n, :].rearrange("p (m d) -> p m d", m=8)
                dst = qr[0:n, :].rearrange("p (m d) -> p m d", m=8)
                x1, x2 = src[:, :, 0:32], src[:, :, 32:64]
                a, b, c, d_ = [t[0:n] for t in rt]
                yield
                k.tt(a, x1, cosv, ALU.mult, r=[qnf, cosT], w=[rt[0]])
                yield
                k.tt(b, x2, sinv, ALU.mult, r=[qnf, sinT], w=[rt[1]])
                yield
                k.tt(dst[:, :, 0:32], a, b, ALU.subtract, r=[rt[0], rt[1]], w=[qr])
                yield
                k.tt(c, x2, cosv, ALU.mult, r=[qnf, cosT], w=[rt[2]])
                yield
                k.tt(d_, x1, sinv, ALU.mult, r=[qnf, sinT], w=[rt[3]])
                yield
                k.tt(dst[:, :, 32:64], c, d_, ALU.add, r=[rt[2], rt[3]], w=[qr])
                pt = PSB[1]
                for h in range(8):
                    yield
                    k.tr(pt[0:64, h * 64:h * 64 + n], qr[0:n, h * 64:(h + 1) * 64], identb[0:n, 0:n], r=[qr, identb], w=[pt])
                for h in range(8):
                    yield
                    k.cp(QT[:, h, 0:n], pt[0:64, h * 64:h * 64 + n], r=[pt], w=[QT])
                yield
                k.act(knf[0:n, :], zkv[0:n, 0:128], AF.Square, r=[zkv], w=[knf])
                yield
                k.red(ssq[0:n, 0:2], knf[0:n, :].rearrange("p (h d) -> p h d", h=2), r=[knf], w=[ssq])
                yield
                k.act(rst[0:n, 0:2], ssq[0:n, 0:2], AF.Ln, r=[ssq, eps_t], w=[rst], bias=eps_t[0:n, :], scale=1.0 / 64)
                yield
                k.act(rst[0:n, 0:2], rst[0:n, 0:2], AF.Exp, r=[], w=[rst], scale=-0.5)
                for h in range(2):
                    cs = slice(h * 64, (h + 1) * 64)
                    yield
                    k.stt(knf[0:n, cs], zkv[0:n, cs], rst[0:n, h:h + 1], L["kn"][0:n, :], ALU.mult, ALU.mult, r=[zkv, rst, L["kn"]], w=[knf])
                src = knf[0:n, :].rearrange("p (m d) -> p m d", m=2)
                x1, x2 = src[:, :, 0:32], src[:, :, 32:64]
                dst = krf[0:n, :].rearrange("p (m d) -> p m d", m=2)
                cos2 = cosT[0:n, 0:64].rearrange("p (m d) -> p m d", m=2)
                sin2 = sinT[0:n, 0:64].rearrange("p (m d) -> p m d", m=2)
                a, b, c, d_ = [t[0:n, 0:2, :] for t in rt]
                yield
                k.tt(a, x1, cos2, ALU.mult, r=[knf, cosT], w=[rt[0]])
                yield
                k.tt(b, x2, sin2, ALU.mult, r=[knf, sinT], w=[rt[1]])
                yield
                k.tt(dst[:, :, 0:32], a, b, ALU.subtract, r=[rt[0], rt[1]], w=[krf])
                yield
                k.tt(c, x2, cos2, ALU.mult, r=[knf, cosT], w=[rt[2]])
                yield
                k.tt(d_, x1, sin2, ALU.mult, r=[knf, sinT], w=[rt[3]])
                yield
                k.tt(dst[:, :, 32:64], c, d_, ALU.add, r=[rt[2], rt[3]], w=[krf])
                yield
                k.cp(krb[0:n, :], krf[0:n, :], r=[krf], w=[krb])
                KTc, VBc = st["KT_free"], st["VB_free"]
                pt = PSB[1]
                for kv in range(2):
                    yield
                    k.tr(pt[0:64, kv * 64:kv * 64 + n], krb[0:n, kv * 64:(kv + 1) * 64], identb[0:n, 0:n], r=[krb, identb], w=[pt])
                for kv in range(2):
                    yield
                    k.cp(KTc[:, kv, 0:n], pt[0:64, kv * 64:kv * 64 + n], r=[pt], w=[KTc])
                yield
                k.cp(vf[0:n, :], zkv[0:n, 128:256], r=[zkv], w=[vf], eng="act")
                yield
                k.cp(VBc[0:n, :], zkv[0:n, 128:256], r=[zkv], w=[VBc])
                kblocks = ((st["KT_old"], st["VB_old"], st["ones_old"], 64, mprev4),
                           (st["KT_mid"], st["VB_mid"], st["ones_mid"], 64, None),
                           (KTc, VBc, onesb, n, mdiag4))
                if n == 64:
                    pNs = (PSF[3], PSF[4])
                    pS = PSF[5]
                    valid = []
                    for bi, (KTk, VBk, ones_k, nk, mask) in enumerate(kblocks):
                        if ones_k is zerosb:
                            continue
                        valid.append(bi)
                    for vi, bi in enumerate(valid):
                        KTk, VBk, ones_k, nk, mask = kblocks[bi]
                        for kv in range(2):
                            yield
                            k.mm(pS[0:nk, kv * 256:kv * 256 + 256], lhsT=KTk[:, kv, 0:nk], rhs=QT[:, kv * 4:(kv + 1) * 4, 0:n],
                                 start=True, stop=True, r=[KTk, QT], w=[pS])
                        Pt = Pm8[bi]
                        yield
                        k.act(Pt[0:nk, :], pS[0:nk, :], AF.Exp, r=[pS], w=[Pt], scale=0.125)
                        if mask is not None:
                            yield
                            k.tt(Pt[0:nk, :].rearrange("p (g x) -> p g x", g=2), Pt[0:nk, :].rearrange("p (g x) -> p g x", g=2),
                                 mask[0:nk, :, :].rearrange("p m t -> p (m t)").unsqueeze(1).to_broadcast([nk, 2, 256]),
                                 ALU.mult, r=[mask], w=[Pt])
                        for kv in range(2):
                            yield
                            k.mm(pNs[kv][0:64, 0:256], lhsT=VBk[0:nk, kv * 64:(kv + 1) * 64], rhs=Pt[0:nk, kv * 256:(kv + 1) * 256],
                                 start=(vi == 0), stop=(vi == len(valid) - 1), r=[VBk, Pt], w=[pNs[kv]], inc=True)
                    Pl = Pm8[valid[-1]]
                    for bi in valid[:-1]:
                        yield
                        k.tt(Pl[:, :], Pl[:, :], Pm8[bi][:, :], ALU.add, r=[Pm8[bi]], w=[Pl])
                    yield
                    k.mm(pS[0:64, :], lhsT=onesb[0:64, 0:64], rhs=Pl[:, :], start=True, stop=True, r=[onesb, Pl], w=[pS])
                    yield
                    k.tt(dent8[:, :], pS[0:64, :], L["esink"][:, :, :].rearrange("p h t -> p (h t)"), ALU.add, r=[pS, L["esink"]], w=[dent8])
                    yield
                    k.act(dent8[:, :], dent8[:, :], AF.Ln, r=[], w=[dent8])
                    yield
                    k.act(dent8[:, :], dent8[:, :], AF.Exp, r=[], w=[dent8], scale=-1.0)
                    for kv in range(2):
                        yield
                        k.tt(ocT[:, kv * 4:(kv + 1) * 4, off:off + n], pNs[kv][0:64, 0:256].rearrange("p (m t) -> p m t", m=4),
                             dent8[:, kv * 256:(kv + 1) * 256].rearrange("p (m t) -> p m t", m=4), ALU.mult, r=[pNs[kv], dent8], w=[ocT])
                else:
                    for kv in range(2):
                        pN = PSF[3]
                        pD = PSF[4]
                        for bi, (KTk, VBk, ones_k, nk, mask) in enumerate(kblocks):
                            pS = PSF[5]
                            yield
                            k.mm(pS[0:nk, 0:4 * n], lhsT=KTk[:, kv, 0:nk], rhs=QT[:, kv * 4:(kv + 1) * 4, 0:n], start=True, stop=True,
                                 r=[KTk, QT], w=[pS])
                            Pt = Pm[bi]
                            yield
                            k.act(Pt[0:nk, :, 0:n], pS[0:nk, 0:4 * n].rearrange("p (m t) -> p m t", m=4), AF.Exp, r=[pS], w=[Pt], scale=0.125)
                            if mask is not None:
                                yield
                                k.tt(Pt[0:nk, :, 0:n], Pt[0:nk, :, 0:n], mask[0:nk, :, 0:n], ALU.mult, r=[mask], w=[Pt])
                            yield
                            k.mm(pN[0:64, 0:4 * n], lhsT=VBk[0:nk, kv * 64:(kv + 1) * 64], rhs=Pt[0:nk, :, 0:n], start=(bi == 0), stop=(bi == 2),
                                 r=[VBk, Pt], w=[pN], inc=True)
                            yield
                            k.mm(pD[0:64, 0:4 * n], lhsT=ones_k[0:nk, 0:64], rhs=Pt[0:nk, :, 0:n], start=(bi == 0), stop=(bi == 2),
                                 r=[ones_k, Pt], w=[pD], inc=True)
                        yield
                        k.tt(dent[:, :, 0:n], pD[0:64, 0:4 * n].rearrange("p (m t) -> p m t", m=4), L["esink"][:, kv * 4:(kv + 1) * 4, 0:n],
                             ALU.add, r=[pD, L["esink"]], w=[dent])
                        yield
                        k.recip(dent[:, :, 0:n], dent[:, :, 0:n], r=[], w=[dent])
                        yield
                        k.tt(ocT[:, kv * 4:(kv + 1) * 4, off:off + n], pN[0:64, 0:4 * n].rearrange("p (m t) -> p m t", m=4), dent[:, :, 0:n],
                             ALU.mult, r=[pN, dent], w=[ocT])

                    yield
            chains = [chainA(), chainB(), chainC()]
            while chains:
                for g_ in list(chains):
                    try:
                        next(g_)
                    except StopIteration:
                        chains.remove(g_)
            if n == 64:
                st["KT_free"], st["KT_old"], st["KT_mid"] = st["KT_old"], st["KT_mid"], KTc
                st["VB_free"], st["VB_old"], st["VB_mid"] = st["VB_old"], st["VB_mid"], VBc
                st["ones_old"], st["ones_mid"] = st["ones_mid"], onesb

        dg_state = [None]

        def load_win(l):
            wv = w_in[l].rearrange("(c p) n -> p c n", p=128)
            for pi in range(5):
                c0, c1 = pi * 512, min((pi + 1) * 512, INC)
                k.dma("pool", WIN[:, :, c0:c1], wv[:, :, c0:c1], w=[WINr[pi]])

        def load_pan(src_ap, nk, ncol, p0=0, rows=slice(0, 128), t=None):
            if t is None:
                t = next_pan()
            k.dma("pool", t[rows, p0:p0 + nk, 0:ncol], src_ap, w=[t])
            return t

        def layer(l, blocks, mix_blocks, p_src):
            L = P[l]
            load_win(l)

            def build_dg(l_):
                for j in range(2):
                    for i in range(31):
                        k.ts(dg[:, j, i, :], identf[:], P[l_]["ptmp"][:, j, i:i + 1], None, ALU.mult, None,
                             r=[P[l_]["ptmp"], identf], w=[dg])
            if dg_state[0] != l:
                build_dg(l)
                dg_state[0] = l
            for off, n, Ht in blocks:
                rmsnorm_T(Ht, n, L["g_mix"], off)
            for mb in mix_blocks:
                mb(l)
            build_dg((l + 1) % DEPTH)
            dg_state[0] = (l + 1) % DEPTH
            wo = w_out[l]
            for ch in range(2):
                cs = slice(ch * 512, (ch + 1) * 512)
                t1 = load_pan(wo[0:512, cs].rearrange("(c p) n -> p c n", p=128), 4, 512)
                t2 = load_pan(wo[512:1024, cs].rearrange("(h d) n -> d h n", d=64), 8, 512, rows=slice(0, 64))
                for off, n, Ht in blocks:
                    p_ = psf()
                    lin_tok(p_, n, off, 0, 512, t1, [t1], mixT, mixT, nk=4, last=False)
                    lin_tok(p_, n, off, 0, 512, t2, [t2], ocT, ocT, nk=8, kp=64, first=False)
                    k.tt(Ht[0:n, cs], Ht[0:n, cs], p_[0:n, :], ALU.add, r=[p_], w=[Ht])
            for off, n, Ht in blocks:
                rmsnorm_T(Ht, n, L["g_ffn"], off)
            wg = w_gate[l].rearrange("(c p) n -> p c n", p=128)
            wu = w_up[l].rearrange("(c p) n -> p c n", p=128)
            for pi in range(6):
                c0, c1 = pi * 512, min((pi + 1) * 512, DFF)
                tg = load_pan(wg[:, :, c0:c1], 8, c1 - c0)
                tu = load_pan(wu[:, :, c0:c1], 8, c1 - c0)
                ntok = blocks[-1][0] + blocks[-1][1]
                for j in range((c1 - c0) // 128):
                    fc = (c0 // 128) + j
                    pg_ = psf()
                    lin_feat(pg_, ntok, 0, j * 128, tg, [tg], xT, xT)
                    pu_ = psf()
                    lin_feat(pu_, ntok, 0, j * 128, tu, [tu], xT, xT)
                    k.act(aT[:, fc, 0:ntok], pg_[:, 0:ntok], AF.Silu, r=[pg_], w=[aT])
                    k.tt(aT[:, fc, 0:ntok], pu_[:, 0:ntok], aT[:, fc, 0:ntok], ALU.mult, r=[pu_], w=[aT])
            wd = w_down[l].rearrange("(c p) n -> p c n", p=128)
            for ch in range(2):
                cs = slice(ch * 512, (ch + 1) * 512)
                accs = [psf() for _ in blocks]
                for rg, (k0, nk) in enumerate(((0, 8), (8, 8), (16, 6))):
                    t = load_pan(wd[:, k0:k0 + nk, cs], nk, 512)
                    for bi, (off, n, Ht) in enumerate(blocks):
                        for kc in range(nk):
                            k.mm(accs[bi][0:n, :], lhsT=aT[:, k0 + kc, off:off + n], rhs=t[:, kc, :],
                                 start=(k0 + kc == 0), stop=(k0 + kc == 21), r=[aT, t], w=[accs[bi]],
                                 inc=(kc == nk - 1))
                for bi, (off, n, Ht) in enumerate(blocks):
                    k.tt(Ht[0:n, cs], Ht[0:n, cs], accs[bi][0:n, :], ALU.add, r=[accs[bi]], w=[Ht])
            for off, n, Ht in blocks:
                rmsnorm_T(Ht, n, L["g_ple"], off)
                k.dma("pool", pb[0:n, :], p_src(l, off, n), w=[pb])
                pt = psb()
                for c in range(2):
                    k.tr(pt[:, c * 128:c * 128 + n], pb[0:n, c * 128:(c + 1) * 128], identb[0:n, 0:n], r=[pb, identb], w=[pt])
                for c in range(2):
                    k.cp(pT[:, c, off:off + n], pt[:, c * 128:c * 128 + n], r=[pt], w=[pT])
            wpg = w_ple_gate[l].rearrange("(c p) n -> p c n", p=128)
            wpp = w_ple_proj[l].rearrange("(c p) n -> p c n", p=128)
            for ch in range(2):
                cs = slice(ch * 512, (ch + 1) * 512)
                tg = load_pan(wpg[:, :, cs], 8, 512)
                tp = load_pan(wpp[:, :, cs], 2, 512)
                for off, n, Ht in blocks:
                    p1 = psf()
                    lin_tok(p1, n, off, 0, 512, tg, [tg], xT, xT)
                    p2 = psf()
                    lin_tok(p2, n, off, 0, 512, tp, [tp], pT, pT, nk=2)
                    k.act(gate_sb[0:n, :], p1[0:n, :], AF.Sigmoid, r=[p1], w=[gate_sb])
                    k.tt(gate_sb[0:n, :], p2[0:n, :], gate_sb[0:n, :], ALU.mult, r=[p2], w=[gate_sb])
                    k.tt(Ht[0:n, cs], Ht[0:n, cs], gate_sb[0:n, :], ALU.add, r=[gate_sb], w=[Ht])

        sst = new_state("s")
        kc_old = sb("kc_old", [64, 128], BF16)
        kc_mid = sb("kc_mid", [64, 128], BF16)

        def sample_mix_block(b):
            def run(l):
                st = sst
                st["ones_old"], st["ones_mid"] = onesb, onesb
                S = st["S"]
                k.dma("sp", S[:], state_hgrn[l, b].rearrange("h k v -> k h v"), w=[S])
                k.cp(st["Sb"][:], S[:], r=[S], w=[st["Sb"]])
                k.dma("sp", halo_tok[0:30, :], state_conv[l, b], w=[halo_tok])
                for j in range(2):
                    p_ = psf()
                    tr32(p_[:, 0:30], halo_tok[0:30, j * 128:(j + 1) * 128], 30, 128, [halo_tok], [p_])
                    k.cp(st["UFh"][:, j, :], p_[:, 0:30], r=[p_], w=[st["UFh"]], eng="act")
                k.dma("pool", kc_old[:], cache_k[l, b, 0:64, :], w=[kc_old])
                k.dma("pool", kc_mid[:], cache_k[l, b, 64:128, :], w=[kc_mid])
                k.dma("pool", st["VB_old"][:], cache_v[l, b, 0:64, :], w=[st["VB_old"]])
                k.dma("pool", st["VB_mid"][:], cache_v[l, b, 64:128, :], w=[st["VB_mid"]])
                for src_t, dst_t in ((kc_old, st["KT_old"]), (kc_mid, st["KT_mid"])):
                    pt = psb()
                    for kv in range(2):
                        k.tr(pt[0:64, kv * 64:(kv + 1) * 64], src_t[:, kv * 64:(kv + 1) * 64], identb[0:64, 0:64], r=[src_t, identb], w=[pt])
                    for kv in range(2):
                        k.cp(dst_t[:, kv, :], pt[0:64, kv * 64:(kv + 1) * 64], r=[pt], w=[dst_t])
                mixer(l, DSEQ, b * DSEQ, st, SEQ)
                k.dma("sp", o_hs[l, b].rearrange("h k v -> k h v"), S[:], r=[S])
                k.dma("sp", o_cs[l, b, 0:26, :], state_conv[l, b, 4:30, :])
                for j in range(2):
                    p_ = psf()
                    tr32(p_[0:DSEQ, 0:128], UF[:, j, 30:30 + DSEQ], 128, DSEQ, [UF], [p_])
                    k.cp(tok30[0:DSEQ, j * 128:(j + 1) * 128], p_[0:DSEQ, 0:128], r=[p_], w=[tok30], eng="act")
                k.dma("sp", o_cs[l, b, 26:30, :], tok30[0:DSEQ, :], r=[tok30])
                k.dma("sp", o_ks[l, b, 0:124, :], cache_k[l, b, 4:128, :])
                k.dma("sp", o_vs[l, b, 0:124, :], cache_v[l, b, 4:128, :])
                k.dma("sp", o_ks[l, b, 124:128, :], krf[0:DSEQ, :], r=[krf])
                k.dma("sp", o_vs[l, b, 124:128, :], vf[0:DSEQ, :], r=[vf])
            return run

        NS = SPC * DSEQ
        k.dma("sp", H[0][0:NS, :], x_sample, w=[H[0]])
        sblocks = [(0, NS, H[0])]
        for l in range(DEPTH):
            layer(l, sblocks, [sample_mix_block(b) for b in range(SPC)], lambda l_, off, n: p_sample[l_, off:off + n, :])
        k.dma("sp", y_sample, H[0][0:NS, :], r=[H[0]])
        if STAGE <= 4:
            k.final_wait()
            return nc

        pst = []
        for l in range(DEPTH):
            st = new_state("p%d" % l)
            for nm in ("S", "Sb", "UFh", "KT_old", "KT_mid", "VB_old", "VB_mid"):
                k.memset(st[nm][:], 0.0, w=[st[nm]])
            pst.append(st)
        nst = SEQ // ST
        NMB = ST // 64
        for si in range(nst):
            t0 = si * ST
            blocks = []
            for bi in range(NB):
                k.dma("sp", H[bi][:], x_prompt[t0 + bi * 128:t0 + (bi + 1) * 128, :], w=[H[bi]])
                blocks.append((bi * 128, 128, H[bi]))
            for l in range(DEPTH):
                def mk(mi):
                    def run(l_):
                        mixer(l_, 64, mi * 64, pst[l_], t0 + mi * 64)
                        if si == nst - 1 and mi >= NMB - 2:
                            half = mi - (NMB - 2)
                            k.dma("sp", o_kp[l_, half * 64:(half + 1) * 64, :], krf[0:64, :], r=[krf])
                            k.dma("sp", o_vp[l_, half * 64:(half + 1) * 64, :], vf[0:64, :], r=[vf])
                        if si == nst - 1 and mi == NMB - 1:
                            stt_ = pst[l_]
                            k.dma("sp", o_hp[l_].rearrange("h k v -> k h v"), stt_["S"][:], r=[stt_["S"]])
                            for j in range(2):
                                p_ = psf()
                                tr32(p_[0:30, 0:128], UF[:, j, 64:94], 128, 30, [UF], [p_])
                                k.cp(tok30[0:30, j * 128:(j + 1) * 128], p_[0:30, 0:128], r=[p_], w=[tok30], eng="act")
                            k.dma("sp", o_cp[l_], tok30[0:30, :], r=[tok30])
                    return run
                layer(l, blocks, [mk(mi) for mi in range(NMB)], lambda l_, off, n: p_prompt[l_, t0 + off:t0 + off + n, :])
            for bi in range(NB):
                k.dma("sp", y_prompt[t0 + bi * 128:t0 + (bi + 1) * 128, :], H[bi][:], r=[H[bi]])
        k.final_wait()
    return nc


_CONSTS = None


def _consts():
    global _CONSTS
    if _CONSTS is None:
        i = np.arange(128)
        same = (i[:, None] // 64) == (i[None, :] // 64)
        U = ((i[:, None] <= i[None, :]) & same).astype(np.float32)
        W = ((i[:, None] > i[None, :]) & same).astype(np.float32)
        mprev = (i[:, None] >= i[None, :]).astype(np.float32)
        mdiag = (i[:, None] <= i[None, :]).astype(np.float32)
        bones = (same.astype(np.float32) / 64.0).astype(np.float32)
        half = 32
        inv = (10000.0 ** (-np.arange(half, dtype=np.float32) / half)).astype(np.float32)
        pos = np.concatenate([np.arange(SEQ, dtype=np.float32), PAST + np.arange(DSEQ, dtype=np.float32)])
        ang = (pos[:, None] * inv[None, :]).astype(np.float32)
        cos = np.tile(np.cos(ang).astype(np.float32), (1, 8))
        sin = np.tile(np.sin(ang).astype(np.float32), (1, 8))
        _CONSTS = dict(c_ident=np.eye(128, dtype=np.float32), c_U=U, c_W=W, c_mprev=mprev, c_mdiag=mdiag,
                       c_bones=bones, c_cos=np.ascontiguousarray(cos), c_sin=np.ascontiguousarray(sin))
    return _CONSTS


def kernel(x_prompt, x_sample, state_hgrn, state_conv, cache_swa_k, cache_swa_v, p_prompt, p_sample,
           a_lower, w_in, a_onorm, conv_w, conv_b, conv_ln_g, conv_ln_b, q_norm, k_norm, sinks, w_out,
           norm_mix, norm_ffn, w_gate, w_up, w_down, ple_norm, w_ple_gate, w_ple_proj):
    f = lambda a: np.ascontiguousarray(np.asarray(a, dtype=np.float32))
    shared = dict(x_prompt=f(x_prompt).reshape(SEQ, D), p_prompt=f(p_prompt).reshape(DEPTH, SEQ, 256),
                  a_lower=f(a_lower), w_in=f(w_in), a_onorm=f(a_onorm), conv_w=f(conv_w), conv_b=f(conv_b),
                  conv_ln_g=f(conv_ln_g), conv_ln_b=f(conv_ln_b), q_norm=f(q_norm), k_norm=f(k_norm), sinks=f(sinks),
                  w_out=f(w_out), norm_mix=f(norm_mix), norm_ffn=f(norm_ffn), w_gate=f(w_gate), w_up=f(w_up),
                  w_down=f(w_down), ple_norm=f(ple_norm), w_ple_gate=f(w_ple_gate), w_ple_proj=f(w_ple_proj))
    shared.update(_consts())
    xs, sh, sc = f(x_sample), f(state_hgrn), f(state_conv)
    ck, cv, ps_ = f(cache_swa_k), f(cache_swa_v), f(p_sample)
    in_maps = []
    for c in range(NCORE):
        b = slice(c * SPC, (c + 1) * SPC)
        m = dict(shared)
        m.update(x_sample=np.ascontiguousarray(xs[b]).reshape(SPC * DSEQ, D),
                 state_hgrn=np.ascontiguousarray(sh[:, b]),
                 state_conv=np.ascontiguousarray(sc[:, b]),
                 cache_k=np.ascontiguousarray(ck[:, b]).reshape(DEPTH, SPC, 128, 128),
                 cache_v=np.ascontiguousarray(cv[:, b]).reshape(DEPTH, SPC, 128, 128),
                 p_sample=np.ascontiguousarray(ps_[:, b]).reshape(DEPTH, SPC * DSEQ, 256))
        in_maps.append(m)
    nc = build_program()
    res = run_bass_kernel_spmd(nc, in_maps, core_ids=list(range(NCORE)))
    R = res.results
    cat = lambda name, ax: np.concatenate([R[c][name] for c in range(NCORE)], axis=ax)
    y_p = R[0]["y_prompt"].reshape(1, SEQ, D)
    y_s = cat("y_sample", 0).reshape(NSEQ, DSEQ, D)
    return (y_p.astype(np.float32), y_s.astype(np.float32),
            R[0]["o_hp"].reshape(DEPTH, 1, 4, 64, 64), R[0]["o_cp"].reshape(DEPTH, 1, 30, 256),
            R[0]["o_kp"].reshape(DEPTH, 1, 128, 2, 64), R[0]["o_vp"].reshape(DEPTH, 1, 128, 2, 64),
            cat("o_hs", 1), cat("o_cs", 1),
            cat("o_ks", 1).reshape(DEPTH, NSEQ, 128, 2, 64), cat("o_vs", 1).reshape(DEPTH, NSEQ, 128, 2, 64))
```
